# Optimizing a Trainium2 kernel written in Bass

```python
import math
import jax, jax.numpy as jnp
from jax import lax
import numpy as np

D_MODEL = 2048
BATCH = 32
SEQ = 256
DEPTH = 2
DEC_BATCH = 2
DEC_SEQ = 2048
PAST_LEN = 512

GRID_W = 64
N_HEADS = D_MODEL // 256
HEAD_DIM = 128
ATT_W = N_HEADS * HEAD_DIM
MAX_KH = 8
KW = 16
QB = 16
KSPAN = QB + KW
N_QBLK = GRID_W // QB
Q_BLOCK = 128
FNET_GROUPS = 4
FNET_GDIM = D_MODEL // 16
FNET_W = FNET_GROUPS * FNET_GDIM
HY_W = D_MODEL // 4
HYENA_ORDER = 2
FILTER_EMB = 33
FILTER_FF = 64
MIN_DECAY = math.log(1e-2) / 1.5
MAX_DECAY = math.log(1e-2) / 0.3
W_MIX = ATT_W + FNET_W + HY_W
OFF_Q = 0
OFF_K = OFF_Q + ATT_W
OFF_V = OFF_K + ATT_W
OFF_GA = OFF_V + ATT_W
OFF_UB = OFF_GA + ATT_W
OFF_GB = OFF_UB + FNET_W
OFF_HC = OFF_GB + FNET_W
OFF_GC = OFF_HC + 3 * HY_W
OFF_MG = OFF_GC + HY_W
N_IN = OFF_MG + 3 * D_MODEL
EPS = 1e-6
NEG = -1e30

kernel_name = 'hybrid_natten_fnet_hyena_dit_step'


def _rmsnorm(x, g):
    xf = x.astype(jnp.float32)
    y = xf * lax.rsqrt(jnp.mean(xf * xf, axis=-1, keepdims=True) + EPS)
    return (y * g.astype(jnp.float32)).astype(x.dtype)


def _modulated_proj(x, cvec, norm_g, w_ada, b_ada, w_in):
    ada = jax.nn.silu(cvec) @ w_ada + b_ada
    shift, scale, gate = jnp.split(ada, 3, axis=-1)
    h = _rmsnorm(x, norm_g) * (1.0 + scale[:, None, :]) + shift[:, None, :]
    return h @ w_in, gate


def _qkv(proj, q_g, k_g):
    B, L, _ = proj.shape
    q = proj[..., OFF_Q:OFF_K].reshape(B, L, N_HEADS, HEAD_DIM)
    k = proj[..., OFF_K:OFF_V].reshape(B, L, N_HEADS, HEAD_DIM)
    v = proj[..., OFF_V:OFF_GA].reshape(B, L, N_HEADS, HEAD_DIM)
    return _rmsnorm(q, q_g), _rmsnorm(k, k_g), v


def _context_attention(q, k, v):
    B, L, H, Dh = q.shape
    scale = Dh ** -0.5
    qb = q.reshape(B, L // Q_BLOCK, Q_BLOCK, H, Dh).swapaxes(0, 1)

    def blk(qi):
        s = jnp.einsum('bqhd,bkhd->bhqk', qi, k).astype(jnp.float32) * scale
        p = jax.nn.softmax(s, axis=-1).astype(v.dtype)
        return jnp.einsum('bhqk,bkhd->bqhd', p, v)

    o = lax.map(blk, qb)
    return o.swapaxes(0, 1).reshape(B, L, H, Dh)


def _latent_attention(q, k, v, k_ctx, v_ctx, rpb):
    B, L, H, Dh = q.shape
    rows = L // GRID_W
    kh = min(MAX_KH, rows)
    scale = Dh ** -0.5
    qg = q.reshape(B, rows, GRID_W, H, Dh)
    kg = k.reshape(B, rows, GRID_W, H, Dh)
    vg = v.reshape(B, rows, GRID_W, H, Dh)
    cols = np.arange(GRID_W)
    col_start = np.clip(cols - KW // 2, 0, GRID_W - KW)
    blks = np.arange(N_QBLK)
    span_start = np.clip(blks * QB - KW // 2, 0, GRID_W - KSPAN)
    span_cols = span_start[:, None] + np.arange(KSPAN)[None, :]
    q_cols = blks[:, None] * QB + np.arange(QB)[None, :]
    q_cs = col_start[q_cols]
    kc = span_cols[:, None, :]
    col_valid = (kc >= q_cs[..., None]) & (kc < q_cs[..., None] + KW)
    dc_idx = np.clip(kc - q_cols[..., None] + KW - 1, 0, 2 * KW - 2)
    col_bias = rpb[:, :, dc_idx]
    valid = col_valid[:, :, None, :]
    n_win = kh * KSPAN

    def row(r):
        rs = jnp.clip(r - kh // 2, 0, rows - kh)
        q_r = lax.dynamic_index_in_dim(qg, r, axis=1, keepdims=False)
        q_r = q_r.reshape(B, N_QBLK, QB, H, Dh)
        k_blk = lax.dynamic_slice_in_dim(kg, rs, kh, axis=1)[:, :, span_cols]
        v_blk = lax.dynamic_slice_in_dim(vg, rs, kh, axis=1)[:, :, span_cols]
        dr_idx = rs + jnp.arange(kh) - r + (MAX_KH - 1)
        bias = jnp.take(col_bias, dr_idx, axis=1).transpose(0, 2, 3, 1, 4)
        s_win = (jnp.einsum('bnqhd,binkhd->bhnqik', q_r, k_blk).astype(jnp.float32) * scale
                 + bias.astype(jnp.float32))
        s_win = jnp.where(valid, s_win, NEG)
        s_ctx = jnp.einsum('bnqhd,bchd->bhnqc', q_r, k_ctx).astype(jnp.float32) * scale
        s = jnp.concatenate([s_win.reshape(B, H, N_QBLK, QB, n_win), s_ctx], axis=-1)
        p = jax.nn.softmax(s, axis=-1).astype(v.dtype)
        p_win = p[..., :n_win].reshape(B, H, N_QBLK, QB, kh, KSPAN)
        p_ctx = p[..., n_win:]
        o = (jnp.einsum('bhnqik,binkhd->bnqhd', p_win, v_blk)
             + jnp.einsum('bhnqc,bchd->bnqhd', p_ctx, v_ctx))
        return o.reshape(B, GRID_W, H, Dh)

    o = lax.map(row, jnp.arange(rows))
    return o.swapaxes(0, 1).reshape(B, L, H, Dh)


def _fourier_mix(u):
    B, L, _ = u.shape
    uf = u.astype(jnp.float32).reshape(B, L, FNET_GROUPS, FNET_GDIM)
    y = jnp.fft.fft2(uf, axes=(1, 3), norm='ortho').real
    return y.reshape(B, L, FNET_W).astype(u.dtype)


def _short_conv(x, w, b):
    L = x.shape[1]
    xp = jnp.pad(x, ((0, 0), (1, 1), (0, 0)))
    return xp[:, 0:L] * w[0] + xp[:, 1:L + 1] * w[1] + xp[:, 2:L + 2] * w[2] + b


def _hyena_filter_fft(L, f_w1, f_b1, f_freq, f_w2, f_b2, f_w3):
    t = jnp.linspace(0.0, 1.0, L, dtype=jnp.float32)[:, None]
    bands = (FILTER_EMB - 1) // 2
    w = (2.0 * math.pi / L) * jnp.arange(L, dtype=jnp.float32)[:, None]
    f = jnp.linspace(1e-4, bands - 1, bands, dtype=jnp.float32)[None, :]
    z = jnp.concatenate([t, jnp.cos(w * f), -jnp.sin(w * f)], axis=-1)
    h = jnp.sin(f_freq * (z @ f_w1 + f_b1))
    h = jnp.sin(f_freq * (h @ f_w2 + f_b2))
    h = (h @ f_w3).astype(jnp.float32).reshape(L, HYENA_ORDER, 2, HY_W)
    deltas = jnp.abs(jnp.linspace(MIN_DECAY, MAX_DECAY, HY_W, dtype=jnp.float32))
    h = h * jnp.exp(-t[:, :, None, None] * deltas)
    h = h / (jnp.sum(jnp.abs(h), axis=0, keepdims=True) + EPS)
    fwd = h[:, :, 0]
    bwd = h[:, :, 1]
    k2 = jnp.concatenate([fwd, jnp.zeros_like(fwd[:1]), bwd[:0:-1]], axis=0)
    return jnp.fft.rfft(k2, axis=0)


def _long_conv(z, kf):
    L = z.shape[1]
    zf = jnp.fft.rfft(z.astype(jnp.float32), n=2 * L, axis=1)
    y = jnp.fft.irfft(zf * kf[None], n=2 * L, axis=1)[:, :L]
    return y.astype(z.dtype)


def _hyena_mix(hin, conv_w, conv_b, kf, hy_bias):
    hc = _short_conv(hin, conv_w, conv_b)
    v, x1, x2 = jnp.split(hc, 3, axis=-1)
    z = v
    for o, xg in enumerate((x1, x2)):
        z = xg * (_long_conv(z, kf[:, o]) + hy_bias[o] * z)
    return z


def _branches_residual(x, proj, gate, att, conv_w, conv_b, kf, hy_bias, w_br, w_out):
    B, L, _ = x.shape
    y_a = att.reshape(B, L, ATT_W) * jax.nn.silu(proj[..., OFF_GA:OFF_UB])
    y_b = _fourier_mix(proj[..., OFF_UB:OFF_GB]) * jax.nn.silu(proj[..., OFF_GB:OFF_HC])
    y_c = (_hyena_mix(proj[..., OFF_HC:OFF_GC], conv_w, conv_b, kf, hy_bias)
           * jax.nn.silu(proj[..., OFF_GC:OFF_MG]))
    g_a, g_b, g_c = jnp.split(jax.nn.sigmoid(proj[..., OFF_MG:]), 3, axis=-1)
    merged = (g_a * (y_a @ w_br[:ATT_W])
              + g_b * (y_b @ w_br[ATT_W:ATT_W + FNET_W])
              + g_c * (y_c @ w_br[ATT_W + FNET_W:]))
    return x + gate[:, None, :] * (merged @ w_out)


def setup_inputs(seed: int = 0) -> dict:
    key = jax.random.key(seed)
    ks = jax.random.split(key, 24)
    nrm = jax.random.normal
    f32 = jnp.float32
    return {
        'x_prompt': nrm(ks[0], (BATCH, SEQ, D_MODEL), f32),
        'x_sample': nrm(ks[1], (DEC_BATCH, DEC_SEQ, D_MODEL), f32),
        'cache_k': nrm(ks[2], (DEC_BATCH, DEPTH, PAST_LEN, N_HEADS, HEAD_DIM), f32),
        'cache_v': nrm(ks[3], (DEC_BATCH, DEPTH, PAST_LEN, N_HEADS, HEAD_DIM), f32),
        'c': nrm(ks[4], (DEC_BATCH, D_MODEL), f32),
        'c_ctx': nrm(ks[5], (D_MODEL,), f32),
        'norm_g': 1.0 + 0.01 * nrm(ks[6], (DEPTH, D_MODEL), f32),
        'w_ada': nrm(ks[7], (DEPTH, D_MODEL, 3 * D_MODEL), f32) * D_MODEL ** -0.5,
        'b_ada': 0.02 * nrm(ks[8], (DEPTH, 3 * D_MODEL), f32),
        'w_in': nrm(ks[9], (DEPTH, D_MODEL, N_IN), f32) * D_MODEL ** -0.5,
        'q_norm_g': 1.0 + 0.01 * nrm(ks[10], (DEPTH, HEAD_DIM), f32),
        'k_norm_g': 1.0 + 0.01 * nrm(ks[11], (DEPTH, HEAD_DIM), f32),
        'rpb': 0.1 * nrm(ks[12], (DEPTH, N_HEADS, 2 * MAX_KH - 1, 2 * KW - 1), f32),
        'conv_w': nrm(ks[13], (DEPTH, 3, 3 * HY_W), f32) * 3 ** -0.5,
        'conv_b': 0.02 * nrm(ks[14], (DEPTH, 3 * HY_W), f32),
        'f_w1': nrm(ks[15], (DEPTH, FILTER_EMB, FILTER_FF), f32) * FILTER_EMB ** -0.5,
        'f_b1': 0.1 * nrm(ks[16], (DEPTH, FILTER_FF), f32),
        'f_freq': 1.0 + 0.01 * nrm(ks[17], (DEPTH, FILTER_FF), f32),
        'f_w2': nrm(ks[18], (DEPTH, FILTER_FF, FILTER_FF), f32) * FILTER_FF ** -0.5,
        'f_b2': 0.1 * nrm(ks[19], (DEPTH, FILTER_FF), f32),
        'f_w3': nrm(ks[20], (DEPTH, FILTER_FF, HYENA_ORDER * 2 * HY_W), f32) * FILTER_FF ** -0.5,
        'hy_bias': 0.5 * nrm(ks[21], (DEPTH, HYENA_ORDER, HY_W), f32),
        'w_br': nrm(ks[22], (DEPTH, W_MIX, D_MODEL), f32) * ATT_W ** -0.5,
        'w_out': nrm(ks[23], (DEPTH, D_MODEL, D_MODEL), f32) * D_MODEL ** -0.5,
    }


def reference(x_prompt, x_sample, cache_k, cache_v, c, c_ctx, norm_g, w_ada, b_ada, w_in,
              q_norm_g, k_norm_g, rpb, conv_w, conv_b, f_w1, f_b1, f_freq, f_w2, f_b2,
              f_w3, hy_bias, w_br, w_out):
    len_prompt = x_prompt.shape[1]
    len_sample = x_sample.shape[1]

    y = x_prompt
    ks, vs = [], []
    for l in range(DEPTH):
        proj, gate = _modulated_proj(y, c_ctx[None, :], norm_g[l], w_ada[l], b_ada[l], w_in[l])
        q, k, v = _qkv(proj, q_norm_g[l], k_norm_g[l])
        ks.append(k)
        vs.append(v)
        att = _context_attention(q, k, v)
        kf = _hyena_filter_fft(len_prompt, f_w1[l], f_b1[l], f_freq[l], f_w2[l], f_b2[l], f_w3[l])
        y = _branches_residual(y, proj, gate, att, conv_w[l], conv_b[l], kf, hy_bias[l],
                               w_br[l], w_out[l])
    new_k = jnp.stack(ks, axis=1)
    new_v = jnp.stack(vs, axis=1)

    z = x_sample
    for l in range(DEPTH):
        proj, gate = _modulated_proj(z, c, norm_g[l], w_ada[l], b_ada[l], w_in[l])
        q, k, v = _qkv(proj, q_norm_g[l], k_norm_g[l])
        att = _latent_attention(q, k, v, cache_k[:, l], cache_v[:, l], rpb[l])
        kf = _hyena_filter_fft(len_sample, f_w1[l], f_b1[l], f_freq[l], f_w2[l], f_b2[l], f_w3[l])
        z = _branches_residual(z, proj, gate, att, conv_w[l], conv_b[l], kf, hy_bias[l],
                               w_br[l], w_out[l])

    return (y, z, new_k, new_v)
```

```python
import numpy as np
import concourse.bass as bass
import concourse.mybir as mybir

F32 = mybir.dt.float32
BF16 = mybir.dt.bfloat16
AF = mybir.ActivationFunctionType
ALU = mybir.AluOpType

COMPUTE = ("pe", "act", "dve", "pool")
NDMA_SEMS = 26
NSW = 6


class Buf:
    __slots__ = ("name", "w", "r")

    def __init__(self, name=""):
        self.name = name
        self.w = None
        self.r = []


class Sched:
    def __init__(self, nc):
        self.nc = nc
        self.prog = {e: [] for e in ("pe", "act", "dve", "pool", "sp")}
        self.flag = {e: set() for e in COMPUTE}
        self.ninstr = {e: 0 for e in COMPUTE}
        self.dma_n = [0] * NDMA_SEMS
        self.dma_rr = 0
        self.sw_rr = 0
        self.all_out_tokens = []
        self.pending_fence = {}

    def _collect(self, reads, writes):
        deps = []
        for b in reads:
            if b.w is not None:
                deps.append(b.w)
        for b in writes:
            if b.w is not None:
                deps.append(b.w)
            deps.extend(b.r)
        return deps

    def _commit(self, tok, reads, writes):
        for b in reads:
            b.r.append(tok)
        for b in writes:
            b.w = tok
            b.r = []

    def fence(self):
        toks = []
        for e in COMPUTE:
            if self.ninstr[e] > 0:
                toks.append(("c", e, self.ninstr[e] - 1))
        for k in range(NDMA_SEMS):
            if self.dma_n[k] > 0:
                toks.append(("d", k, self.dma_n[k]))
        self.pending_fence = {q: list(toks) for q in self.prog}

    def op(self, eng, fn, reads=(), writes=()):
        deps = self._collect(reads, writes)
        deps += self.pending_fence.pop(eng, [])
        idx = self.ninstr[eng]
        self.ninstr[eng] += 1
        tok = ("c", eng, idx)
        self.prog[eng].append([deps, fn, tok])
        self._commit(tok, reads, writes)
        return tok

    def dma(self, q, fn, reads=(), writes=(), is_output=False):
        deps = self._collect(reads, writes)
        deps += self.pending_fence.pop(q, [])
        if q == "pool":
            k = self.sw_rr
            self.sw_rr = (self.sw_rr + 1) % NSW
        else:
            k = NSW + self.dma_rr
            self.dma_rr = (self.dma_rr + 1) % (NDMA_SEMS - NSW)
        if self.dma_n[k] > 0:
            deps.append(("d", k, self.dma_n[k]))
        self.dma_n[k] += 1
        tok = ("d", k, self.dma_n[k])
        self.prog[q].append([deps, fn, tok])
        self._commit(tok, reads, writes)
        if is_output:
            self.all_out_tokens.append(tok)
        return tok

    def emit(self):
        nc = self.nc
        for e, items in self.prog.items():
            for deps, fn, tok in items:
                for d in deps:
                    if d[0] == "c":
                        if d[1] == "pe" and e == "pe":
                            continue
                        self.flag[d[1]].add(d[2])
        final_deps = list(self.all_out_tokens)
        rank = {}
        for e in COMPUTE:
            r = 0
            m = {}
            fl = self.flag[e]
            for i in range(self.ninstr[e]):
                if i in fl:
                    r += 1
                    m[i] = r
            rank[e] = m
        self.rank = rank

        def tokval(d):
            if d[0] == "c":
                return ("c", d[1]), rank[d[1]][d[2]]
            return ("d", d[1]), 16 * d[2]

        import contextlib
        with contextlib.ExitStack() as st:
            sems = {}
            for e in COMPUTE:
                sems[("c", e)] = st.enter_context(nc.semaphore("sem_" + e))
            for k in range(NDMA_SEMS):
                sems[("d", k)] = st.enter_context(nc.semaphore("sem_dma%d" % k))
            block = st.enter_context(nc.Block())

            def run(ekey, engine_obj, extra_final=None):
                seen = {}
                for deps, fn, tok in self.prog[ekey]:
                    need = {}
                    for d in deps:
                        if d[0] == "c" and d[1] == "pe" and ekey == "pe":
                            continue
                        key, val = tokval(d)
                        if seen.get(key, 0) < val:
                            if need.get(key, 0) < val:
                                need[key] = val
                    for key, val in need.items():
                        engine_obj.wait_ge(sems[key], val)
                        seen[key] = val
                    ins = fn(engine_obj)
                    if tok[0] == "c":
                        if tok[2] in self.flag[tok[1]]:
                            ins.then_inc(sems[("c", tok[1])], 1)
                    else:
                        ins.then_inc(sems[("d", tok[1])], 16)
                if extra_final:
                    need = {}
                    for d in extra_final:
                        key, val = tokval(d)
                        if need.get(key, 0) < val:
                            need[key] = val
                    for key, val in need.items():
                        engine_obj.wait_ge(sems[key], val)

            @block.sync
            def _(e):
                run("sp", e, final_deps)

            @block.tensor
            def _(e):
                run("pe", e)

            @block.scalar
            def _(e):
                run("act", e)

            @block.vector
            def _(e):
                run("dve", e)

            @block.gpsimd
            def _(e):
                run("pool", e, final_deps)

import math
from contextlib import ExitStack
import ml_dtypes
from concourse.bass_utils import run_bass_kernel_spmd

AX = mybir.AxisListType
NT = 3072
DM = 2048
NIN = 13312
NL = 2
EPS = 1e-6
MIN_DECAY = math.log(1e-2) / 1.5
MAX_DECAY = math.log(1e-2) / 0.3
DEBUG = False
NL_RUN = 2
STAGES = None

_bf = lambda a: np.ascontiguousarray(a.astype(ml_dtypes.bfloat16))
_f32 = lambda a: np.ascontiguousarray(a, dtype=np.float32)

_CONST = None


def make_consts():
    global _CONST
    if _CONST is not None:
        return _CONST
    c = {}
    c["ident"] = np.eye(128, dtype=np.float32)
    n = np.arange(128)
    ang = 2 * np.pi * (np.outer(n, n) % 128) / 128
    c["fnCS"] = _bf(np.concatenate([np.cos(ang), np.sin(ang)], 1) / np.sqrt(128))
    for L in (256, 2048):
        l = np.arange(L)
        ang = 2 * np.pi * (np.outer(l, l) % L) / L
        c["fnC%d" % L] = _bf(np.cos(ang) / np.sqrt(L))
        c["fnSn%d" % L] = _bf(-np.sin(ang) / np.sqrt(L))
        N = 2 * L
        m = np.outer(2 * l + 1, 2 * l + 1) % (4 * N)
        ang = 2 * np.pi * m / (4 * N)
        c["hyC%d" % L] = _bf(np.cos(ang))
        c["hyS%d" % L] = _bf(np.sin(ang))
        ph = np.pi * (l + 0.5) / N
        tab = np.stack([np.cos(ph) * 2 / N, np.sin(ph) * 2 / N], -1)
        c["phi%d" % L] = _f32(tab.reshape(L // 128, 128, 2).transpose(1, 0, 2))
        t = np.linspace(0.0, 1.0, L, dtype=np.float32)[:, None]
        w = (np.float32(2.0 * math.pi / L) * np.arange(L, dtype=np.float32))[:, None]
        fr = np.linspace(1e-4, 15, 16, dtype=np.float32)[None, :]
        z = np.concatenate([t, np.cos(w * fr), -np.sin(w * fr)], -1).astype(np.float32)
        c["zfT%d" % L] = _f32(z.T)
        deltas = np.abs(np.linspace(MIN_DECAY, MAX_DECAY, 512, dtype=np.float32))
        c["decay%d" % L] = _f32(np.concatenate([np.exp(-t * deltas[None, :]), np.zeros((1, 512), np.float32)], 0))
    p = np.arange(128)
    cp = p % 64
    cq = np.arange(64)
    cs = np.clip(cq - 8, 0, 48)
    valid = (cp[:, None] >= cs[None, :]) & (cp[:, None] < cs[None, :] + 16)
    c["nmask"] = _f32(np.broadcast_to(valid[:, None, :], (128, 14, 64)))
    _CONST = c
    return c


def rpb_gather(rpb):
    p = np.arange(128)
    cp = p % 64
    half = p // 64
    cq = np.arange(64)
    dc = cp[:, None] - cq[None, :] + 15
    ok = (dc >= 0) & (dc <= 30)
    dcc = np.clip(dc, 0, 30)
    dr = np.arange(14)
    drr = dr[None, :] + half[:, None]
    out = rpb[:, :, drr[:, :, None], dcc[:, None, :]]
    out = np.where(ok[None, None, :, None, :], out, 0.0)
    return _f32(out.reshape(NL, 8, 128, 14 * 64))


def build_program():
    nc = bass.Bass("TRN2", target_bir_lowering=False)
    S = Sched(nc)
    C = make_consts()

    def din(name, shape, dt=F32):
        return nc.dram_tensor(name, list(shape), dt, kind="ExternalInput").ap()

    def dscr(name, shape, dt=F32):
        kind = "ExternalOutput" if DEBUG else "Internal"
        return nc.dram_tensor(name, list(shape), dt, kind=kind).ap()

    xin = din("xin", [DM, NT])
    ck = din("ck", [NL, 512, 1024])
    cv = din("cv", [NL, 512, 1024])
    cvecT = din("cvecT", [128, 16, 2])
    norm_gT = din("norm_gT", [NL, 128, 16])
    b_adaT = din("b_adaT", [NL, 128, 48])
    w_ada = din("w_ada", [NL, DM, 3 * DM])
    w_in = din("w_in", [NL, DM, NIN])
    w_br = din("w_br", [NL, DM, DM])
    w_out = din("w_out", [NL, DM, DM])
    qg = din("qg", [NL, 128, 1])
    kg = din("kg", [NL, 128, 1])
    kg_rep = din("kg_rep", [NL, 128, 128])
    rpbT2 = din("rpbT2", [NL, 8, 128, 14 * 64])
    conv_wT = din("conv_wT", [NL, 128, 12, 3])
    conv_bT = din("conv_bT", [NL, 128, 12])
    hy_biasT = din("hy_biasT", [NL, 128, 2, 4])
    f_w1 = din("f_w1", [NL, 33, 64])
    f_b1T = din("f_b1T", [NL, 64, 1])
    f_freqT = din("f_freqT", [NL, 64, 1])
    f_w2 = din("f_w2", [NL, 64, 64])
    f_b2T = din("f_b2T", [NL, 64, 1])
    f_w3 = din("f_w3", [NL, 64, 2048])
    cd = {}
    for k, v in C.items():
        cd[k] = din("c_" + k, v.shape, BF16 if v.dtype == ml_dtypes.bfloat16 else F32)

    y_out = nc.dram_tensor("y_out", [DM, NT], F32, kind="ExternalOutput").ap()
    newk = nc.dram_tensor("newk", [NL, 1024, 1024], F32, kind="ExternalOutput").ap()
    newv = nc.dram_tensor("newv", [NL, 1024, 1024], F32, kind="ExternalOutput").ap()

    xT = dscr("s_xT", [DM, NT])
    qT = dscr("s_qT", [1024, NT], BF16)
    kT = dscr("s_kT", [1024, NT], BF16)
    vtok = dscr("s_vtok", [NT + 128, 1024], BF16)
    gT = dscr("s_gT", [2048, NT], BF16)
    ubT = dscr("s_ubT", [512, NT], BF16)
    hcT = dscr("s_hcT", [1536, NT])
    hccT = dscr("s_hccT", [1536, NT])
    z1T = dscr("s_z1T", [512, NT])
    mgT = dscr("s_mgT", [6144, NT], BF16)
    yT = dscr("s_yT", [2048, NT], BF16)
    ebs = dscr("s_eb", [8, 128, 14 * 64])
    w16 = dscr("s_w16", [8, 128, 16 * 512], BF16)
    hd_s = {L: dscr("s_hd%d" % L, [L + 128, 2048]) for L in (256, 2048)}
    kf_s = {L: dscr("s_kf%d" % L, [2, 2, L, 512]) for L in (256, 2048)}

    def sbp(name, shape, dt=F32):
        return nc.alloc_sbuf_tensor(name, list(shape), dt)

    ident = sbp("ident", [128, 128]); b_ident = Buf()
    identb = sbp("identb", [128, 128], BF16)
    onesb = sbp("onesb", [128, 128], BF16)
    onesf = sbp("onesf", [128, 128])
    epsc = sbp("epsc", [128, 1])
    zeroc = sbp("zeroc", [128, 1])
    b_const = Buf()
    adaT = [sbp("adaT%d" % l, [128, 48, 2]) for l in range(NL)]
    Gm = [sbp("Gm%d" % l, [128, 16, 2]) for l in range(NL)]
    b_ada = Buf()
    small = {}
    for l in range(NL):
        small[l] = dict(
            ng=sbp("ng%d" % l, [128, 16]), ba=sbp("ba%d" % l, [128, 48]),
            qg=sbp("qg%d" % l, [128, 1]), kg=sbp("kg%d" % l, [128, 1]),
            kgr=sbp("kgr%d" % l, [128, 128]),
            cw=sbp("cw%d" % l, [128, 12, 3]), cb=sbp("cb%d" % l, [128, 12]),
            hb=sbp("hb%d" % l, [128, 2, 4]),
        )
    b_small = Buf()

    pst = [nc.alloc_psum_tensor("ps%d" % i, [128, 512], F32) for i in range(8)]
    psb = [Buf() for _ in range(8)]
    ps_avail = list(range(8))
    psi = [0]

    def PS():
        psi[0] = (psi[0] + 1) % len(ps_avail)
        i = ps_avail[psi[0]]
        return pst[i], psb[i]

    def PS_hold(n):
        held = [ps_avail.pop() for _ in range(n)]
        psi[0] = 0
        return [(pst[i], psb[i]) for i in held], held

    def PS_release(held):
        ps_avail.extend(held)

    uid = [0]

    def uname(name):
        uid[0] += 1
        return "%s_%d" % (name, uid[0])

    class Ring:
        def __init__(self, st, name, shape, dt, n):
            self.t = [st.enter_context(nc.sbuf_tensor(uname(name), list(shape), dt)) for i in range(n)]
            self.b = [Buf() for _ in range(n)]
            self.i = 0

        def next(self):
            i = self.i
            self.i = (i + 1) % len(self.t)
            return self.t[i], self.b[i]

    class WStream:
        def __init__(self, ring, srcs, pre=False):
            self.ring = ring; self.srcs = srcs; self.i = 0; self.pre = pre
            self.cur = self._issue(0)

        def _issue(self, i):
            if i >= len(self.srcs):
                return None
            wt, wb = self.ring.next()
            if self.pre:
                LD(wt[:].rearrange("p k n -> p (k n)"), self.srcs[i], [wb], q="pool")
            else:
                LD(wt[:], self.srcs[i].rearrange("(k p) n -> p k n", p=128), [wb], q="pool")
            return wt, wb

        def get(self):
            c = self.cur
            self.i += 1
            self.cur = self._issue(self.i)
            return c

    def sb(st, name, shape, dt=F32):
        return st.enter_context(nc.sbuf_tensor(uname(name), list(shape), dt))

    def MM(ps, lhsT, rhs, start, stop, reads, pb, skip=False):
        S.op("pe", lambda e: e.matmul(ps, lhsT=lhsT, rhs=rhs, start=start, stop=stop,
                                      skip_group_check=skip), reads=reads, writes=[pb])

    def TR(ps, in_, idt, reads, pb):
        S.op("pe", lambda e: e.transpose(out=ps, in_=in_, identity=idt), reads=reads + [b_ident], writes=[pb])

    def ACT(out, in_, func, reads, writes, scale=None, bias=None):
        kw = {}
        if scale is not None:
            kw["scale"] = scale
        if bias is not None:
            kw["bias"] = bias
        S.op("act", lambda e: e.activation(out=out, in_=in_, func=func, **kw), reads=reads, writes=writes)

    def TT(out, in0, in1, op, reads, writes, eng="dve"):
        S.op(eng, lambda e: e.tensor_tensor(out=out, in0=in0, in1=in1, op=op), reads=reads, writes=writes)

    def TS(out, in0, s1, s2, op0, op1, reads, writes, eng="dve"):
        if op1 is None:
            S.op(eng, lambda e: e.tensor_scalar(out=out, in0=in0, scalar1=s1, scalar2=None, op0=op0), reads=reads, writes=writes)
        else:
            S.op(eng, lambda e: e.tensor_scalar(out=out, in0=in0, scalar1=s1, scalar2=s2, op0=op0, op1=op1), reads=reads, writes=writes)

    def STT(out, in0, scalar, in1, op0, op1, reads, writes):
        S.op("dve", lambda e: e.scalar_tensor_tensor(out=out, in0=in0, scalar=scalar, in1=in1, op0=op0, op1=op1),
             reads=reads, writes=writes)

    def RED(out, in_, reads, writes):
        S.op("dve", lambda e: e.tensor_reduce(out=out, in_=in_, axis=AX.X, op=ALU.add), reads=reads, writes=writes)

    def CP(out, in_, reads, writes, eng="dve"):
        S.op(eng, lambda e: e.tensor_copy(out=out, in_=in_), reads=reads, writes=writes)

    def LD(out, in_, writes, reads=(), q="sp"):
        S.dma(q, lambda e: e.dma_start(out=out, in_=in_), reads=list(reads), writes=list(writes))

    STQ = ["sp"]

    def STo(out, in_, reads, writes=(), is_output=False):
        S.dma(STQ[0], lambda e: e.dma_start(out=out, in_=in_), reads=list(reads), writes=list(writes), is_output=is_output)

    def rstd_from(ps_sum, out, scale, reads, writes):
        ACT(out, ps_sum, AF.Ln, reads + [b_const], writes, scale=scale, bias=epsc[:out.shape[0], 0:1])
        ACT(out, out, AF.Exp, writes, writes, scale=-0.5)

    LD(ident[:], cd["ident"][:, :], [b_ident])
    S.op("dve", lambda e: e.tensor_copy(out=identb[:], in_=ident[:]), reads=[b_ident], writes=[b_ident])
    S.op("dve", lambda e: e.memset(onesb[:], 1.0), writes=[b_const])
    S.op("dve", lambda e: e.memset(onesf[:], 1.0), writes=[b_const])
    S.op("dve", lambda e: e.memset(epsc[:], EPS), writes=[b_const])
    S.op("dve", lambda e: e.memset(zeroc[:], 0.0), writes=[b_const])
    for l in range(NL):
        sm = small[l]
        LD(sm["ng"][:], norm_gT[l], [b_small]); LD(sm["ba"][:], b_adaT[l], [b_small])
        LD(sm["qg"][:], qg[l], [b_small]); LD(sm["kg"][:], kg[l], [b_small])
        LD(sm["kgr"][:], kg_rep[l], [b_small])
        LD(sm["cw"][:], conv_wT[l], [b_small]); LD(sm["cb"][:], conv_bT[l], [b_small])
        LD(sm["hb"][:], hy_biasT[l], [b_small])
        TS(sm["qg"][:], sm["qg"][:], float(128 ** -0.5), None, ALU.mult, None, [b_small], [b_small])

    with ExitStack() as st:
        war = Ring(st, "wada", [128, 16, 512], BF16, 3)
        sil0 = sb(st, "sil0", [128, 16, 2]); b_sil = Buf()
        sil = sb(st, "sil", [128, 16, 2], BF16)
        LD(sil0[:], cvecT[:, :, :], [b_sil])
        ACT(sil[:], sil0[:], AF.Silu, [b_sil], [b_sil])
        for l in range(NL):
            ps, pb = PS()
            for sbk in range(12):
                wt, wb = war.next()
                LD(wt[:], w_ada[l, :, sbk * 512:(sbk + 1) * 512].rearrange("(k p) n -> p k n", p=128), [wb], q="pool")
                for j4 in range(4):
                    j = sbk * 4 + j4
                    for kc in range(16):
                        MM(ps[:, 2 * j:2 * j + 2], wt[:, kc, j4 * 128:(j4 + 1) * 128], sil[:, kc, :],
                           kc == 0, kc == 15, [wb, b_sil], pb)
            for c in range(2):
                TT(adaT[l][:, :, c], ps[:, c:96:2], small[l]["ba"][:], ALU.add, [pb, b_small], [b_ada])
                STT(Gm[l][:, :, c], adaT[l][:, 16:32, c], 1.0, small[l]["ng"][:], ALU.add, ALU.mult,
                    [b_ada, b_small], [b_ada])
        S.fence()

    with ExitStack() as st:
        zpad = sb(st, "zpad", [128, 1024], BF16); bzp = Buf()
        S.op("dve", lambda e: e.memset(zpad[:], 0.0), writes=[bzp])
        STo(vtok[NT:NT + 128, :], zpad[:], [bzp])
        S.fence()

    def x_src(l):
        return xin if l == 0 else xT

    def x_dst(l):
        return y_out if l == NL_RUN - 1 else xT

    marks = [('start', dict(S.ninstr))]
    build_program.marks = marks

    def hyena_filter_full(l, L):
        STQ[0] = "sp"
        nch = L // 128
        kf = kf_s[L]
        with ExitStack() as st:
            hb = sb(st, "fhb", [128, nch, 2048], BF16); bhbs = [Buf() for _ in range(nch)]
            recs = sb(st, "frecs", [128, 2048]); brec = Buf()
            with ExitStack() as st1:
                w1 = sb(st1, "fw1", [33, 64]); w2 = sb(st1, "fw2", [64, 64]); w3 = sb(st1, "fw3", [64, 2048])
                zf = sb(st1, "fzf", [33, L])
                b1 = sb(st1, "fb1", [64, 1]); b2 = sb(st1, "fb2", [64, 1]); fq = sb(st1, "ffq", [64, 1])
                h1 = sb(st1, "fh1", [64, L]); h2 = sb(st1, "fh2", [64, L + 1])
                tmp = Ring(st1, "ftmp", [64, 512], F32, 2)
                tmp2 = Ring(st1, "ftmp2", [64, 512], F32, 2)
                bw = Buf(); bh1 = Buf(); bh2 = Buf()
                LD(w1[:], f_w1[l], [bw]); LD(w2[:], f_w2[l], [bw]); LD(w3[:], f_w3[l], [bw])
                LD(zf[:], cd["zfT%d" % L][:, :], [bw])
                LD(b1[:], f_b1T[l], [bw]); LD(b2[:], f_b2T[l], [bw]); LD(fq[:], f_freqT[l], [bw])

                def sin_layer(w, bcol, src, bsrc, K, dst, bdst):
                    for c0 in range(0, L, 512):
                        n = min(512, L - c0)
                        ps, pb = PS()
                        MM(ps[:64, :n], w[:K, :], src[:K, c0:c0 + n], True, True, [bw, bsrc], pb)
                        a, ab = tmp.next()
                        TS(a[:, :n], ps[:64, :n], bcol[:, 0:1], fq[:, 0:1], ALU.add, ALU.mult, [pb, bw], [ab])
                        m, mb = tmp2.next()
                        TS(m[:, :n], a[:, :n], float(np.pi), float(-2 * np.pi), ALU.is_gt, ALU.mult, [ab], [mb])
                        TT(a[:, :n], a[:, :n], m[:, :n], ALU.add, [ab, mb], [ab])
                        TS(m[:, :n], a[:, :n], float(-np.pi), float(2 * np.pi), ALU.is_lt, ALU.mult, [ab], [mb])
                        TT(a[:, :n], a[:, :n], m[:, :n], ALU.add, [ab, mb], [ab])
                        TS(a[:, :n], a[:, :n], float(np.pi), float(-np.pi), ALU.min, ALU.max, [ab], [ab])
                        ACT(dst[:, c0:c0 + n], a[:, :n], AF.Sin, [ab], [bdst])

                S.op("dve", lambda e: e.memset(h2[:, L:L + 1], 0.0), writes=[bh2])
                sin_layer(w1, b1, zf, bw, 33, h1, bh1)
                sin_layer(w2, b2, h1, bh1, 64, h2, bh2)
                dsr = Ring(st1, "fdsh", [128, 512], F32, 2)
                dec = Ring(st1, "fdec", [128, 512], F32, 2)
                hdr = Ring(st1, "fhd", [128, 2048], F32, 2)
                habs = Ring(st1, "fhabs", [128, 2048], BF16, 3)
                sps, sheld = PS_hold(4)
                fpend = []
                for dc in range(nch):
                    dt_, db_ = dec.next()
                    LD(dt_[:], cd["decay%d" % L][dc * 128:(dc + 1) * 128, :], [db_])
                    ht, hb_ = hdr.next()
                    for cs in range(4):
                        ps, pb = PS()
                        MM(ps[:, :], h2[:, dc * 128:(dc + 1) * 128], w3[:, cs * 512:(cs + 1) * 512], True, True, [bh2, bw], pb)
                        TT(ht[:, cs * 512:(cs + 1) * 512], ps[:, :], dt_[:], ALU.mult, [pb, db_], [hb_])
                    for o in range(2):
                        CP(hb[:, dc, o * 1024:o * 1024 + 512], ht[:, o * 1024:o * 1024 + 512], [hb_], [bhbs[dc]], eng="pool")
                    ds_, dsb = dsr.next()
                    LD(ds_[:], cd["decay%d" % L][dc * 128 + 1:(dc + 1) * 128 + 1, :], [dsb])
                    for o in range(2):
                        ps, pb = PS()
                        MM(ps[:, :], h2[:, dc * 128 + 1:(dc + 1) * 128 + 1], w3[:, o * 1024 + 512:o * 1024 + 1024], True, True, [bh2, bw], pb)
                        TT(hb[:, dc, o * 1024 + 512:o * 1024 + 1024], ps[:, :], ds_[:], ALU.mult, [pb, dsb], [bhbs[dc]])
                    at, ab_ = habs.next()
                    ACT(at[:], ht[:], AF.Abs, [hb_], [ab_])
                    if fpend:
                        fpend.pop(0)()

                    def ones_mm(at=at, ab_=ab_, dc=dc):
                        for cs in range(4):
                            MM(sps[cs][0][:, :], onesb[:], at[:, cs * 512:(cs + 1) * 512], dc == 0, dc == nch - 1,
                               [b_const, ab_], sps[cs][1])
                    fpend.append(ones_mm)
                while fpend:
                    fpend.pop(0)()
                for cs in range(4):
                    TS(recs[:, cs * 512:(cs + 1) * 512], sps[cs][0][:, :], EPS, None, ALU.add, None, [sps[cs][1]], [brec])
                S.op("dve", lambda e: e.reciprocal(out=recs[:], in_=recs[:]), reads=[brec], writes=[brec])
                nt = Ring(st1, "fnt", [128, 512], F32, 4)
                for dc in range(nch):
                    for o in range(2):
                        cF = slice(o * 1024, o * 1024 + 512); cB = slice(o * 1024 + 512, o * 1024 + 1024)
                        t1, t1b = nt.next(); t2, t2b = nt.next()
                        TT(t1[:], hb[:, dc, cF], recs[:, cF], ALU.mult, [bhbs[dc], brec], [t1b])
                        TT(t2[:], hb[:, dc, cB], recs[:, cB], ALU.mult, [bhbs[dc], brec], [t2b], eng="pool")
                        TT(hb[:, dc, cF], t1[:], t2[:], ALU.add, [t1b, t2b], [bhbs[dc]])
                        TT(hb[:, dc, cB], t1[:], t2[:], ALU.subtract, [t1b, t2b], [bhbs[dc]], eng="pool")
                PS_release(sheld)
                S.fence()
            with ExitStack() as st2:
                phi = sb(st2, "fphi", [128, nch, 2]); bphi = Buf()
                LD(phi[:], cd["phi%d" % L][:, :, :], [bphi])
                nsl = min(512, L)
                cr = Ring(st2, "fC", [128, nch, nsl], BF16, 2)
                sr = Ring(st2, "fS", [128, nch, nsl], BF16, 2)
                ko = Ring(st2, "fko", [128, 2, 1024], F32, 2)
                tr_ = Ring(st2, "ft", [128, 512], F32, 8)
                for f0 in range(0, L, nsl):
                    ct, cb_ = cr.next(); st_, sb__ = sr.next()
                    LD(ct[:], cd["hyC%d" % L][:, f0:f0 + nsl].rearrange("(k p) f -> p k f", p=128), [cb_])
                    LD(st_[:], cd["hyS%d" % L][:, f0:f0 + nsl].rearrange("(k p) f -> p k f", p=128), [sb__])
                    for fs in range(nsl // 128):
                        fc = f0 // 128 + fs
                        fsl = slice(fs * 128, (fs + 1) * 128)
                        cph = phi[:, fc, 0:1]; sph = phi[:, fc, 1:2]
                        kt, kb_ = ko.next()
                        for o in range(2):
                            cF = slice(o * 1024, o * 1024 + 512); cB = slice(o * 1024 + 512, o * 1024 + 1024)
                            psA, pbA = PS(); psB, pbB = PS()
                            for dc in range(nch):
                                MM(psA[:, :], ct[:, dc, fsl], hb[:, dc, cF], dc == 0, dc == nch - 1, [cb_, bhbs[dc]], pbA)
                            for dc in range(nch):
                                MM(psB[:, :], st_[:, dc, fsl], hb[:, dc, cB], dc == 0, dc == nch - 1, [sb__, bhbs[dc]], pbB)
                            t1, t1b = tr_.next(); t2, t2b = tr_.next()
                            TS(t1[:], psB[:, :], sph, None, ALU.mult, None, [pbB, bphi], [t1b])
                            STT(kt[:, 0, o * 512:(o + 1) * 512], psA[:, :], cph, t1[:], ALU.mult, ALU.add, [pbA, bphi, t1b], [kb_])
                            TS(t2[:], psB[:, :], cph, None, ALU.mult, None, [pbB, bphi], [t2b])
                            STT(kt[:, 1, o * 512:(o + 1) * 512], psA[:, :], sph, t2[:], ALU.mult, ALU.subtract, [pbA, bphi, t2b], [kb_])
                        for ri in range(2):
                            STo(kf[ri, :, fc * 128:(fc + 1) * 128, :].rearrange("o f c -> f o c"),
                                kt[:, ri, :].rearrange("p (o c) -> p o c", o=2), [kb_])
                S.fence()

    def pass1(l):
        STQ[0] = "sp"
        sm = small[l]
        with ExitStack() as st:
            hTs = [sb(st, "hT", [128, 16, 1024], BF16) for _ in range(2)]
            b_hTs = [Buf(), Buf()]
            wr = Ring(st, "wsb", [128, 16, 512], BF16, 2)
            xc = Ring(st, "xc", [128, 16, 512], F32, 2)
            sqr = Ring(st, "sq", [128, 512], BF16, 3)
            sqf = Ring(st, "sqf", [128, 512], F32, 2)
            accr = Ring(st, "acc", [128, 512], F32, 2)
            rsr = Ring(st, "rs", [128, 512], F32, 3)
            tmr = Ring(st, "tm", [128, 512], F32, 2)
            ef = Ring(st, "ef", [128, 512], F32, 4)
            eb = Ring(st, "eb", [128, 512], BF16, 4)
            sm4 = Ring(st, "sm4", [128, 8], F32, 2)
            loaded = {}

            def prep_load(blk, tc):
                xt_, xb_ = xc.next()
                t0 = blk * 1024
                LD(xt_[:], x_src(l)[:, t0 + tc * 512:t0 + (tc + 1) * 512].rearrange("(k p) t -> p k t", p=128), [xb_])
                loaded[(blk, tc)] = (xt_, xb_)

            def prep_compute(blk, tc):
                cond = 0 if blk == 0 else 1
                hT = hTs[blk % 2]; b_hT = b_hTs[blk % 2]
                xt_, xb_ = loaded.pop((blk, tc))
                psr, pbr = PS()
                if blk == 0:
                    for kc in range(16):
                        sq, sqb = sqr.next()
                        ACT(sq[:], xt_[:, kc, :], AF.Square, [xb_], [sqb])
                        MM(psr[:, :], onesb[:], sq[:], kc == 0, kc == 15, [b_const, sqb], pbr)
                else:
                    acc, accb = accr.next()
                    for kc in range(16):
                        if kc == 0:
                            TT(acc[:], xt_[:, 0, :], xt_[:, 0, :], ALU.mult, [xb_], [accb], eng="pool")
                        else:
                            sq, sqb = sqf.next()
                            TT(sq[:], xt_[:, kc, :], xt_[:, kc, :], ALU.mult, [xb_], [sqb], eng="pool")
                            TT(acc[:], acc[:], sq[:], ALU.add, [accb, sqb], [accb], eng="pool")
                    MM(psr[:, :], onesf[:], acc[:], True, True, [b_const, accb], pbr)
                rs, rsb = rsr.next()
                rstd_from(psr[:, :], rs[:], 1.0 / DM, [pbr], [rsb])
                for kc in range(16):
                    tm, tmb = tmr.next()
                    TT(tm[:], xt_[:, kc, :], rs[:], ALU.mult, [xb_, rsb], [tmb])
                    ACT(hT[:, kc, tc * 512:(tc + 1) * 512], tm[:], AF.Identity, [tmb, b_ada], [b_hT],
                        scale=Gm[l][:, kc, cond:cond + 1], bias=adaT[l][:, kc, cond:cond + 1])

            pending = []

            def flush():
                while pending:
                    pending.pop(0)()

            wstream = WStream(wr, [w_in[l, :, sbk * 512:(sbk + 1) * 512] for blk in range(3) for sbk in range(26)])
            prep_load(0, 0); prep_load(0, 1); prep_compute(0, 0); prep_compute(0, 1)
            for blk in range(3):
                t0 = blk * 1024
                hT = hTs[blk % 2]; b_hT = b_hTs[blk % 2]
                for sbk in range(26):
                    if blk + 1 < 3:
                        if sbk == 1:
                            prep_load(blk + 1, 0)
                        if sbk == 6:
                            prep_compute(blk + 1, 0)
                        if sbk == 8:
                            prep_load(blk + 1, 1)
                        if sbk == 14:
                            prep_compute(blk + 1, 1)
                    wt, wb = wstream.get()
                    fm = sbk not in (4, 5)
                    if fm:
                        for cb4 in range(4):
                            for tc in range(2):
                                ps, pb = PS()
                                for kc in range(16):
                                    MM(ps[:, :], wt[:, kc, cb4 * 128:(cb4 + 1) * 128], hT[:, kc, tc * 512:(tc + 1) * 512],
                                       kc == 0, kc == 15, [wb, b_hT], pb)
                                tsl = slice(t0 + tc * 512, t0 + (tc + 1) * 512)
                                if sbk < 4:
                                    raw, rb = ef.next()
                                    ACT(raw[:], ps[:, :], AF.Identity, [pb], [rb])
                                    sq, sqb = sqr.next()
                                    TT(sq[:], raw[:], raw[:], ALU.mult, [rb], [sqb])
                                    flush()

                                    def partB(raw=raw, rb=rb, sq=sq, sqb=sqb, sbk=sbk, cb4=cb4, tsl=tsl):
                                        ps2, pb2 = PS()
                                        MM(ps2[:, :], onesb[:], sq[:], True, True, [b_const, sqb], pb2)
                                        rs, rsb = rsr.next()
                                        rstd_from(ps2[:, :], rs[:], 1.0 / 128, [pb2], [rsb])
                                        o, ob = eb.next()
                                        gcol = sm["qg"] if sbk < 2 else sm["kg"]
                                        STT(o[:], raw[:], gcol[:, 0:1], rs[:], ALU.mult, ALU.mult, [rb, rsb, b_small], [ob])
                                        dst = qT if sbk < 2 else kT
                                        r0 = (sbk % 2) * 512 + cb4 * 128
                                        STo(dst[r0:r0 + 128, tsl], o[:], [ob])
                                    pending.append(partB)
                                    continue
                                flush()
                                if sbk in (6, 7, 9, 13):
                                    o, ob = eb.next()
                                    ACT(o[:], ps[:, :], AF.Silu, [pb], [ob])
                                    r0 = {6: 0, 7: 512, 9: 1024, 13: 1536}[sbk] + cb4 * 128
                                    STo(gT[r0:r0 + 128, tsl], o[:], [ob])
                                elif sbk == 8:
                                    o, ob = eb.next()
                                    CP(o[:], ps[:, :], [pb], [ob])
                                    STo(ubT[cb4 * 128:(cb4 + 1) * 128, tsl], o[:], [ob])
                                elif sbk in (10, 11, 12):
                                    o, ob = ef.next()
                                    CP(o[:], ps[:, :], [pb], [ob])
                                    r0 = (sbk - 10) * 512 + cb4 * 128
                                    STo(hcT[r0:r0 + 128, tsl], o[:], [ob])
                                else:
                                    o, ob = eb.next()
                                    ACT(o[:], ps[:, :], AF.Sigmoid, [pb], [ob])
                                    r0 = (sbk - 14) * 512 + cb4 * 128
                                    STo(mgT[r0:r0 + 128, tsl], o[:], [ob])
                    if sbk in (4, 5) or (sbk in (2, 3) and blk == 0):
                        for tt in range(8):
                            ps, pb = PS()
                            for kc in range(16):
                                MM(ps[:, :], hT[:, kc, tt * 128:(tt + 1) * 128], wt[:, kc, :], kc == 0, kc == 15, [wb, b_hT], pb)
                            flush()
                            raw, rb = ef.next()
                            ACT(raw[:], ps[:, :], AF.Identity, [pb], [rb])
                            tok0 = t0 + tt * 128
                            if sbk in (4, 5):
                                c0 = (sbk - 4) * 512
                                if blk == 0:
                                    STo(newv[l, tok0:tok0 + 128, c0:c0 + 512], raw[:], [rb], is_output=True)
                                o, ob = eb.next()
                                CP(o[:], raw[:], [rb], [ob])
                                STo(vtok[tok0:tok0 + 128, c0:c0 + 512], o[:], [ob])
                            else:
                                c0 = (sbk - 2) * 512
                                sq, sqb = ef.next()
                                TT(sq[:], raw[:], raw[:], ALU.mult, [rb], [sqb])
                                s4, s4b = sm4.next()
                                RED(s4[:, 0:4], sq[:].rearrange("p (h d) -> p h d", h=4), [sqb], [s4b])
                                rstd_from(s4[:, 0:4], s4[:, 4:8], 1.0 / 128, [s4b], [s4b])
                                for h in range(4):
                                    STT(sq[:, h * 128:(h + 1) * 128], raw[:, h * 128:(h + 1) * 128], s4[:, 4 + h:5 + h],
                                        sm["kgr"][:], ALU.mult, ALU.mult, [rb, s4b, b_small, sqb], [sqb])
                                STo(newk[l, tok0:tok0 + 128, c0:c0 + 512], sq[:], [sqb], is_output=True)
                flush()
            S.fence()

    def attention(l):
        sm = small[l]
        STQ[0] = "pool"
        with ExitStack() as st:
            tr_ = Ring(st, "abt", [128, 14 * 64], F32, 2)
            mk = sb(st, "amask", [128, 14 * 64]); bmk = Buf()
            LD(mk[:], cd["nmask"].rearrange("p a c -> p (a c)"), [bmk])
            for h in range(8):
                t, tb = tr_.next()
                LD(t[:], rpbT2[l, h], [tb])
                ACT(t[:], t[:], AF.Exp, [tb], [tb])
                TT(t[:], t[:], mk[:], ALU.mult, [tb, bmk], [tb])
                STo(ebs[h], t[:], [tb])
            S.fence()
        with ExitStack() as st:
            qr = Ring(st, "aq", [128, 2048], BF16, 3)
            kr = Ring(st, "ak", [128, 2048], BF16, 3)
            v0r = Ring(st, "av0", [128, 16, 128], BF16, 3)
            v1r = Ring(st, "av1", [128, 16, 128], BF16, 2)
            ckr = Ring(st, "ack", [128, 4, 128], F32, 2)
            cktr = Ring(st, "ackT", [128, 512], BF16, 2)
            cvr = Ring(st, "acv", [128, 4, 128], BF16, 2)
            ebr = Ring(st, "aeb", [128, 14, 64], F32, 2)
            pcr = Ring(st, "apc", [128, 4, 512], BF16, 2)
            pwr = Ring(st, "apw", [128, 8, 64], BF16, 6)
            gar = Ring(st, "aga", [128, 512], BF16, 3)
            rcr = Ring(st, "arc", [128, 512], F32, 2)
            t2r = Ring(st, "at2", [128, 512], F32, 2)
            yor = Ring(st, "ayo", [128, 512], BF16, 2)
            chr_ = Ring(st, "yh", [128, NT], F32, 2)
            cor_ = Ring(st, "yo", [128, NT], F32, 2)
            segs = [(s_ * 256, 256) for s_ in range(4)] + [(1024, 2048)]
            conv_todo = list(range(12))
            wcr = Ring(st, "awc", [128, 16, 512], BF16, 2)
            wc_todo = [(w, sbk) for w in (w_br, w_out) for sbk in range(4)]
            wc_pend = []

            def wcast_unit():
                while wc_pend:
                    wc_pend.pop(0)()
                if not wc_todo:
                    return
                w, sbk = wc_todo.pop(0)
                idx = 7 - len(wc_todo)
                wt, wb = wcr.next()
                LD(wt[:], w[l, :, sbk * 512:(sbk + 1) * 512].rearrange("(k p) n -> p k n", p=128), [wb], q="pool")
                wc_pend.append(lambda: S.dma("pool", lambda e: e.dma_start(out=w16[idx], in_=wt[:].rearrange("p k n -> p (k n)")),
                                             reads=[wb], writes=[]))

            def conv_unit():
                if not conv_todo:
                    return
                cb = conv_todo.pop(0)
                ht, hb_ = chr_.next()
                LD(ht[:], hcT[cb * 128:(cb + 1) * 128, :], [hb_])
                ot, ob = cor_.next()
                ACT(ot[:], ht[:], AF.Identity, [hb_, b_small], [ob], scale=sm["cw"][:, cb, 1:2], bias=sm["cb"][:, cb:cb + 1])
                for (t0, L) in segs:
                    STT(ot[:, t0 + 1:t0 + L], ht[:, t0:t0 + L - 1], sm["cw"][:, cb, 0:1], ot[:, t0 + 1:t0 + L], ALU.mult, ALU.add, [hb_, b_small, ob], [ob])
                    STT(ot[:, t0:t0 + L - 1], ht[:, t0 + 1:t0 + L], sm["cw"][:, cb, 2:3], ot[:, t0:t0 + L - 1], ALU.mult, ALU.add, [hb_, b_small, ob], [ob])
                STo(hccT[cb * 128:(cb + 1) * 128, :], ot[:], [ob])

            def epilogue(psO, pbO, psD, pbD, n, grow, tsl):
                rc, rcb = rcr.next()
                S.op("dve", lambda e: e.reciprocal(out=rc[:, :n], in_=psD[:, :n]), reads=[pbD], writes=[rcb])
                ga, gab = gar.next()
                LD(ga[:, :n], gT[grow:grow + 128, tsl], [gab])
                t2, t2b = t2r.next()
                TT(t2[:, :n], psO[:, :n], rc[:, :n], ALU.mult, [pbO, rcb], [t2b])
                yo, yob = yor.next()
                TT(yo[:, :n], t2[:, :n], ga[:, :n], ALU.mult, [t2b, gab], [yob])
                STo(yT[grow:grow + 128, tsl], yo[:, :n], [yob])

            pend = []
            for s in range(4):
                for h in range(8):
                    t0 = s * 256
                    qt, qb = qr.next(); kt, kb = kr.next(); vt, vb = v0r.next()
                    LD(qt[:, 0:256], qT[h * 128:(h + 1) * 128, t0:t0 + 256], [qb])
                    LD(kt[:, 0:256], kT[h * 128:(h + 1) * 128, t0:t0 + 256], [kb])
                    LD(vt[:, 0:2, :], vtok[t0:t0 + 256, h * 128:(h + 1) * 128].rearrange("(k p) d -> p k d", p=128), [vb])
                    pc, pcb = pcr.next()
                    ps, pb = PS()
                    for kc in range(2):
                        MM(ps[:, kc * 256:(kc + 1) * 256], kt[:, kc * 128:(kc + 1) * 128], qt[:, 0:256], True, True, [kb, qb], pb)
                    ACT(pc[:, 0, :], ps[:, :], AF.Exp, [pb], [pcb])
                    while pend:
                        pend.pop(0)()

                    def ph2(pc=pc, pcb=pcb, vt=vt, vb=vb, h=h, t0=t0):
                        psD, pbD = PS(); psO, pbO = PS()
                        for kc in range(2):
                            MM(psD[:, 0:256], onesb[:], pc[:, 0, kc * 256:(kc + 1) * 256], kc == 0, kc == 1, [b_const, pcb], pbD)
                        for kc in range(2):
                            MM(psO[:, 0:256], vt[:, kc, :], pc[:, 0, kc * 256:(kc + 1) * 256], kc == 0, kc == 1, [vb, pcb], pbO)
                        epilogue(psO, pbO, psD, pbD, 256, h * 128, slice(t0, t0 + 256))
                    pend.append(ph2)
            while pend:
                pend.pop(0)()
            for h in range(8):
                qt, qb = qr.next(); kt, kb = kr.next(); v0, v0b = v0r.next(); v1, v1b = v1r.next()
                LD(qt[:], qT[h * 128:(h + 1) * 128, 1024:3072], [qb])
                LD(kt[:], kT[h * 128:(h + 1) * 128, 1024:3072], [kb])
                LD(v0[:], vtok[1024:3072, h * 128:(h + 1) * 128].rearrange("(k p) d -> p k d", p=128), [v0b])
                LD(v1[:], vtok[1088:3136, h * 128:(h + 1) * 128].rearrange("(k p) d -> p k d", p=128), [v1b])
                ckt, ckb = ckr.next()
                LD(ckt[:], ck[l, :, h * 128:(h + 1) * 128].rearrange("(k p) d -> p k d", p=128), [ckb])
                cvt, cvb = cvr.next()
                LD(cvt[:], cv[l, :, h * 128:(h + 1) * 128].rearrange("(k p) d -> p k d", p=128), [cvb], q="pool")
                ebt, ebb = ebr.next()
                LD(ebt[:], ebs[h].rearrange("p (a c) -> p a c", a=14), [ebb])
                ps, pb = PS()
                for kc in range(4):
                    TR(ps[:, kc * 128:(kc + 1) * 128], ckt[:, kc, :], ident[:], [ckb], pb)
                cT, cTb = cktr.next()
                CP(cT[:], ps[:, :], [pb], [cTb])
                for g in range(4):
                    qs = slice(g * 512, (g + 1) * 512)
                    pc, pcb = pcr.next()
                    for kc in range(4):
                        ps, pb = PS()
                        MM(ps[:, :], cT[:, kc * 128:(kc + 1) * 128], qt[:, qs], True, True, [cTb, qb], pb)
                        ACT(pc[:, kc, :], ps[:, :], AF.Exp, [pb], [pcb])
                    (hp, held) = PS_hold(2)
                    (psD, pbD), (psO, pbO) = hp
                    for kc in range(4):
                        MM(psD[:, :], onesb[:], pc[:, kc, :], kc == 0, False, [b_const, pcb], pbD, skip=True)
                    for kc in range(4):
                        MM(psO[:, :], cvt[:, kc, :], pc[:, kc, :], kc == 0, False, [cvb, pcb], pbO, skip=True)
                    rowinfo = []
                    for pr in range(4):
                        psS, pbS = PS()
                        pw, pwb = pwr.next()
                        for half in range(2):
                            r = g * 8 + pr * 2 + half
                            rs_ = min(max(r - 4, 0), 24)
                            q64 = slice(r * 64, (r + 1) * 64)
                            for j in range(4):
                                k0 = (rs_ + 2 * j) * 64
                                c0 = half * 256 + j * 64
                                MM(psS[:, c0:c0 + 64], kt[:, k0:k0 + 128], qt[:, q64], True, True, [kb, qb], pbS)
                        ACT(pw[:].rearrange("p a c -> p (a c)"), psS[:, :], AF.Exp, [pbS], [pwb])
                        for half in range(2):
                            r = g * 8 + pr * 2 + half
                            rs_ = min(max(r - 4, 0), 24)
                            dr0 = rs_ - r + 7
                            TT(pw[:, half * 4:(half + 1) * 4, :], pw[:, half * 4:(half + 1) * 4, :], ebt[:, dr0:dr0 + 7:2, :],
                               ALU.mult, [pwb, ebb], [pwb])
                            rowinfo.append((pr * 2 + half, rs_, pw, pwb, half))
                    for (rr, rs_, pw, pwb, half) in rowinfo:
                        o64 = slice(rr * 64, (rr + 1) * 64)
                        last = rr == 7
                        for j in range(4):
                            MM(psD[:, o64], onesb[:], pw[:, half * 4 + j, :], False, last and j == 3, [b_const, pwb], pbD, skip=True)
                        for j in range(4):
                            row0 = rs_ + 2 * j
                            if row0 % 2 == 0:
                                vap = v0[:, row0 // 2, :]; vbb = v0b
                            else:
                                vap = v1[:, (row0 - 1) // 2, :]; vbb = v1b
                            MM(psO[:, o64], vap, pw[:, half * 4 + j, :], False, last and j == 3, [vbb, pwb], pbO, skip=True)
                    epilogue(psO, pbO, psD, pbD, 512, h * 128, slice(1024 + g * 512, 1024 + (g + 1) * 512))
                    PS_release(held)
                    if g % 2 == 1:
                        conv_unit()
                    else:
                        wcast_unit()
            while conv_todo:
                conv_unit()
            while wc_todo or wc_pend:
                wcast_unit()
            S.fence()
        STQ[0] = "sp"

    def fnet(l):
        STQ[0] = "pool"
        with ExitStack() as st:
            cs_ = sb(st, "ncs", [128, 256], BF16); bcs = Buf()
            LD(cs_[:], cd["fnCS"][:, :], [bcs])
            for (L, seqs) in ((256, [(s * 256) for s in range(4)]), (2048, [1024])):
                nch = L // 128
                with ExitStack() as st1:
                    ur = Ring(st1, "nu", [128, L], BF16, 2)
                    P12 = sb(st1, "nP", [128, 4, nch, 256], BF16); bP = Buf()
                    nsl = min(512, L)
                    clr = Ring(st1, "ncl", [128, nch, nsl], BF16, 2)
                    slr = Ring(st1, "nsl", [128, nch, nsl], BF16, 2)
                    gbr = Ring(st1, "ngb", [128, 512], BF16, 2)
                    yor = Ring(st1, "nyo", [128, 512], BF16, 2)
                    for t0 in seqs:
                        for g in range(4):
                            ut, ub_ = ur.next()
                            LD(ut[:], ubT[g * 128:(g + 1) * 128, t0:t0 + L], [ub_])
                            for lc in range(nch):
                                ps, pb = PS()
                                MM(ps[:, 0:256], ut[:, lc * 128:(lc + 1) * 128], cs_[:], True, True, [ub_, bcs], pb)
                                if lc % 2 == 0:
                                    CP(P12[:, g, lc, :], ps[:, 0:256], [pb], [bP])
                                else:
                                    ACT(P12[:, g, lc, :], ps[:, 0:256], AF.Identity, [pb], [bP])
                        for c0 in range(0, L, nsl):
                            ct, cb_ = clr.next(); st_, sb__ = slr.next()
                            LD(ct[:], cd["fnC%d" % L][:, c0:c0 + nsl].rearrange("(k p) n -> p k n", p=128), [cb_])
                            LD(st_[:], cd["fnSn%d" % L][:, c0:c0 + nsl].rearrange("(k p) n -> p k n", p=128), [sb__])
                            for g in range(4):
                                ps, pb = PS()
                                for lc in range(nch):
                                    MM(ps[:, :nsl], P12[:, g, lc, 0:128], ct[:, lc, :], lc == 0, False, [bP, cb_], pb)
                                for lc in range(nch):
                                    MM(ps[:, :nsl], P12[:, g, lc, 128:256], st_[:, lc, :], False, lc == nch - 1, [bP, sb__], pb)
                                gb, gbb = gbr.next()
                                tsl = slice(t0 + c0, t0 + c0 + nsl)
                                LD(gb[:, :nsl], gT[1024 + g * 128:1024 + (g + 1) * 128, tsl], [gbb])
                                yo, yob = yor.next()
                                TT(yo[:, :nsl], ps[:, :nsl], gb[:, :nsl], ALU.mult, [pb, gbb], [yob])
                                STo(yT[1024 + g * 128:1024 + (g + 1) * 128, tsl], yo[:, :nsl], [yob])
                    S.fence()
            S.fence()

    def hyena(l):
        sm = small[l]
        STQ[0] = "pool"
        for (L, seqs) in ((256, [s * 256 for s in range(4)]), (2048, [1024])):
            nch = L // 128
            nsl = min(512, L)
            kf = kf_s[L]
            with ExitStack() as st:
                nbuf = 2 if L == 256 else 1
                ztr = Ring(st, "yz", [128, nch, 512], BF16, nbuf)
                yfr = Ring(st, "yY", [128, nch, 2, 512], BF16, nbuf)
                zin = Ring(st, "yzin", [128, L], F32, nbuf)
                zbf = Ring(st, "yzbf", [128, L], BF16, nbuf)
                cr = Ring(st, "yC", [128, nch, nsl], BF16, 2)
                sr = Ring(st, "yS", [128, nch, nsl], BF16, 2)
                kr_ = Ring(st, "yk", [128, 2, 512], F32, 2)
                t1 = Ring(st, "yt1", [128, 512], F32, 3)
                xr_ = Ring(st, "yx", [128, 512], F32, 3)
                o32 = Ring(st, "yo32", [128, 512], F32, 2)
                o16 = Ring(st, "yo16", [128, 512], BF16, 2)

                cs_cache = {}
                kf_cache = {}
                kfr_small = Ring(st, "ykc", [128, 2, 512], F32, 4) if L == 256 else None

                def get_cs(f0):
                    if L == 256 and f0 in cs_cache:
                        return cs_cache[f0]
                    ct, cb_ = cr.next(); st_, sb__ = sr.next()
                    LD(ct[:], cd["hyC%d" % L][:, f0:f0 + nsl].rearrange("(k p) n -> p k n", p=128), [cb_])
                    LD(st_[:], cd["hyS%d" % L][:, f0:f0 + nsl].rearrange("(k p) n -> p k n", p=128), [sb__])
                    cs_cache[f0] = (ct, cb_, st_, sb__)
                    return cs_cache[f0]

                def get_kf(o, fc):
                    if L == 256 and (o, fc) in kf_cache:
                        return kf_cache[(o, fc)]
                    kt, kb_ = (kfr_small if L == 256 else kr_).next()
                    LD(kt[:, 0, :], kf[0, o, fc * 128:(fc + 1) * 128, :], [kb_])
                    LD(kt[:, 1, :], kf[1, o, fc * 128:(fc + 1) * 128, :], [kb_])
                    kf_cache[(o, fc)] = (kt, kb_)
                    return kf_cache[(o, fc)]

                def load_ztok(src, t0, ztok, bz):
                    for cb in range(4):
                        zt, zb_ = zin.next()
                        LD(zt[:], src[cb * 128:(cb + 1) * 128, t0:t0 + L], [zb_])
                        zh, zhb = zbf.next()
                        CP(zh[:], zt[:], [zb_], [zhb])
                        for tc0 in range(0, nch, 4):
                            nb = min(4, nch - tc0)
                            ps, pb = PS()
                            psv = ps[:, :].bitcast(BF16)
                            for j in range(nb):
                                TR(psv[:, j * 128:(j + 1) * 128], zh[:, (tc0 + j) * 128:(tc0 + j + 1) * 128], identb[:], [zhb], pb)
                            CP(ztok[:, tc0:tc0 + nb, cb * 128:(cb + 1) * 128],
                               psv[:, 0:nb * 128].rearrange("p (j c) -> p j c", j=nb), [pb], [bz])

                for o in range(2):
                    for t0 in seqs:
                        ztok, bz = ztr.next()
                        Yf, bY = yfr.next()
                        load_ztok(hccT if o == 0 else z1T, t0, ztok, bz)
                        for f0 in range(0, L, nsl):
                            ct, cb_, st_, sb__ = get_cs(f0)
                            for fs in range(nsl // 128):
                                fc = f0 // 128 + fs
                                kt, kb_ = get_kf(o, fc)
                                psA, pbA = PS(); psB, pbB = PS()
                                for tc in range(nch):
                                    MM(psA[:, :], ct[:, tc, fs * 128:(fs + 1) * 128], ztok[:, tc, :], tc == 0, tc == nch - 1, [cb_, bz], pbA)
                                for tc in range(nch):
                                    MM(psB[:, :], st_[:, tc, fs * 128:(fs + 1) * 128], ztok[:, tc, :], tc == 0, tc == nch - 1, [sb__, bz], pbB)
                                a, ab = t1.next(); b, bb = t1.next()
                                TT(a[:], psA[:, :], kt[:, 0, :], ALU.mult, [pbA, kb_], [ab])
                                TT(b[:], psB[:, :], kt[:, 1, :], ALU.mult, [pbB, kb_], [bb])
                                TT(Yf[:, fc, 0, :], a[:], b[:], ALU.add, [ab, bb], [bY])
                                a2, ab2 = t1.next(); b2, bb2 = t1.next()
                                TT(a2[:], psB[:, :], kt[:, 0, :], ALU.mult, [pbB, kb_], [ab2])
                                TT(b2[:], psA[:, :], kt[:, 1, :], ALU.mult, [pbA, kb_], [bb2])
                                TT(Yf[:, fc, 1, :], a2[:], b2[:], ALU.subtract, [ab2, bb2], [bY])
                        for c0 in range(0, L, nsl):
                            ct, cb_, st_, sb__ = get_cs(c0)
                            tsl = slice(t0 + c0, t0 + c0 + nsl)
                            for cb in range(4):
                                ps, pb = PS()
                                for fc in range(nch):
                                    MM(ps[:, :nsl], Yf[:, fc, 0, cb * 128:(cb + 1) * 128], ct[:, fc, :], fc == 0, False, [bY, cb_], pb)
                                for fc in range(nch):
                                    MM(ps[:, :nsl], Yf[:, fc, 1, cb * 128:(cb + 1) * 128], st_[:, fc, :], False, fc == nch - 1, [bY, sb__], pb)
                                zi, zib = xr_.next(); xg, xgb = xr_.next()
                                if o == 0:
                                    LD(zi[:, :nsl], hccT[cb * 128:(cb + 1) * 128, tsl], [zib])
                                    LD(xg[:, :nsl], hccT[512 + cb * 128:512 + (cb + 1) * 128, tsl], [xgb])
                                else:
                                    LD(zi[:, :nsl], z1T[cb * 128:(cb + 1) * 128, tsl], [zib])
                                    LD(xg[:, :nsl], hccT[1024 + cb * 128:1024 + (cb + 1) * 128, tsl], [xgb])
                                r, rb = o32.next()
                                STT(r[:, :nsl], zi[:, :nsl], sm["hb"][:, o, cb:cb + 1], ps[:, :nsl], ALU.mult, ALU.add, [zib, b_small, pb], [rb])
                                TT(r[:, :nsl], r[:, :nsl], xg[:, :nsl], ALU.mult, [rb, xgb], [rb])
                                if o == 0:
                                    STo(z1T[cb * 128:(cb + 1) * 128, tsl], r[:, :nsl], [rb])
                                else:
                                    gc, gcb = o16.next()
                                    LD(gc[:, :nsl], gT[1536 + cb * 128:1536 + (cb + 1) * 128, tsl], [gcb])
                                    yo, yob = o16.next()
                                    TT(yo[:, :nsl], r[:, :nsl], gc[:, :nsl], ALU.mult, [rb, gcb], [yob])
                                    STo(yT[1536 + cb * 128:1536 + (cb + 1) * 128, tsl], yo[:, :nsl], [yob])
                    S.fence()
                S.fence()

    def pass2(l):
        STQ[0] = "sp"
        with ExitStack() as st:
            ySr = Ring(st, "pY", [128, 16, 1024], BF16, 2)
            mS = sb(st, "pM", [128, 16, 1024], BF16); bMs = Buf()
            ynext = ySr.next()
            LD(ynext[0][:], yT[:, 0:1024].rearrange("(k p) t -> p k t", p=128), [ynext[1]])
            wr = Ring(st, "pw", [128, 16, 512], BF16, 2)
            wstream = WStream(wr, [w16[i] for blk in range(3) for i in range(8)], pre=True)
            gr = Ring(st, "pg", [128, 3, 512], BF16, 3)
            t1 = Ring(st, "pt", [128, 512], F32, 9)
            xr_ = Ring(st, "px", [128, 512], F32, 2)
            xo = Ring(st, "pxo", [128, 512], F32, 2)
            p2pend = []
            for blk in range(3):
                cond = 0 if blk == 0 else 1
                t0 = blk * 1024
                yS, bYs = ynext
                if blk < 2:
                    ynext = ySr.next()
                    LD(ynext[0][:], yT[:, t0 + 1024:t0 + 2048].rearrange("(k p) t -> p k t", p=128), [ynext[1]])
                for sbk in range(4):
                    wt, wb = wstream.get()
                    for cb4 in range(4):
                        cb = sbk * 4 + cb4
                        for tc in range(2):
                            tsl = slice(t0 + tc * 512, t0 + (tc + 1) * 512)
                            gt, gb_ = gr.next()
                            for i in range(3):
                                LD(gt[:, i, :], mgT[i * 2048 + cb * 128:i * 2048 + (cb + 1) * 128, tsl], [gb_])
                            ts_ = []
                            for i, (k0, k1) in enumerate(((0, 8), (8, 12), (12, 16))):
                                ps, pb = PS()
                                for kc in range(k0, k1):
                                    MM(ps[:, :], wt[:, kc, cb4 * 128:(cb4 + 1) * 128], yS[:, kc, tc * 512:(tc + 1) * 512],
                                       kc == k0, kc == k1 - 1, [wb, bYs], pb)
                                t, tb = t1.next()
                                TT(t[:], ps[:, :], gt[:, i, :], ALU.mult, [pb, gb_], [tb])
                                ts_.append((t, tb))
                            while p2pend:
                                p2pend.pop(0)()

                            def adds(ts_=ts_, cb=cb, tc=tc):
                                (ta, tab), (tb_, tbb), (tc_, tcb) = ts_
                                TT(ta[:], ta[:], tb_[:], ALU.add, [tab, tbb], [tab], eng="pool")
                                TT(mS[:, cb, tc * 512:(tc + 1) * 512], ta[:], tc_[:], ALU.add, [tab, tcb], [bMs], eng="pool")
                            p2pend.append(adds)
                while p2pend:
                    p2pend.pop(0)()
                for sbk in range(4):
                    wt, wb = wstream.get()
                    for cb4 in range(4):
                        cb = sbk * 4 + cb4
                        for tc in range(2):
                            tsl = slice(t0 + tc * 512, t0 + (tc + 1) * 512)
                            ps, pb = PS()
                            for kc in range(16):
                                MM(ps[:, :], wt[:, kc, cb4 * 128:(cb4 + 1) * 128], mS[:, kc, tc * 512:(tc + 1) * 512],
                                   kc == 0, kc == 15, [wb, bMs], pb)
                            xt_, xb_ = xr_.next()
                            LD(xt_[:], x_src(l)[cb * 128:(cb + 1) * 128, tsl], [xb_])
                            o, ob = xo.next()
                            STT(o[:], ps[:, :], adaT[l][:, 32 + cb, cond:cond + 1], xt_[:], ALU.mult, ALU.add, [pb, b_ada, xb_], [ob])
                            STo(x_dst(l)[cb * 128:(cb + 1) * 128, tsl], o[:], [ob], is_output=(l == NL_RUN - 1))
            S.fence()

    def on(name):
        marks.append((name, dict(S.ninstr)))
        return STAGES is None or name in STAGES

    for l in range(NL_RUN):
        if on("filt256"):
            hyena_filter_full(l, 256)
        if on("filt2048"):
            hyena_filter_full(l, 2048)
        if on("pass1"):
            pass1(l)
        if on("attn"):
            attention(l)
        if on("fnet"):
            fnet(l)
        if on("hyena"):
            hyena(l)
        if on("pass2"):
            pass2(l)

    S.emit()
    build_program.stats = {e: len(v) for e, v in S.prog.items()}
    return nc


_NC = None


def _host_inputs(inp, core):
    C = make_consts()
    b = core // 4
    xp = inp["x_prompt"][4 * core:4 * core + 4].reshape(1024, DM)
    xs = inp["x_sample"][b]
    m = {}
    m["xin"] = _f32(np.concatenate([xp, xs], 0).T)
    m["ck"] = _f32(inp["cache_k"][b].reshape(NL, 512, 1024))
    m["cv"] = _f32(inp["cache_v"][b].reshape(NL, 512, 1024))
    cvec = np.stack([inp["c_ctx"], inp["c"][b]], -1)
    m["cvecT"] = _f32(cvec.reshape(16, 128, 2).transpose(1, 0, 2))
    return m


def _shared_inputs(inp):
    C = make_consts()
    m = {}
    m["norm_gT"] = _f32(inp["norm_g"].reshape(NL, 16, 128).transpose(0, 2, 1))
    m["b_adaT"] = _f32(inp["b_ada"].reshape(NL, 48, 128).transpose(0, 2, 1))
    m["w_ada"] = _f32(inp["w_ada"]); m["w_in"] = _f32(inp["w_in"])
    m["w_br"] = _f32(inp["w_br"]); m["w_out"] = _f32(inp["w_out"])
    m["qg"] = _f32(inp["q_norm_g"].reshape(NL, 128, 1))
    m["kg"] = _f32(inp["k_norm_g"].reshape(NL, 128, 1))
    m["kg_rep"] = _f32(np.broadcast_to(inp["k_norm_g"][:, None, :], (NL, 128, 128)))
    m["rpbT2"] = rpb_gather(np.asarray(inp["rpb"]))
    m["conv_wT"] = _f32(inp["conv_w"].reshape(NL, 3, 12, 128).transpose(0, 3, 2, 1))
    m["conv_bT"] = _f32(inp["conv_b"].reshape(NL, 12, 128).transpose(0, 2, 1))
    m["hy_biasT"] = _f32(inp["hy_bias"].reshape(NL, 2, 4, 128).transpose(0, 3, 1, 2))
    m["f_w1"] = _f32(inp["f_w1"]); m["f_w2"] = _f32(inp["f_w2"]); m["f_w3"] = _f32(inp["f_w3"])
    m["f_b1T"] = _f32(inp["f_b1"].reshape(NL, 64, 1))
    m["f_b2T"] = _f32(inp["f_b2"].reshape(NL, 64, 1))
    m["f_freqT"] = _f32(inp["f_freq"].reshape(NL, 64, 1))
    for k, v in C.items():
        m["c_" + k] = v
    return m


def kernel(**inputs):
    global _NC
    inp = {k: np.asarray(v) for k, v in inputs.items()}
    if _NC is None:
        _NC = build_program()
    nc = _NC
    shared = _shared_inputs(inp)
    in_maps = []
    for core in range(8):
        m = dict(shared)
        m.update(_host_inputs(inp, core))
        in_maps.append(m)
    res = run_bass_kernel_spmd(nc, in_maps, core_ids=list(range(8)))
    R = res.results
    y_prompt = np.concatenate([np.ascontiguousarray(R[c]["y_out"][:, :1024].T).reshape(4, 256, DM) for c in range(8)], 0)
    y_sample = np.stack([np.ascontiguousarray(R[0]["y_out"][:, 1024:].T), np.ascontiguousarray(R[4]["y_out"][:, 1024:].T)], 0)
    nk = np.concatenate([R[c]["newk"].reshape(NL, 4, 256, 8, 128).transpose(1, 0, 2, 3, 4) for c in range(8)], 0)
    nv = np.concatenate([R[c]["newv"].reshape(NL, 4, 256, 8, 128).transpose(1, 0, 2, 3, 4) for c in range(8)], 0)
    if DEBUG:
        kernel.debug = R
    return (y_prompt.astype(np.float32), y_sample.astype(np.float32), nk.astype(np.float32), nv.astype(np.float32))
```

```python
import numpy as np
import concourse.bass as bass
import concourse.mybir as mybir

F32 = mybir.dt.float32
BF16 = mybir.dt.bfloat16
AF = mybir.ActivationFunctionType
ALU = mybir.AluOpType

COMPUTE = ("pe", "act", "dve", "pool")
NDMA_SEMS = 26
NSW = 6


class Buf:
    __slots__ = ("name", "w", "r")

    def __init__(self, name=""):
        self.name = name
        self.w = None
        self.r = []


class Sched:
    def __init__(self, nc):
        self.nc = nc
        self.prog = {e: [] for e in ("pe", "act", "dve", "pool", "sp")}
        self.flag = {e: set() for e in COMPUTE}
        self.ninstr = {e: 0 for e in COMPUTE}
        self.dma_n = [0] * NDMA_SEMS
        self.dma_rr = 0
        self.sw_rr = 0
        self.all_out_tokens = []
        self.pending_fence = {}

    def _collect(self, reads, writes):
        deps = []
        for b in reads:
            if b.w is not None:
                deps.append(b.w)
        for b in writes:
            if b.w is not None:
                deps.append(b.w)
            deps.extend(b.r)
        return deps

    def _commit(self, tok, reads, writes):
        for b in reads:
            b.r.append(tok)
        for b in writes:
            b.w = tok
            b.r = []

    def fence(self):
        toks = []
        for e in COMPUTE:
            if self.ninstr[e] > 0:
                toks.append(("c", e, self.ninstr[e] - 1))
        for k in range(NDMA_SEMS):
            if self.dma_n[k] > 0:
                toks.append(("d", k, self.dma_n[k]))
        self.pending_fence = {q: list(toks) for q in self.prog}

    def op(self, eng, fn, reads=(), writes=()):
        deps = self._collect(reads, writes)
        deps += self.pending_fence.pop(eng, [])
        idx = self.ninstr[eng]
        self.ninstr[eng] += 1
        tok = ("c", eng, idx)
        self.prog[eng].append([deps, fn, tok])
        self._commit(tok, reads, writes)
        return tok

    def dma(self, q, fn, reads=(), writes=(), is_output=False):
        deps = self._collect(reads, writes)
        deps += self.pending_fence.pop(q, [])
        if q == "pool":
            k = self.sw_rr
            self.sw_rr = (self.sw_rr + 1) % NSW
        else:
            k = NSW + self.dma_rr
            self.dma_rr = (self.dma_rr + 1) % (NDMA_SEMS - NSW)
        if self.dma_n[k] > 0:
            deps.append(("d", k, self.dma_n[k]))
        self.dma_n[k] += 1
        tok = ("d", k, self.dma_n[k])
        self.prog[q].append([deps, fn, tok])
        self._commit(tok, reads, writes)
        if is_output:
            self.all_out_tokens.append(tok)
        return tok

    def emit(self):
        nc = self.nc
        for e, items in self.prog.items():
            for deps, fn, tok in items:
                for d in deps:
                    if d[0] == "c":
                        if d[1] == "pe" and e == "pe":
                            continue
                        self.flag[d[1]].add(d[2])
        final_deps = list(self.all_out_tokens)
        rank = {}
        for e in COMPUTE:
            r = 0
            m = {}
            fl = self.flag[e]
            for i in range(self.ninstr[e]):
                if i in fl:
                    r += 1
                    m[i] = r
            rank[e] = m
        self.rank = rank

        def tokval(d):
            if d[0] == "c":
                return ("c", d[1]), rank[d[1]][d[2]]
            return ("d", d[1]), 16 * d[2]

        import contextlib
        with contextlib.ExitStack() as st:
            sems = {}
            for e in COMPUTE:
                sems[("c", e)] = st.enter_context(nc.semaphore("sem_" + e))
            for k in range(NDMA_SEMS):
                sems[("d", k)] = st.enter_context(nc.semaphore("sem_dma%d" % k))
            block = st.enter_context(nc.Block())

            def run(ekey, engine_obj, extra_final=None):
                seen = {}
                for deps, fn, tok in self.prog[ekey]:
                    need = {}
                    for d in deps:
                        if d[0] == "c" and d[1] == "pe" and ekey == "pe":
                            continue
                        key, val = tokval(d)
                        if seen.get(key, 0) < val:
                            if need.get(key, 0) < val:
                                need[key] = val
                    for key, val in need.items():
                        engine_obj.wait_ge(sems[key], val)
                        seen[key] = val
                    ins = fn(engine_obj)
                    if tok[0] == "c":
                        if tok[2] in self.flag[tok[1]]:
                            ins.then_inc(sems[("c", tok[1])], 1)
                    else:
                        ins.then_inc(sems[("d", tok[1])], 16)
                if extra_final:
                    need = {}
                    for d in extra_final:
                        key, val = tokval(d)
                        if need.get(key, 0) < val:
                            need[key] = val
                    for key, val in need.items():
                        engine_obj.wait_ge(sems[key], val)

            @block.sync
            def _(e):
                run("sp", e, final_deps)

            @block.tensor
            def _(e):
                run("pe", e)

            @block.scalar
            def _(e):
                run("act", e)

            @block.vector
            def _(e):
                run("dve", e)

            @block.gpsimd
            def _(e):
                run("pool", e, final_deps)

import math
from contextlib import ExitStack
import ml_dtypes
from concourse.bass_utils import run_bass_kernel_spmd

AX = mybir.AxisListType
NT = 3072
DM = 2048
NIN = 13312
NL = 2
EPS = 1e-6
MIN_DECAY = math.log(1e-2) / 1.5
MAX_DECAY = math.log(1e-2) / 0.3
DEBUG = False
NL_RUN = 2
STAGES = None

_bf = lambda a: np.ascontiguousarray(a.astype(ml_dtypes.bfloat16))
_f32 = lambda a: np.ascontiguousarray(a, dtype=np.float32)

_CONST = None


def make_consts():
    global _CONST
    if _CONST is not None:
        return _CONST
    c = {}
    c["ident"] = np.eye(128, dtype=np.float32)
    n = np.arange(128)
    ang = 2 * np.pi * (np.outer(n, n) % 128) / 128
    c["fnCS"] = _bf(np.concatenate([np.cos(ang), np.sin(ang)], 1) / np.sqrt(128))
    for L in (256, 2048):
        l = np.arange(L)
        ang = 2 * np.pi * (np.outer(l, l) % L) / L
        c["fnC%d" % L] = _bf(np.cos(ang) / np.sqrt(L))
        c["fnSn%d" % L] = _bf(-np.sin(ang) / np.sqrt(L))
        N = 2 * L
        m = np.outer(2 * l + 1, 2 * l + 1) % (4 * N)
        ang = 2 * np.pi * m / (4 * N)
        c["hyC%d" % L] = _bf(np.cos(ang))
        c["hyS%d" % L] = _bf(np.sin(ang))
        ph = np.pi * (l + 0.5) / N
        tab = np.stack([np.cos(ph) * 2 / N, np.sin(ph) * 2 / N], -1)
        c["phi%d" % L] = _f32(tab.reshape(L // 128, 128, 2).transpose(1, 0, 2))
        t = np.linspace(0.0, 1.0, L, dtype=np.float32)[:, None]
        w = (np.float32(2.0 * math.pi / L) * np.arange(L, dtype=np.float32))[:, None]
        fr = np.linspace(1e-4, 15, 16, dtype=np.float32)[None, :]
        z = np.concatenate([t, np.cos(w * fr), -np.sin(w * fr)], -1).astype(np.float32)
        c["zfT%d" % L] = _f32(z.T)
        deltas = np.abs(np.linspace(MIN_DECAY, MAX_DECAY, 512, dtype=np.float32))
        c["decay%d" % L] = _f32(np.concatenate([np.exp(-t * deltas[None, :]), np.zeros((1, 512), np.float32)], 0))
    p = np.arange(128)
    cp = p % 64
    cq = np.arange(64)
    cs = np.clip(cq - 8, 0, 48)
    valid = (cp[:, None] >= cs[None, :]) & (cp[:, None] < cs[None, :] + 16)
    c["nmask"] = _f32(np.broadcast_to(valid[:, None, :], (128, 14, 64)))
    _CONST = c
    return c


def rpb_gather(rpb):
    p = np.arange(128)
    cp = p % 64
    half = p // 64
    cq = np.arange(64)
    dc = cp[:, None] - cq[None, :] + 15
    ok = (dc >= 0) & (dc <= 30)
    dcc = np.clip(dc, 0, 30)
    dr = np.arange(14)
    drr = dr[None, :] + half[:, None]
    out = rpb[:, :, drr[:, :, None], dcc[:, None, :]]
    out = np.where(ok[None, None, :, None, :], out, 0.0)
    return _f32(out.reshape(NL, 8, 128, 14 * 64))


def build_program():
    nc = bass.Bass("TRN2", target_bir_lowering=False)
    S = Sched(nc)
    C = make_consts()

    def din(name, shape, dt=F32):
        return nc.dram_tensor(name, list(shape), dt, kind="ExternalInput").ap()

    def dscr(name, shape, dt=F32):
        kind = "ExternalOutput" if DEBUG else "Internal"
        return nc.dram_tensor(name, list(shape), dt, kind=kind).ap()

    xin = din("xin", [DM, NT])
    ck = din("ck", [NL, 512, 1024])
    cv = din("cv", [NL, 512, 1024])
    cvecT = din("cvecT", [128, 16, 2])
    norm_gT = din("norm_gT", [NL, 128, 16])
    b_adaT = din("b_adaT", [NL, 128, 48])
    w_ada = din("w_ada", [NL, DM, 3 * DM])
    w_in = din("w_in", [NL, DM, NIN])
    w_br = din("w_br", [NL, DM, DM])
    w_out = din("w_out", [NL, DM, DM])
    qg = din("qg", [NL, 128, 1])
    kg = din("kg", [NL, 128, 1])
    kg_rep = din("kg_rep", [NL, 128, 128])
    rpbT2 = din("rpbT2", [NL, 8, 128, 14 * 64])
    conv_wT = din("conv_wT", [NL, 128, 12, 3])
    conv_bT = din("conv_bT", [NL, 128, 12])
    hy_biasT = din("hy_biasT", [NL, 128, 2, 4])
    f_w1 = din("f_w1", [NL, 33, 64])
    f_b1T = din("f_b1T", [NL, 64, 1])
    f_freqT = din("f_freqT", [NL, 64, 1])
    f_w2 = din("f_w2", [NL, 64, 64])
    f_b2T = din("f_b2T", [NL, 64, 1])
    f_w3 = din("f_w3", [NL, 64, 2048])
    cd = {}
    for k, v in C.items():
        cd[k] = din("c_" + k, v.shape, BF16 if v.dtype == ml_dtypes.bfloat16 else F32)

    y_out = nc.dram_tensor("y_out", [DM, NT], F32, kind="ExternalOutput").ap()
    newk = nc.dram_tensor("newk", [NL, 1024, 1024], F32, kind="ExternalOutput").ap()
    newv = nc.dram_tensor("newv", [NL, 1024, 1024], F32, kind="ExternalOutput").ap()

    xT = dscr("s_xT", [DM, NT])
    qT = dscr("s_qT", [1024, NT], BF16)
    kT = dscr("s_kT", [1024, NT], BF16)
    vtok = dscr("s_vtok", [NT + 128, 1024], BF16)
    gT = dscr("s_gT", [2048, NT], BF16)
    ubT = dscr("s_ubT", [512, NT], BF16)
    hcT = dscr("s_hcT", [1536, NT])
    hccT = dscr("s_hccT", [1536, NT])
    z1T = dscr("s_z1T", [512, NT])
    mgT = dscr("s_mgT", [6144, NT], BF16)
    yT = dscr("s_yT", [2048, NT], BF16)
    ebs = dscr("s_eb", [8, 128, 14 * 64])
    w16 = dscr("s_w16", [8, 128, 16 * 512], BF16)
    hd_s = {L: dscr("s_hd%d" % L, [L + 128, 2048]) for L in (256, 2048)}
    kf_s = {L: dscr("s_kf%d" % L, [2, 2, L, 512]) for L in (256, 2048)}

    def sbp(name, shape, dt=F32):
        return nc.alloc_sbuf_tensor(name, list(shape), dt)

    ident = sbp("ident", [128, 128]); b_ident = Buf()
    identb = sbp("identb", [128, 128], BF16)
    onesb = sbp("onesb", [128, 128], BF16)
    onesf = sbp("onesf", [128, 128])
    epsc = sbp("epsc", [128, 1])
    zeroc = sbp("zeroc", [128, 1])
    b_const = Buf()
    adaT = [sbp("adaT%d" % l, [128, 48, 2]) for l in range(NL)]
    Gm = [sbp("Gm%d" % l, [128, 16, 2]) for l in range(NL)]
    b_ada = Buf()
    small = {}
    for l in range(NL):
        small[l] = dict(
            ng=sbp("ng%d" % l, [128, 16]), ba=sbp("ba%d" % l, [128, 48]),
            qg=sbp("qg%d" % l, [128, 1]), kg=sbp("kg%d" % l, [128, 1]),
            kgr=sbp("kgr%d" % l, [128, 128]),
            cw=sbp("cw%d" % l, [128, 12, 3]), cb=sbp("cb%d" % l, [128, 12]),
            hb=sbp("hb%d" % l, [128, 2, 4]),
        )
    b_small = Buf()

    pst = [nc.alloc_psum_tensor("ps%d" % i, [128, 512], F32) for i in range(8)]
    psb = [Buf() for _ in range(8)]
    ps_avail = list(range(8))
    psi = [0]

    def PS():
        psi[0] = (psi[0] + 1) % len(ps_avail)
        i = ps_avail[psi[0]]
        return pst[i], psb[i]

    def PS_hold(n):
        held = [ps_avail.pop() for _ in range(n)]
        psi[0] = 0
        return [(pst[i], psb[i]) for i in held], held

    def PS_release(held):
        ps_avail.extend(held)

    uid = [0]

    def uname(name):
        uid[0] += 1
        return "%s_%d" % (name, uid[0])

    class Ring:
        def __init__(self, st, name, shape, dt, n):
            self.t = [st.enter_context(nc.sbuf_tensor(uname(name), list(shape), dt)) for i in range(n)]
            self.b = [Buf() for _ in range(n)]
            self.i = 0

        def next(self):
            i = self.i
            self.i = (i + 1) % len(self.t)
            return self.t[i], self.b[i]

    class WStream:
        def __init__(self, ring, srcs, pre=False):
            self.ring = ring; self.srcs = srcs; self.i = 0; self.pre = pre
            self.cur = self._issue(0)

        def _issue(self, i):
            if i >= len(self.srcs):
                return None
            wt, wb = self.ring.next()
            if self.pre:
                LD(wt[:].rearrange("p k n -> p (k n)"), self.srcs[i], [wb], q="pool")
            else:
                LD(wt[:], self.srcs[i].rearrange("(k p) n -> p k n", p=128), [wb], q="pool")
            return wt, wb

        def get(self):
            c = self.cur
            self.i += 1
            self.cur = self._issue(self.i)
            return c

    def sb(st, name, shape, dt=F32):
        return st.enter_context(nc.sbuf_tensor(uname(name), list(shape), dt))

    def MM(ps, lhsT, rhs, start, stop, reads, pb, skip=False):
        S.op("pe", lambda e: e.matmul(ps, lhsT=lhsT, rhs=rhs, start=start, stop=stop,
                                      skip_group_check=skip), reads=reads, writes=[pb])

    def TR(ps, in_, idt, reads, pb):
        S.op("pe", lambda e: e.transpose(out=ps, in_=in_, identity=idt), reads=reads + [b_ident], writes=[pb])

    def ACT(out, in_, func, reads, writes, scale=None, bias=None):
        kw = {}
        if scale is not None:
            kw["scale"] = scale
        if bias is not None:
            kw["bias"] = bias
        S.op("act", lambda e: e.activation(out=out, in_=in_, func=func, **kw), reads=reads, writes=writes)

    def TT(out, in0, in1, op, reads, writes, eng="dve"):
        S.op(eng, lambda e: e.tensor_tensor(out=out, in0=in0, in1=in1, op=op), reads=reads, writes=writes)

    def TS(out, in0, s1, s2, op0, op1, reads, writes, eng="dve"):
        if op1 is None:
            S.op(eng, lambda e: e.tensor_scalar(out=out, in0=in0, scalar1=s1, scalar2=None, op0=op0), reads=reads, writes=writes)
        else:
            S.op(eng, lambda e: e.tensor_scalar(out=out, in0=in0, scalar1=s1, scalar2=s2, op0=op0, op1=op1), reads=reads, writes=writes)

    def STT(out, in0, scalar, in1, op0, op1, reads, writes):
        S.op("dve", lambda e: e.scalar_tensor_tensor(out=out, in0=in0, scalar=scalar, in1=in1, op0=op0, op1=op1),
             reads=reads, writes=writes)

    def RED(out, in_, reads, writes):
        S.op("dve", lambda e: e.tensor_reduce(out=out, in_=in_, axis=AX.X, op=ALU.add), reads=reads, writes=writes)

    def CP(out, in_, reads, writes, eng="dve"):
        S.op(eng, lambda e: e.tensor_copy(out=out, in_=in_), reads=reads, writes=writes)

    def LD(out, in_, writes, reads=(), q="sp"):
        S.dma(q, lambda e: e.dma_start(out=out, in_=in_), reads=list(reads), writes=list(writes))

    STQ = ["sp"]

    def STo(out, in_, reads, writes=(), is_output=False):
        S.dma(STQ[0], lambda e: e.dma_start(out=out, in_=in_), reads=list(reads), writes=list(writes), is_output=is_output)

    def rstd_from(ps_sum, out, scale, reads, writes):
        ACT(out, ps_sum, AF.Ln, reads + [b_const], writes, scale=scale, bias=epsc[:out.shape[0], 0:1])
        ACT(out, out, AF.Exp, writes, writes, scale=-0.5)

    LD(ident[:], cd["ident"][:, :], [b_ident])
    S.op("dve", lambda e: e.tensor_copy(out=identb[:], in_=ident[:]), reads=[b_ident], writes=[b_ident])
    S.op("dve", lambda e: e.memset(onesb[:], 1.0), writes=[b_const])
    S.op("dve", lambda e: e.memset(onesf[:], 1.0), writes=[b_const])
    S.op("dve", lambda e: e.memset(epsc[:], EPS), writes=[b_const])
    S.op("dve", lambda e: e.memset(zeroc[:], 0.0), writes=[b_const])
    for l in range(NL):
        sm = small[l]
        LD(sm["ng"][:], norm_gT[l], [b_small]); LD(sm["ba"][:], b_adaT[l], [b_small])
        LD(sm["qg"][:], qg[l], [b_small]); LD(sm["kg"][:], kg[l], [b_small])
        LD(sm["kgr"][:], kg_rep[l], [b_small])
        LD(sm["cw"][:], conv_wT[l], [b_small]); LD(sm["cb"][:], conv_bT[l], [b_small])
        LD(sm["hb"][:], hy_biasT[l], [b_small])
        TS(sm["qg"][:], sm["qg"][:], float(128 ** -0.5), None, ALU.mult, None, [b_small], [b_small])

    with ExitStack() as st:
        war = Ring(st, "wada", [128, 16, 512], BF16, 3)
        sil0 = sb(st, "sil0", [128, 16, 2]); b_sil = Buf()
        sil = sb(st, "sil", [128, 16, 2], BF16)
        LD(sil0[:], cvecT[:, :, :], [b_sil])
        ACT(sil[:], sil0[:], AF.Silu, [b_sil], [b_sil])
        for l in range(NL):
            ps, pb = PS()
            for sbk in range(12):
                wt, wb = war.next()
                LD(wt[:], w_ada[l, :, sbk * 512:(sbk + 1) * 512].rearrange("(k p) n -> p k n", p=128), [wb], q="pool")
                for j4 in range(4):
                    j = sbk * 4 + j4
                    for kc in range(16):
                        MM(ps[:, 2 * j:2 * j + 2], wt[:, kc, j4 * 128:(j4 + 1) * 128], sil[:, kc, :],
                           kc == 0, kc == 15, [wb, b_sil], pb)
            for c in range(2):
                TT(adaT[l][:, :, c], ps[:, c:96:2], small[l]["ba"][:], ALU.add, [pb, b_small], [b_ada])
                STT(Gm[l][:, :, c], adaT[l][:, 16:32, c], 1.0, small[l]["ng"][:], ALU.add, ALU.mult,
                    [b_ada, b_small], [b_ada])
        S.fence()

    with ExitStack() as st:
        zpad = sb(st, "zpad", [128, 1024], BF16); bzp = Buf()
        S.op("dve", lambda e: e.memset(zpad[:], 0.0), writes=[bzp])
        STo(vtok[NT:NT + 128, :], zpad[:], [bzp])
        S.fence()

    def x_src(l):
        return xin if l == 0 else xT

    def x_dst(l):
        return y_out if l == NL_RUN - 1 else xT

    marks = [('start', dict(S.ninstr))]
    build_program.marks = marks

    def hyena_filter_full(l, L):
        STQ[0] = "sp"
        nch = L // 128
        kf = kf_s[L]
        with ExitStack() as st:
            hb = sb(st, "fhb", [128, nch, 2048], BF16); bhbs = [Buf() for _ in range(nch)]
            recs = sb(st, "frecs", [128, 2048]); brec = Buf()
            with ExitStack() as st1:
                w1 = sb(st1, "fw1", [33, 64]); w2 = sb(st1, "fw2", [64, 64]); w3 = sb(st1, "fw3", [64, 2048])
                zf = sb(st1, "fzf", [33, L])
                b1 = sb(st1, "fb1", [64, 1]); b2 = sb(st1, "fb2", [64, 1]); fq = sb(st1, "ffq", [64, 1])
                h1 = sb(st1, "fh1", [64, L]); h2 = sb(st1, "fh2", [64, L + 1])
                tmp = Ring(st1, "ftmp", [64, 512], F32, 2)
                tmp2 = Ring(st1, "ftmp2", [64, 512], F32, 2)
                bw = Buf(); bh1 = Buf(); bh2 = Buf()
                LD(w1[:], f_w1[l], [bw]); LD(w2[:], f_w2[l], [bw]); LD(w3[:], f_w3[l], [bw])
                LD(zf[:], cd["zfT%d" % L][:, :], [bw])
                LD(b1[:], f_b1T[l], [bw]); LD(b2[:], f_b2T[l], [bw]); LD(fq[:], f_freqT[l], [bw])

                def sin_layer(w, bcol, src, bsrc, K, dst, bdst):
                    for c0 in range(0, L, 512):
                        n = min(512, L - c0)
                        ps, pb = PS()
                        MM(ps[:64, :n], w[:K, :], src[:K, c0:c0 + n], True, True, [bw, bsrc], pb)
                        a, ab = tmp.next()
                        TS(a[:, :n], ps[:64, :n], bcol[:, 0:1], fq[:, 0:1], ALU.add, ALU.mult, [pb, bw], [ab])
                        m, mb = tmp2.next()
                        TS(m[:, :n], a[:, :n], float(np.pi), float(-2 * np.pi), ALU.is_gt, ALU.mult, [ab], [mb])
                        TT(a[:, :n], a[:, :n], m[:, :n], ALU.add, [ab, mb], [ab])
                        TS(m[:, :n], a[:, :n], float(-np.pi), float(2 * np.pi), ALU.is_lt, ALU.mult, [ab], [mb])
                        TT(a[:, :n], a[:, :n], m[:, :n], ALU.add, [ab, mb], [ab])
                        TS(a[:, :n], a[:, :n], float(np.pi), float(-np.pi), ALU.min, ALU.max, [ab], [ab])
                        ACT(dst[:, c0:c0 + n], a[:, :n], AF.Sin, [ab], [bdst])

                S.op("dve", lambda e: e.memset(h2[:, L:L + 1], 0.0), writes=[bh2])
                sin_layer(w1, b1, zf, bw, 33, h1, bh1)
                sin_layer(w2, b2, h1, bh1, 64, h2, bh2)
                dsr = Ring(st1, "fdsh", [128, 512], F32, 2)
                dec = Ring(st1, "fdec", [128, 512], F32, 2)
                hdr = Ring(st1, "fhd", [128, 2048], F32, 2)
                habs = Ring(st1, "fhabs", [128, 2048], BF16, 3)
                sps, sheld = PS_hold(4)
                fpend = []
                for dc in range(nch):
                    dt_, db_ = dec.next()
                    LD(dt_[:], cd["decay%d" % L][dc * 128:(dc + 1) * 128, :], [db_])
                    ht, hb_ = hdr.next()
                    for cs in range(4):
                        ps, pb = PS()
                        MM(ps[:, :], h2[:, dc * 128:(dc + 1) * 128], w3[:, cs * 512:(cs + 1) * 512], True, True, [bh2, bw], pb)
                        TT(ht[:, cs * 512:(cs + 1) * 512], ps[:, :], dt_[:], ALU.mult, [pb, db_], [hb_])
                    for o in range(2):
                        CP(hb[:, dc, o * 1024:o * 1024 + 512], ht[:, o * 1024:o * 1024 + 512], [hb_], [bhbs[dc]], eng="pool")
                    ds_, dsb = dsr.next()
                    LD(ds_[:], cd["decay%d" % L][dc * 128 + 1:(dc + 1) * 128 + 1, :], [dsb])
                    for o in range(2):
                        ps, pb = PS()
                        MM(ps[:, :], h2[:, dc * 128 + 1:(dc + 1) * 128 + 1], w3[:, o * 1024 + 512:o * 1024 + 1024], True, True, [bh2, bw], pb)
                        TT(hb[:, dc, o * 1024 + 512:o * 1024 + 1024], ps[:, :], ds_[:], ALU.mult, [pb, dsb], [bhbs[dc]])
                    at, ab_ = habs.next()
                    ACT(at[:], ht[:], AF.Abs, [hb_], [ab_])
                    if fpend:
                        fpend.pop(0)()

                    def ones_mm(at=at, ab_=ab_, dc=dc):
                        for cs in range(4):
                            MM(sps[cs][0][:, :], onesb[:], at[:, cs * 512:(cs + 1) * 512], dc == 0, dc == nch - 1,
                               [b_const, ab_], sps[cs][1])
                    fpend.append(ones_mm)
                while fpend:
                    fpend.pop(0)()
                for cs in range(4):
                    TS(recs[:, cs * 512:(cs + 1) * 512], sps[cs][0][:, :], EPS, None, ALU.add, None, [sps[cs][1]], [brec])
                S.op("dve", lambda e: e.reciprocal(out=recs[:], in_=recs[:]), reads=[brec], writes=[brec])
                nt = Ring(st1, "fnt", [128, 512], F32, 4)
                for dc in range(nch):
                    for o in range(2):
                        cF = slice(o * 1024, o * 1024 + 512); cB = slice(o * 1024 + 512, o * 1024 + 1024)
                        t1, t1b = nt.next(); t2, t2b = nt.next()
                        TT(t1[:], hb[:, dc, cF], recs[:, cF], ALU.mult, [bhbs[dc], brec], [t1b])
                        TT(t2[:], hb[:, dc, cB], recs[:, cB], ALU.mult, [bhbs[dc], brec], [t2b])
                        TT(hb[:, dc, cF], t1[:], t2[:], ALU.add, [t1b, t2b], [bhbs[dc]])
                        TT(hb[:, dc, cB], t1[:], t2[:], ALU.subtract, [t1b, t2b], [bhbs[dc]], eng="pool")
                PS_release(sheld)
                S.fence()
            with ExitStack() as st2:
                phi = sb(st2, "fphi", [128, nch, 2]); bphi = Buf()
                LD(phi[:], cd["phi%d" % L][:, :, :], [bphi])
                nsl = min(512, L)
                cr = Ring(st2, "fC", [128, nch, nsl], BF16, 2)
                sr = Ring(st2, "fS", [128, nch, nsl], BF16, 2)
                ko = Ring(st2, "fko", [128, 2, 1024], F32, 2)
                tr_ = Ring(st2, "ft", [128, 512], F32, 8)
                for f0 in range(0, L, nsl):
                    ct, cb_ = cr.next(); st_, sb__ = sr.next()
                    LD(ct[:], cd["hyC%d" % L][:, f0:f0 + nsl].rearrange("(k p) f -> p k f", p=128), [cb_])
                    LD(st_[:], cd["hyS%d" % L][:, f0:f0 + nsl].rearrange("(k p) f -> p k f", p=128), [sb__])
                    for fs in range(nsl // 128):
                        fc = f0 // 128 + fs
                        fsl = slice(fs * 128, (fs + 1) * 128)
                        cph = phi[:, fc, 0:1]; sph = phi[:, fc, 1:2]
                        kt, kb_ = ko.next()
                        for o in range(2):
                            cF = slice(o * 1024, o * 1024 + 512); cB = slice(o * 1024 + 512, o * 1024 + 1024)
                            psA, pbA = PS(); psB, pbB = PS()
                            for dc in range(nch):
                                MM(psA[:, :], ct[:, dc, fsl], hb[:, dc, cF], dc == 0, dc == nch - 1, [cb_, bhbs[dc]], pbA)
                            for dc in range(nch):
                                MM(psB[:, :], st_[:, dc, fsl], hb[:, dc, cB], dc == 0, dc == nch - 1, [sb__, bhbs[dc]], pbB)
                            t1, t1b = tr_.next(); t2, t2b = tr_.next()
                            TS(t1[:], psB[:, :], sph, None, ALU.mult, None, [pbB, bphi], [t1b])
                            STT(kt[:, 0, o * 512:(o + 1) * 512], psA[:, :], cph, t1[:], ALU.mult, ALU.add, [pbA, bphi, t1b], [kb_])
                            TS(t2[:], psB[:, :], cph, None, ALU.mult, None, [pbB, bphi], [t2b])
                            STT(kt[:, 1, o * 512:(o + 1) * 512], psA[:, :], sph, t2[:], ALU.mult, ALU.subtract, [pbA, bphi, t2b], [kb_])
                        for ri in range(2):
                            STo(kf[ri, :, fc * 128:(fc + 1) * 128, :].rearrange("o f c -> f o c"),
                                kt[:, ri, :].rearrange("p (o c) -> p o c", o=2), [kb_])
                S.fence()

    def pass1(l):
        STQ[0] = "sp"
        sm = small[l]
        with ExitStack() as st:
            hTs = [sb(st, "hT", [128, 16, 1024], BF16) for _ in range(2)]
            b_hTs = [Buf(), Buf()]
            wr = Ring(st, "wsb", [128, 16, 512], BF16, 2)
            xc = Ring(st, "xc", [128, 16, 512], F32, 2)
            sqr = Ring(st, "sq", [128, 512], BF16, 3)
            sqf = Ring(st, "sqf", [128, 512], F32, 2)
            accr = Ring(st, "acc", [128, 512], F32, 2)
            rsr = Ring(st, "rs", [128, 512], F32, 3)
            tmr = Ring(st, "tm", [128, 512], F32, 2)
            ef = Ring(st, "ef", [128, 512], F32, 4)
            eb = Ring(st, "eb", [128, 512], BF16, 4)
            sm4 = Ring(st, "sm4", [128, 8], F32, 2)
            loaded = {}

            def prep_load(blk, tc):
                xt_, xb_ = xc.next()
                t0 = blk * 1024
                LD(xt_[:], x_src(l)[:, t0 + tc * 512:t0 + (tc + 1) * 512].rearrange("(k p) t -> p k t", p=128), [xb_])
                loaded[(blk, tc)] = (xt_, xb_)

            def prep_compute(blk, tc):
                cond = 0 if blk == 0 else 1
                hT = hTs[blk % 2]; b_hT = b_hTs[blk % 2]
                xt_, xb_ = loaded.pop((blk, tc))
                psr, pbr = PS()
                if blk == 0:
                    for kc in range(16):
                        sq, sqb = sqr.next()
                        ACT(sq[:], xt_[:, kc, :], AF.Square, [xb_], [sqb])
                        MM(psr[:, :], onesb[:], sq[:], kc == 0, kc == 15, [b_const, sqb], pbr)
                else:
                    acc, accb = accr.next()
                    for kc in range(16):
                        if kc == 0:
                            TT(acc[:], xt_[:, 0, :], xt_[:, 0, :], ALU.mult, [xb_], [accb], eng="pool")
                        else:
                            sq, sqb = sqf.next()
                            TT(sq[:], xt_[:, kc, :], xt_[:, kc, :], ALU.mult, [xb_], [sqb], eng="pool")
                            TT(acc[:], acc[:], sq[:], ALU.add, [accb, sqb], [accb], eng="pool")
                    MM(psr[:, :], onesf[:], acc[:], True, True, [b_const, accb], pbr)
                rs, rsb = rsr.next()
                rstd_from(psr[:, :], rs[:], 1.0 / DM, [pbr], [rsb])
                for kc in range(16):
                    tm, tmb = tmr.next()
                    TT(tm[:], xt_[:, kc, :], rs[:], ALU.mult, [xb_, rsb], [tmb])
                    ACT(hT[:, kc, tc * 512:(tc + 1) * 512], tm[:], AF.Identity, [tmb, b_ada], [b_hT],
                        scale=Gm[l][:, kc, cond:cond + 1], bias=adaT[l][:, kc, cond:cond + 1])

            pending = []

            def flush():
                while pending:
                    pending.pop(0)()

            wstream = WStream(wr, [w_in[l, :, sbk * 512:(sbk + 1) * 512] for blk in range(3) for sbk in range(26)])
            prep_load(0, 0); prep_load(0, 1); prep_compute(0, 0); prep_compute(0, 1)
            for blk in range(3):
                t0 = blk * 1024
                hT = hTs[blk % 2]; b_hT = b_hTs[blk % 2]
                for sbk in range(26):
                    if blk + 1 < 3:
                        if sbk == 1:
                            prep_load(blk + 1, 0)
                        if sbk == 6:
                            prep_compute(blk + 1, 0)
                        if sbk == 8:
                            prep_load(blk + 1, 1)
                        if sbk == 14:
                            prep_compute(blk + 1, 1)
                    wt, wb = wstream.get()
                    fm = sbk not in (4, 5)
                    if fm:
                        for cb4 in range(4):
                            for tc in range(2):
                                ps, pb = PS()
                                for kc in range(16):
                                    MM(ps[:, :], wt[:, kc, cb4 * 128:(cb4 + 1) * 128], hT[:, kc, tc * 512:(tc + 1) * 512],
                                       kc == 0, kc == 15, [wb, b_hT], pb)
                                tsl = slice(t0 + tc * 512, t0 + (tc + 1) * 512)
                                if sbk < 4:
                                    raw, rb = ef.next()
                                    ACT(raw[:], ps[:, :], AF.Identity, [pb], [rb])
                                    sq, sqb = sqr.next()
                                    TT(sq[:], raw[:], raw[:], ALU.mult, [rb], [sqb])
                                    flush()

                                    def partB(raw=raw, rb=rb, sq=sq, sqb=sqb, sbk=sbk, cb4=cb4, tsl=tsl):
                                        ps2, pb2 = PS()
                                        MM(ps2[:, :], onesb[:], sq[:], True, True, [b_const, sqb], pb2)
                                        rs, rsb = rsr.next()
                                        rstd_from(ps2[:, :], rs[:], 1.0 / 128, [pb2], [rsb])
                                        o, ob = eb.next()
                                        gcol = sm["qg"] if sbk < 2 else sm["kg"]
                                        STT(o[:], raw[:], gcol[:, 0:1], rs[:], ALU.mult, ALU.mult, [rb, rsb, b_small], [ob])
                                        dst = qT if sbk < 2 else kT
                                        r0 = (sbk % 2) * 512 + cb4 * 128
                                        STo(dst[r0:r0 + 128, tsl], o[:], [ob])
                                    pending.append(partB)
                                    continue
                                flush()
                                if sbk in (6, 7, 9, 13):
                                    o, ob = eb.next()
                                    ACT(o[:], ps[:, :], AF.Silu, [pb], [ob])
                                    r0 = {6: 0, 7: 512, 9: 1024, 13: 1536}[sbk] + cb4 * 128
                                    STo(gT[r0:r0 + 128, tsl], o[:], [ob])
                                elif sbk == 8:
                                    o, ob = eb.next()
                                    CP(o[:], ps[:, :], [pb], [ob])
                                    STo(ubT[cb4 * 128:(cb4 + 1) * 128, tsl], o[:], [ob])
                                elif sbk in (10, 11, 12):
                                    o, ob = ef.next()
                                    CP(o[:], ps[:, :], [pb], [ob])
                                    r0 = (sbk - 10) * 512 + cb4 * 128
                                    STo(hcT[r0:r0 + 128, tsl], o[:], [ob])
                                else:
                                    o, ob = eb.next()
                                    ACT(o[:], ps[:, :], AF.Sigmoid, [pb], [ob])
                                    r0 = (sbk - 14) * 512 + cb4 * 128
                                    STo(mgT[r0:r0 + 128, tsl], o[:], [ob])
                    if sbk in (4, 5) or (sbk in (2, 3) and blk == 0):
                        for tt in range(8):
                            ps, pb = PS()
                            for kc in range(16):
                                MM(ps[:, :], hT[:, kc, tt * 128:(tt + 1) * 128], wt[:, kc, :], kc == 0, kc == 15, [wb, b_hT], pb)
                            flush()
                            raw, rb = ef.next()
                            ACT(raw[:], ps[:, :], AF.Identity, [pb], [rb])
                            tok0 = t0 + tt * 128
                            if sbk in (4, 5):
                                c0 = (sbk - 4) * 512
                                if blk == 0:
                                    STo(newv[l, tok0:tok0 + 128, c0:c0 + 512], raw[:], [rb], is_output=True)
                                o, ob = eb.next()
                                CP(o[:], raw[:], [rb], [ob])
                                STo(vtok[tok0:tok0 + 128, c0:c0 + 512], o[:], [ob])
                            else:
                                c0 = (sbk - 2) * 512
                                sq, sqb = ef.next()
                                TT(sq[:], raw[:], raw[:], ALU.mult, [rb], [sqb])
                                s4, s4b = sm4.next()
                                RED(s4[:, 0:4], sq[:].rearrange("p (h d) -> p h d", h=4), [sqb], [s4b])
                                rstd_from(s4[:, 0:4], s4[:, 4:8], 1.0 / 128, [s4b], [s4b])
                                for h in range(4):
                                    STT(sq[:, h * 128:(h + 1) * 128], raw[:, h * 128:(h + 1) * 128], s4[:, 4 + h:5 + h],
                                        sm["kgr"][:], ALU.mult, ALU.mult, [rb, s4b, b_small, sqb], [sqb])
                                STo(newk[l, tok0:tok0 + 128, c0:c0 + 512], sq[:], [sqb], is_output=True)
                flush()
            S.fence()

    def attention(l):
        sm = small[l]
        STQ[0] = "pool"
        with ExitStack() as st:
            tr_ = Ring(st, "abt", [128, 14 * 64], F32, 2)
            mk = sb(st, "amask", [128, 14 * 64]); bmk = Buf()
            LD(mk[:], cd["nmask"].rearrange("p a c -> p (a c)"), [bmk])
            for h in range(8):
                t, tb = tr_.next()
                LD(t[:], rpbT2[l, h], [tb])
                ACT(t[:], t[:], AF.Exp, [tb], [tb])
                TT(t[:], t[:], mk[:], ALU.mult, [tb, bmk], [tb])
                STo(ebs[h], t[:], [tb])
            S.fence()
        with ExitStack() as st:
            qr = Ring(st, "aq", [128, 2048], BF16, 3)
            kr = Ring(st, "ak", [128, 2048], BF16, 3)
            v0r = Ring(st, "av0", [128, 16, 128], BF16, 3)
            v1r = Ring(st, "av1", [128, 16, 128], BF16, 2)
            ckr = Ring(st, "ack", [128, 4, 128], F32, 2)
            cktr = Ring(st, "ackT", [128, 512], BF16, 2)
            cvr = Ring(st, "acv", [128, 4, 128], BF16, 2)
            ebr = Ring(st, "aeb", [128, 14, 64], F32, 2)
            pcr = Ring(st, "apc", [128, 4, 512], BF16, 2)
            pwr = Ring(st, "apw", [128, 8, 64], BF16, 6)
            gar = Ring(st, "aga", [128, 512], BF16, 3)
            rcr = Ring(st, "arc", [128, 512], F32, 2)
            t2r = Ring(st, "at2", [128, 512], F32, 2)
            yor = Ring(st, "ayo", [128, 512], BF16, 2)
            chr_ = Ring(st, "yh", [128, NT], F32, 2)
            cor_ = Ring(st, "yo", [128, NT], F32, 2)
            segs = [(s_ * 256, 256) for s_ in range(4)] + [(1024, 2048)]
            conv_todo = list(range(12))
            wcr = Ring(st, "awc", [128, 16, 512], BF16, 2)
            wc_todo = [(w, sbk) for w in (w_br, w_out) for sbk in range(4)]
            wc_pend = []

            def wcast_unit():
                while wc_pend:
                    wc_pend.pop(0)()
                if not wc_todo:
                    return
                w, sbk = wc_todo.pop(0)
                idx = 7 - len(wc_todo)
                wt, wb = wcr.next()
                LD(wt[:], w[l, :, sbk * 512:(sbk + 1) * 512].rearrange("(k p) n -> p k n", p=128), [wb], q="pool")
                wc_pend.append(lambda: S.dma("pool", lambda e: e.dma_start(out=w16[idx], in_=wt[:].rearrange("p k n -> p (k n)")),
                                             reads=[wb], writes=[]))

            def conv_unit():
                if not conv_todo:
                    return
                cb = conv_todo.pop(0)
                ht, hb_ = chr_.next()
                LD(ht[:], hcT[cb * 128:(cb + 1) * 128, :], [hb_])
                ot, ob = cor_.next()
                ACT(ot[:], ht[:], AF.Identity, [hb_, b_small], [ob], scale=sm["cw"][:, cb, 1:2], bias=sm["cb"][:, cb:cb + 1])
                for (t0, L) in segs:
                    STT(ot[:, t0 + 1:t0 + L], ht[:, t0:t0 + L - 1], sm["cw"][:, cb, 0:1], ot[:, t0 + 1:t0 + L], ALU.mult, ALU.add, [hb_, b_small, ob], [ob])
                    STT(ot[:, t0:t0 + L - 1], ht[:, t0 + 1:t0 + L], sm["cw"][:, cb, 2:3], ot[:, t0:t0 + L - 1], ALU.mult, ALU.add, [hb_, b_small, ob], [ob])
                STo(hccT[cb * 128:(cb + 1) * 128, :], ot[:], [ob])

            def epilogue(psO, pbO, psD, pbD, n, grow, tsl):
                rc, rcb = rcr.next()
                S.op("dve", lambda e: e.reciprocal(out=rc[:, :n], in_=psD[:, :n]), reads=[pbD], writes=[rcb])
                ga, gab = gar.next()
                LD(ga[:, :n], gT[grow:grow + 128, tsl], [gab])
                t2, t2b = t2r.next()
                TT(t2[:, :n], psO[:, :n], rc[:, :n], ALU.mult, [pbO, rcb], [t2b])
                yo, yob = yor.next()
                TT(yo[:, :n], t2[:, :n], ga[:, :n], ALU.mult, [t2b, gab], [yob])
                STo(yT[grow:grow + 128, tsl], yo[:, :n], [yob])

            pend = []
            for s in range(4):
                for h in range(8):
                    t0 = s * 256
                    qt, qb = qr.next(); kt, kb = kr.next(); vt, vb = v0r.next()
                    LD(qt[:, 0:256], qT[h * 128:(h + 1) * 128, t0:t0 + 256], [qb])
                    LD(kt[:, 0:256], kT[h * 128:(h + 1) * 128, t0:t0 + 256], [kb])
                    LD(vt[:, 0:2, :], vtok[t0:t0 + 256, h * 128:(h + 1) * 128].rearrange("(k p) d -> p k d", p=128), [vb])
                    pc, pcb = pcr.next()
                    ps, pb = PS()
                    for kc in range(2):
                        MM(ps[:, kc * 256:(kc + 1) * 256], kt[:, kc * 128:(kc + 1) * 128], qt[:, 0:256], True, True, [kb, qb], pb)
                    ACT(pc[:, 0, :], ps[:, :], AF.Exp, [pb], [pcb])
                    while pend:
                        pend.pop(0)()

                    def ph2(pc=pc, pcb=pcb, vt=vt, vb=vb, h=h, t0=t0):
                        psD, pbD = PS(); psO, pbO = PS()
                        for kc in range(2):
                            MM(psD[:, 0:256], onesb[:], pc[:, 0, kc * 256:(kc + 1) * 256], kc == 0, kc == 1, [b_const, pcb], pbD)
                        for kc in range(2):
                            MM(psO[:, 0:256], vt[:, kc, :], pc[:, 0, kc * 256:(kc + 1) * 256], kc == 0, kc == 1, [vb, pcb], pbO)
                        epilogue(psO, pbO, psD, pbD, 256, h * 128, slice(t0, t0 + 256))
                    pend.append(ph2)
            while pend:
                pend.pop(0)()
            for h in range(8):
                qt, qb = qr.next(); kt, kb = kr.next(); v0, v0b = v0r.next(); v1, v1b = v1r.next()
                LD(qt[:], qT[h * 128:(h + 1) * 128, 1024:3072], [qb])
                LD(kt[:], kT[h * 128:(h + 1) * 128, 1024:3072], [kb])
                LD(v0[:], vtok[1024:3072, h * 128:(h + 1) * 128].rearrange("(k p) d -> p k d", p=128), [v0b])
                LD(v1[:], vtok[1088:3136, h * 128:(h + 1) * 128].rearrange("(k p) d -> p k d", p=128), [v1b])
                ckt, ckb = ckr.next()
                LD(ckt[:], ck[l, :, h * 128:(h + 1) * 128].rearrange("(k p) d -> p k d", p=128), [ckb])
                cvt, cvb = cvr.next()
                LD(cvt[:], cv[l, :, h * 128:(h + 1) * 128].rearrange("(k p) d -> p k d", p=128), [cvb], q="pool")
                ebt, ebb = ebr.next()
                LD(ebt[:], ebs[h].rearrange("p (a c) -> p a c", a=14), [ebb])
                ps, pb = PS()
                for kc in range(4):
                    TR(ps[:, kc * 128:(kc + 1) * 128], ckt[:, kc, :], ident[:], [ckb], pb)
                cT, cTb = cktr.next()
                CP(cT[:], ps[:, :], [pb], [cTb])
                for g in range(4):
                    qs = slice(g * 512, (g + 1) * 512)
                    pc, pcb = pcr.next()
                    for kc in range(4):
                        ps, pb = PS()
                        MM(ps[:, :], cT[:, kc * 128:(kc + 1) * 128], qt[:, qs], True, True, [cTb, qb], pb)
                        ACT(pc[:, kc, :], ps[:, :], AF.Exp, [pb], [pcb])
                    (hp, held) = PS_hold(2)
                    (psD, pbD), (psO, pbO) = hp
                    for kc in range(4):
                        MM(psD[:, :], onesb[:], pc[:, kc, :], kc == 0, False, [b_const, pcb], pbD, skip=True)
                    for kc in range(4):
                        MM(psO[:, :], cvt[:, kc, :], pc[:, kc, :], kc == 0, False, [cvb, pcb], pbO, skip=True)
                    rowinfo = []
                    for pr in range(4):
                        psS, pbS = PS()
                        pw, pwb = pwr.next()
                        for half in range(2):
                            r = g * 8 + pr * 2 + half
                            rs_ = min(max(r - 4, 0), 24)
                            q64 = slice(r * 64, (r + 1) * 64)
                            for j in range(4):
                                k0 = (rs_ + 2 * j) * 64
                                c0 = half * 256 + j * 64
                                MM(psS[:, c0:c0 + 64], kt[:, k0:k0 + 128], qt[:, q64], True, True, [kb, qb], pbS)
                        ACT(pw[:].rearrange("p a c -> p (a c)"), psS[:, :], AF.Exp, [pbS], [pwb])
                        for half in range(2):
                            r = g * 8 + pr * 2 + half
                            rs_ = min(max(r - 4, 0), 24)
                            dr0 = rs_ - r + 7
                            TT(pw[:, half * 4:(half + 1) * 4, :], pw[:, half * 4:(half + 1) * 4, :], ebt[:, dr0:dr0 + 7:2, :],
                               ALU.mult, [pwb, ebb], [pwb])
                            rowinfo.append((pr * 2 + half, rs_, pw, pwb, half))
                    for (rr, rs_, pw, pwb, half) in rowinfo:
                        o64 = slice(rr * 64, (rr + 1) * 64)
                        last = rr == 7
                        for j in range(4):
                            MM(psD[:, o64], onesb[:], pw[:, half * 4 + j, :], False, last and j == 3, [b_const, pwb], pbD, skip=True)
                        for j in range(4):
                            row0 = rs_ + 2 * j
                            if row0 % 2 == 0:
                                vap = v0[:, row0 // 2, :]; vbb = v0b
                            else:
                                vap = v1[:, (row0 - 1) // 2, :]; vbb = v1b
                            MM(psO[:, o64], vap, pw[:, half * 4 + j, :], False, last and j == 3, [vbb, pwb], pbO, skip=True)
                    epilogue(psO, pbO, psD, pbD, 512, h * 128, slice(1024 + g * 512, 1024 + (g + 1) * 512))
                    PS_release(held)
                    if g % 2 == 1:
                        conv_unit()
                    else:
                        wcast_unit()
            while conv_todo:
                conv_unit()
            while wc_todo or wc_pend:
                wcast_unit()
            S.fence()
        STQ[0] = "sp"

    def fnet(l):
        STQ[0] = "pool"
        with ExitStack() as st:
            cs_ = sb(st, "ncs", [128, 256], BF16); bcs = Buf()
            LD(cs_[:], cd["fnCS"][:, :], [bcs])
            for (L, seqs) in ((256, [(s * 256) for s in range(4)]), (2048, [1024])):
                nch = L // 128
                with ExitStack() as st1:
                    ur = Ring(st1, "nu", [128, L], BF16, 2)
                    P12 = sb(st1, "nP", [128, 4, nch, 256], BF16); bP = Buf()
                    nsl = min(512, L)
                    clr = Ring(st1, "ncl", [128, nch, nsl], BF16, 2)
                    slr = Ring(st1, "nsl", [128, nch, nsl], BF16, 2)
                    gbr = Ring(st1, "ngb", [128, 512], BF16, 2)
                    yor = Ring(st1, "nyo", [128, 512], BF16, 2)
                    for t0 in seqs:
                        for g in range(4):
                            ut, ub_ = ur.next()
                            LD(ut[:], ubT[g * 128:(g + 1) * 128, t0:t0 + L], [ub_])
                            for lc in range(nch):
                                ps, pb = PS()
                                MM(ps[:, 0:256], ut[:, lc * 128:(lc + 1) * 128], cs_[:], True, True, [ub_, bcs], pb)
                                if lc % 2 == 0:
                                    CP(P12[:, g, lc, :], ps[:, 0:256], [pb], [bP])
                                else:
                                    ACT(P12[:, g, lc, :], ps[:, 0:256], AF.Identity, [pb], [bP])
                        for c0 in range(0, L, nsl):
                            ct, cb_ = clr.next(); st_, sb__ = slr.next()
                            LD(ct[:], cd["fnC%d" % L][:, c0:c0 + nsl].rearrange("(k p) n -> p k n", p=128), [cb_])
                            LD(st_[:], cd["fnSn%d" % L][:, c0:c0 + nsl].rearrange("(k p) n -> p k n", p=128), [sb__])
                            for g in range(4):
                                ps, pb = PS()
                                for lc in range(nch):
                                    MM(ps[:, :nsl], P12[:, g, lc, 0:128], ct[:, lc, :], lc == 0, False, [bP, cb_], pb)
                                for lc in range(nch):
                                    MM(ps[:, :nsl], P12[:, g, lc, 128:256], st_[:, lc, :], False, lc == nch - 1, [bP, sb__], pb)
                                gb, gbb = gbr.next()
                                tsl = slice(t0 + c0, t0 + c0 + nsl)
                                LD(gb[:, :nsl], gT[1024 + g * 128:1024 + (g + 1) * 128, tsl], [gbb])
                                yo, yob = yor.next()
                                TT(yo[:, :nsl], ps[:, :nsl], gb[:, :nsl], ALU.mult, [pb, gbb], [yob])
                                STo(yT[1024 + g * 128:1024 + (g + 1) * 128, tsl], yo[:, :nsl], [yob])
                    S.fence()
            S.fence()

    def hyena(l):
        sm = small[l]
        STQ[0] = "pool"
        for (L, seqs) in ((256, [s * 256 for s in range(4)]), (2048, [1024])):
            nch = L // 128
            nsl = min(512, L)
            kf = kf_s[L]
            with ExitStack() as st:
                nbuf = 4 if L == 256 else 1
                ztr = Ring(st, "yz", [128, nch, 512], BF16, nbuf)
                yfr = Ring(st, "yY", [128, nch, 2, 512], BF16, nbuf)
                zin = Ring(st, "yzin", [128, L], F32, nbuf)
                zbf = Ring(st, "yzbf", [128, L], BF16, nbuf)
                cr = Ring(st, "yC", [128, nch, nsl], BF16, 2)
                sr = Ring(st, "yS", [128, nch, nsl], BF16, 2)
                kr_ = Ring(st, "yk", [128, 2, 512], F32, 2)
                t1 = Ring(st, "yt1", [128, 512], F32, 8)
                xr_ = Ring(st, "yx", [128, 512], F32, 6)
                o32 = Ring(st, "yo32", [128, 512], F32, 2)
                o16 = Ring(st, "yo16", [128, 512], BF16, 4)

                cs_cache = {}
                kf_cache = {}
                kfr_small = Ring(st, "ykc", [128, 2, 512], F32, 4) if L == 256 else None

                def get_cs(f0):
                    if L == 256 and f0 in cs_cache:
                        return cs_cache[f0]
                    ct, cb_ = cr.next(); st_, sb__ = sr.next()
                    LD(ct[:], cd["hyC%d" % L][:, f0:f0 + nsl].rearrange("(k p) n -> p k n", p=128), [cb_])
                    LD(st_[:], cd["hyS%d" % L][:, f0:f0 + nsl].rearrange("(k p) n -> p k n", p=128), [sb__])
                    cs_cache[f0] = (ct, cb_, st_, sb__)
                    return cs_cache[f0]

                def get_kf(o, fc):
                    if L == 256 and (o, fc) in kf_cache:
                        return kf_cache[(o, fc)]
                    kt, kb_ = (kfr_small if L == 256 else kr_).next()
                    LD(kt[:, 0, :], kf[0, o, fc * 128:(fc + 1) * 128, :], [kb_])
                    LD(kt[:, 1, :], kf[1, o, fc * 128:(fc + 1) * 128, :], [kb_])
                    kf_cache[(o, fc)] = (kt, kb_)
                    return kf_cache[(o, fc)]

                def load_ztok(src, t0, ztok, bz):
                    for cb in range(4):
                        zt, zb_ = zin.next()
                        LD(zt[:], src[cb * 128:(cb + 1) * 128, t0:t0 + L], [zb_])
                        zh, zhb = zbf.next()
                        CP(zh[:], zt[:], [zb_], [zhb])
                        for tc0 in range(0, nch, 4):
                            nb = min(4, nch - tc0)
                            ps, pb = PS()
                            psv = ps[:, :].bitcast(BF16)
                            for j in range(nb):
                                TR(psv[:, j * 128:(j + 1) * 128], zh[:, (tc0 + j) * 128:(tc0 + j + 1) * 128], identb[:], [zhb], pb)
                            CP(ztok[:, tc0:tc0 + nb, cb * 128:(cb + 1) * 128],
                               psv[:, 0:nb * 128].rearrange("p (j c) -> p j c", j=nb), [pb], [bz])

                for o in range(2):
                    units = []
                    for t0 in seqs:
                        ztok, bz = ztr.next()
                        Yf, bY = yfr.next()
                        load_ztok(hccT if o == 0 else z1T, t0, ztok, bz)
                        units.append((t0, ztok, bz, Yf, bY))
                    for (t0, ztok, bz, Yf, bY) in units:
                        for f0 in range(0, L, nsl):
                            ct, cb_, st_, sb__ = get_cs(f0)
                            for fs in range(nsl // 128):
                                fc = f0 // 128 + fs
                                kt, kb_ = get_kf(o, fc)
                                psA, pbA = PS(); psB, pbB = PS()
                                for tc in range(nch):
                                    MM(psA[:, :], ct[:, tc, fs * 128:(fs + 1) * 128], ztok[:, tc, :], tc == 0, tc == nch - 1, [cb_, bz], pbA)
                                for tc in range(nch):
                                    MM(psB[:, :], st_[:, tc, fs * 128:(fs + 1) * 128], ztok[:, tc, :], tc == 0, tc == nch - 1, [sb__, bz], pbB)
                                a, ab = t1.next(); b, bb = t1.next()
                                TT(a[:], psA[:, :], kt[:, 0, :], ALU.mult, [pbA, kb_], [ab])
                                TT(b[:], psB[:, :], kt[:, 1, :], ALU.mult, [pbB, kb_], [bb])
                                TT(Yf[:, fc, 0, :], a[:], b[:], ALU.add, [ab, bb], [bY], eng="pool")
                                a2, ab2 = t1.next(); b2, bb2 = t1.next()
                                TT(a2[:], psB[:, :], kt[:, 0, :], ALU.mult, [pbB, kb_], [ab2])
                                TT(b2[:], psA[:, :], kt[:, 1, :], ALU.mult, [pbA, kb_], [bb2])
                                TT(Yf[:, fc, 1, :], a2[:], b2[:], ALU.subtract, [ab2, bb2], [bY], eng="pool")
                    for (t0, ztok, bz, Yf, bY) in units:
                        for c0 in range(0, L, nsl):
                            ct, cb_, st_, sb__ = get_cs(c0)
                            tsl = slice(t0 + c0, t0 + c0 + nsl)
                            for cb in range(4):
                                ps, pb = PS()
                                for fc in range(nch):
                                    MM(ps[:, :nsl], Yf[:, fc, 0, cb * 128:(cb + 1) * 128], ct[:, fc, :], fc == 0, False, [bY, cb_], pb)
                                for fc in range(nch):
                                    MM(ps[:, :nsl], Yf[:, fc, 1, cb * 128:(cb + 1) * 128], st_[:, fc, :], False, fc == nch - 1, [bY, sb__], pb)
                                zi, zib = xr_.next(); xg, xgb = xr_.next()
                                if o == 0:
                                    LD(zi[:, :nsl], hccT[cb * 128:(cb + 1) * 128, tsl], [zib])
                                    LD(xg[:, :nsl], hccT[512 + cb * 128:512 + (cb + 1) * 128, tsl], [xgb])
                                else:
                                    LD(zi[:, :nsl], z1T[cb * 128:(cb + 1) * 128, tsl], [zib])
                                    LD(xg[:, :nsl], hccT[1024 + cb * 128:1024 + (cb + 1) * 128, tsl], [xgb])
                                r, rb = o32.next()
                                STT(r[:, :nsl], zi[:, :nsl], sm["hb"][:, o, cb:cb + 1], ps[:, :nsl], ALU.mult, ALU.add, [zib, b_small, pb], [rb])
                                TT(r[:, :nsl], r[:, :nsl], xg[:, :nsl], ALU.mult, [rb, xgb], [rb])
                                if o == 0:
                                    STo(z1T[cb * 128:(cb + 1) * 128, tsl], r[:, :nsl], [rb])
                                else:
                                    gc, gcb = o16.next()
                                    LD(gc[:, :nsl], gT[1536 + cb * 128:1536 + (cb + 1) * 128, tsl], [gcb])
                                    yo, yob = o16.next()
                                    TT(yo[:, :nsl], r[:, :nsl], gc[:, :nsl], ALU.mult, [rb, gcb], [yob])
                                    STo(yT[1536 + cb * 128:1536 + (cb + 1) * 128, tsl], yo[:, :nsl], [yob])
                    S.fence()
                S.fence()

    def pass2(l):
        STQ[0] = "sp"
        with ExitStack() as st:
            ySr = Ring(st, "pY", [128, 16, 1024], BF16, 2)
            mS = sb(st, "pM", [128, 16, 1024], BF16); bMs = Buf()
            ynext = ySr.next()
            LD(ynext[0][:], yT[:, 0:1024].rearrange("(k p) t -> p k t", p=128), [ynext[1]])
            wr = Ring(st, "pw", [128, 16, 512], BF16, 2)
            wstream = WStream(wr, [w16[i] for blk in range(3) for i in range(8)], pre=True)
            gr = Ring(st, "pg", [128, 3, 512], BF16, 3)
            t1 = Ring(st, "pt", [128, 512], F32, 9)
            xr_ = Ring(st, "px", [128, 512], F32, 2)
            xo = Ring(st, "pxo", [128, 512], F32, 2)
            p2pend = []
            for blk in range(3):
                cond = 0 if blk == 0 else 1
                t0 = blk * 1024
                yS, bYs = ynext
                if blk < 2:
                    ynext = ySr.next()
                    LD(ynext[0][:], yT[:, t0 + 1024:t0 + 2048].rearrange("(k p) t -> p k t", p=128), [ynext[1]])
                for sbk in range(4):
                    wt, wb = wstream.get()
                    for cb4 in range(4):
                        cb = sbk * 4 + cb4
                        for tc in range(2):
                            tsl = slice(t0 + tc * 512, t0 + (tc + 1) * 512)
                            gt, gb_ = gr.next()
                            for i in range(3):
                                LD(gt[:, i, :], mgT[i * 2048 + cb * 128:i * 2048 + (cb + 1) * 128, tsl], [gb_])
                            ts_ = []
                            for i, (k0, k1) in enumerate(((0, 8), (8, 12), (12, 16))):
                                ps, pb = PS()
                                for kc in range(k0, k1):
                                    MM(ps[:, :], wt[:, kc, cb4 * 128:(cb4 + 1) * 128], yS[:, kc, tc * 512:(tc + 1) * 512],
                                       kc == k0, kc == k1 - 1, [wb, bYs], pb)
                                t, tb = t1.next()
                                TT(t[:], ps[:, :], gt[:, i, :], ALU.mult, [pb, gb_], [tb])
                                ts_.append((t, tb))
                            while p2pend:
                                p2pend.pop(0)()

                            def adds(ts_=ts_, cb=cb, tc=tc):
                                (ta, tab), (tb_, tbb), (tc_, tcb) = ts_
                                TT(ta[:], ta[:], tb_[:], ALU.add, [tab, tbb], [tab], eng="pool")
                                TT(mS[:, cb, tc * 512:(tc + 1) * 512], ta[:], tc_[:], ALU.add, [tab, tcb], [bMs], eng="pool")
                            p2pend.append(adds)
                while p2pend:
                    p2pend.pop(0)()
                for sbk in range(4):
                    wt, wb = wstream.get()
                    for cb4 in range(4):
                        cb = sbk * 4 + cb4
                        for tc in range(2):
                            tsl = slice(t0 + tc * 512, t0 + (tc + 1) * 512)
                            ps, pb = PS()
                            for kc in range(16):
                                MM(ps[:, :], wt[:, kc, cb4 * 128:(cb4 + 1) * 128], mS[:, kc, tc * 512:(tc + 1) * 512],
                                   kc == 0, kc == 15, [wb, bMs], pb)
                            xt_, xb_ = xr_.next()
                            LD(xt_[:], x_src(l)[cb * 128:(cb + 1) * 128, tsl], [xb_])
                            o, ob = xo.next()
                            STT(o[:], ps[:, :], adaT[l][:, 32 + cb, cond:cond + 1], xt_[:], ALU.mult, ALU.add, [pb, b_ada, xb_], [ob])
                            STo(x_dst(l)[cb * 128:(cb + 1) * 128, tsl], o[:], [ob], is_output=(l == NL_RUN - 1))
            S.fence()

    def on(name):
        marks.append((name, dict(S.ninstr)))
        return STAGES is None or name in STAGES

    for l in range(NL_RUN):
        if on("filt256"):
            hyena_filter_full(l, 256)
        if on("filt2048"):
            hyena_filter_full(l, 2048)
        if on("pass1"):
            pass1(l)
        if on("attn"):
            attention(l)
        if on("fnet"):
            fnet(l)
        if on("hyena"):
            hyena(l)
        if on("pass2"):
            pass2(l)

    S.emit()
    build_program.stats = {e: len(v) for e, v in S.prog.items()}
    return nc


_NC = None


def _host_inputs(inp, core):
    C = make_consts()
    b = core // 4
    xp = inp["x_prompt"][4 * core:4 * core + 4].reshape(1024, DM)
    xs = inp["x_sample"][b]
    m = {}
    m["xin"] = _f32(np.concatenate([xp, xs], 0).T)
    m["ck"] = _f32(inp["cache_k"][b].reshape(NL, 512, 1024))
    m["cv"] = _f32(inp["cache_v"][b].reshape(NL, 512, 1024))
    cvec = np.stack([inp["c_ctx"], inp["c"][b]], -1)
    m["cvecT"] = _f32(cvec.reshape(16, 128, 2).transpose(1, 0, 2))
    return m


def _shared_inputs(inp):
    C = make_consts()
    m = {}
    m["norm_gT"] = _f32(inp["norm_g"].reshape(NL, 16, 128).transpose(0, 2, 1))
    m["b_adaT"] = _f32(inp["b_ada"].reshape(NL, 48, 128).transpose(0, 2, 1))
    m["w_ada"] = _f32(inp["w_ada"]); m["w_in"] = _f32(inp["w_in"])
    m["w_br"] = _f32(inp["w_br"]); m["w_out"] = _f32(inp["w_out"])
    m["qg"] = _f32(inp["q_norm_g"].reshape(NL, 128, 1))
    m["kg"] = _f32(inp["k_norm_g"].reshape(NL, 128, 1))
    m["kg_rep"] = _f32(np.broadcast_to(inp["k_norm_g"][:, None, :], (NL, 128, 128)))
    m["rpbT2"] = rpb_gather(np.asarray(inp["rpb"]))
    m["conv_wT"] = _f32(inp["conv_w"].reshape(NL, 3, 12, 128).transpose(0, 3, 2, 1))
    m["conv_bT"] = _f32(inp["conv_b"].reshape(NL, 12, 128).transpose(0, 2, 1))
    m["hy_biasT"] = _f32(inp["hy_bias"].reshape(NL, 2, 4, 128).transpose(0, 3, 1, 2))
    m["f_w1"] = _f32(inp["f_w1"]); m["f_w2"] = _f32(inp["f_w2"]); m["f_w3"] = _f32(inp["f_w3"])
    m["f_b1T"] = _f32(inp["f_b1"].reshape(NL, 64, 1))
    m["f_b2T"] = _f32(inp["f_b2"].reshape(NL, 64, 1))
    m["f_freqT"] = _f32(inp["f_freq"].reshape(NL, 64, 1))
    for k, v in C.items():
        m["c_" + k] = v
    return m


def kernel(**inputs):
    global _NC
    inp = {k: np.asarray(v) for k, v in inputs.items()}
    if _NC is None:
        _NC = build_program()
    nc = _NC
    shared = _shared_inputs(inp)
    in_maps = []
    for core in range(8):
        m = dict(shared)
        m.update(_host_inputs(inp, core))
        in_maps.append(m)
    res = run_bass_kernel_spmd(nc, in_maps, core_ids=list(range(8)))
    R = res.results
    y_prompt = np.concatenate([np.ascontiguousarray(R[c]["y_out"][:, :1024].T).reshape(4, 256, DM) for c in range(8)], 0)
    y_sample = np.stack([np.ascontiguousarray(R[0]["y_out"][:, 1024:].T), np.ascontiguousarray(R[4]["y_out"][:, 1024:].T)], 0)
    nk = np.concatenate([R[c]["newk"].reshape(NL, 4, 256, 8, 128).transpose(1, 0, 2, 3, 4) for c in range(8)], 0)
    nv = np.concatenate([R[c]["newv"].reshape(NL, 4, 256, 8, 128).transpose(1, 0, 2, 3, 4) for c in range(8)], 0)
    if DEBUG:
        kernel.debug = R
    return (y_prompt.astype(np.float32), y_sample.astype(np.float32), nk.astype(np.float32), nv.astype(np.float32))
```

```python
import numpy as np
import concourse.bass as bass
import concourse.mybir as mybir

F32 = mybir.dt.float32
BF16 = mybir.dt.bfloat16
AF = mybir.ActivationFunctionType
ALU = mybir.AluOpType

COMPUTE = ("pe", "act", "dve", "pool")
NDMA_SEMS = 26
NSW = 6


class Buf:
    __slots__ = ("name", "w", "r")

    def __init__(self, name=""):
        self.name = name
        self.w = None
        self.r = []


class Sched:
    def __init__(self, nc):
        self.nc = nc
        self.prog = {e: [] for e in ("pe", "act", "dve", "pool", "sp")}
        self.flag = {e: set() for e in COMPUTE}
        self.ninstr = {e: 0 for e in COMPUTE}
        self.dma_n = [0] * NDMA_SEMS
        self.dma_rr = 0
        self.sw_rr = 0
        self.all_out_tokens = []
        self.pending_fence = {}

    def _collect(self, reads, writes):
        deps = []
        for b in reads:
            if b.w is not None:
                deps.append(b.w)
        for b in writes:
            if b.w is not None:
                deps.append(b.w)
            deps.extend(b.r)
        return deps

    def _commit(self, tok, reads, writes):
        for b in reads:
            b.r.append(tok)
        for b in writes:
            b.w = tok
            b.r = []

    def fence(self):
        toks = []
        for e in COMPUTE:
            if self.ninstr[e] > 0:
                toks.append(("c", e, self.ninstr[e] - 1))
        for k in range(NDMA_SEMS):
            if self.dma_n[k] > 0:
                toks.append(("d", k, self.dma_n[k]))
        self.pending_fence = {q: list(toks) for q in self.prog}

    def op(self, eng, fn, reads=(), writes=()):
        deps = self._collect(reads, writes)
        deps += self.pending_fence.pop(eng, [])
        idx = self.ninstr[eng]
        self.ninstr[eng] += 1
        tok = ("c", eng, idx)
        self.prog[eng].append([deps, fn, tok])
        self._commit(tok, reads, writes)
        return tok

    def dma(self, q, fn, reads=(), writes=(), is_output=False):
        deps = self._collect(reads, writes)
        deps += self.pending_fence.pop(q, [])
        if q == "pool":
            k = self.sw_rr
            self.sw_rr = (self.sw_rr + 1) % NSW
        else:
            k = NSW + self.dma_rr
            self.dma_rr = (self.dma_rr + 1) % (NDMA_SEMS - NSW)
        if self.dma_n[k] > 0:
            deps.append(("d", k, self.dma_n[k]))
        self.dma_n[k] += 1
        tok = ("d", k, self.dma_n[k])
        self.prog[q].append([deps, fn, tok])
        self._commit(tok, reads, writes)
        if is_output:
            self.all_out_tokens.append(tok)
        return tok

    def emit(self):
        nc = self.nc
        for e, items in self.prog.items():
            for deps, fn, tok in items:
                for d in deps:
                    if d[0] == "c":
                        if d[1] == "pe" and e == "pe":
                            continue
                        self.flag[d[1]].add(d[2])
        final_deps = list(self.all_out_tokens)
        rank = {}
        for e in COMPUTE:
            r = 0
            m = {}
            fl = self.flag[e]
            for i in range(self.ninstr[e]):
                if i in fl:
                    r += 1
                    m[i] = r
            rank[e] = m
        self.rank = rank

        def tokval(d):
            if d[0] == "c":
                return ("c", d[1]), rank[d[1]][d[2]]
            return ("d", d[1]), 16 * d[2]

        import contextlib
        with contextlib.ExitStack() as st:
            sems = {}
            for e in COMPUTE:
                sems[("c", e)] = st.enter_context(nc.semaphore("sem_" + e))
            for k in range(NDMA_SEMS):
                sems[("d", k)] = st.enter_context(nc.semaphore("sem_dma%d" % k))
            block = st.enter_context(nc.Block())

            def run(ekey, engine_obj, extra_final=None):
                seen = {}
                for deps, fn, tok in self.prog[ekey]:
                    need = {}
                    for d in deps:
                        if d[0] == "c" and d[1] == "pe" and ekey == "pe":
                            continue
                        key, val = tokval(d)
                        if seen.get(key, 0) < val:
                            if need.get(key, 0) < val:
                                need[key] = val
                    for key, val in need.items():
                        engine_obj.wait_ge(sems[key], val)
                        seen[key] = val
                    ins = fn(engine_obj)
                    if tok[0] == "c":
                        if tok[2] in self.flag[tok[1]]:
                            ins.then_inc(sems[("c", tok[1])], 1)
                    else:
                        ins.then_inc(sems[("d", tok[1])], 16)
                if extra_final:
                    need = {}
                    for d in extra_final:
                        key, val = tokval(d)
                        if need.get(key, 0) < val:
                            need[key] = val
                    for key, val in need.items():
                        engine_obj.wait_ge(sems[key], val)

            @block.sync
            def _(e):
                run("sp", e, final_deps)

            @block.tensor
            def _(e):
                run("pe", e)

            @block.scalar
            def _(e):
                run("act", e)

            @block.vector
            def _(e):
                run("dve", e)

            @block.gpsimd
            def _(e):
                run("pool", e, final_deps)

import math
from contextlib import ExitStack
import ml_dtypes
from concourse.bass_utils import run_bass_kernel_spmd

AX = mybir.AxisListType
NT = 3072
DM = 2048
NIN = 13312
NL = 2
EPS = 1e-6
MIN_DECAY = math.log(1e-2) / 1.5
MAX_DECAY = math.log(1e-2) / 0.3
DEBUG = False
NL_RUN = 2
STAGES = None

_bf = lambda a: np.ascontiguousarray(a.astype(ml_dtypes.bfloat16))
_f32 = lambda a: np.ascontiguousarray(a, dtype=np.float32)

_CONST = None


def make_consts():
    global _CONST
    if _CONST is not None:
        return _CONST
    c = {}
    c["ident"] = np.eye(128, dtype=np.float32)
    n = np.arange(128)
    ang = 2 * np.pi * (np.outer(n, n) % 128) / 128
    c["fnCS"] = _bf(np.concatenate([np.cos(ang), np.sin(ang)], 1) / np.sqrt(128))
    for L in (256, 2048):
        l = np.arange(L)
        ang = 2 * np.pi * (np.outer(l, l) % L) / L
        c["fnC%d" % L] = _bf(np.cos(ang) / np.sqrt(L))
        c["fnSn%d" % L] = _bf(-np.sin(ang) / np.sqrt(L))
        N = 2 * L
        m = np.outer(2 * l + 1, 2 * l + 1) % (4 * N)
        ang = 2 * np.pi * m / (4 * N)
        c["hyC%d" % L] = _bf(np.cos(ang))
        c["hyS%d" % L] = _bf(np.sin(ang))
        ph = np.pi * (l + 0.5) / N
        tab = np.stack([np.cos(ph) * 2 / N, np.sin(ph) * 2 / N], -1)
        c["phi%d" % L] = _f32(tab.reshape(L // 128, 128, 2).transpose(1, 0, 2))
        t = np.linspace(0.0, 1.0, L, dtype=np.float32)[:, None]
        w = (np.float32(2.0 * math.pi / L) * np.arange(L, dtype=np.float32))[:, None]
        fr = np.linspace(1e-4, 15, 16, dtype=np.float32)[None, :]
        z = np.concatenate([t, np.cos(w * fr), -np.sin(w * fr)], -1).astype(np.float32)
        c["zfT%d" % L] = _f32(z.T)
        deltas = np.abs(np.linspace(MIN_DECAY, MAX_DECAY, 512, dtype=np.float32))
        c["decay%d" % L] = _f32(np.concatenate([np.exp(-t * deltas[None, :]), np.zeros((1, 512), np.float32)], 0))
    p = np.arange(128)
    cp = p % 64
    cq = np.arange(64)
    cs = np.clip(cq - 8, 0, 48)
    valid = (cp[:, None] >= cs[None, :]) & (cp[:, None] < cs[None, :] + 16)
    c["nmask"] = _f32(np.broadcast_to(valid[:, None, :], (128, 14, 64)))
    _CONST = c
    return c


def rpb_gather(rpb):
    p = np.arange(128)
    cp = p % 64
    half = p // 64
    cq = np.arange(64)
    dc = cp[:, None] - cq[None, :] + 15
    ok = (dc >= 0) & (dc <= 30)
    dcc = np.clip(dc, 0, 30)
    dr = np.arange(14)
    drr = dr[None, :] + half[:, None]
    out = rpb[:, :, drr[:, :, None], dcc[:, None, :]]
    out = np.where(ok[None, None, :, None, :], out, 0.0)
    return _f32(out.reshape(NL, 8, 128, 14 * 64))


def build_program():
    nc = bass.Bass("TRN2", target_bir_lowering=False)
    S = Sched(nc)
    C = make_consts()

    def din(name, shape, dt=F32):
        return nc.dram_tensor(name, list(shape), dt, kind="ExternalInput").ap()

    def dscr(name, shape, dt=F32):
        kind = "ExternalOutput" if DEBUG else "Internal"
        return nc.dram_tensor(name, list(shape), dt, kind=kind).ap()

    xin = din("xin", [DM, NT])
    ck = din("ck", [NL, 512, 1024])
    cv = din("cv", [NL, 512, 1024])
    cvecT = din("cvecT", [128, 16, 2])
    norm_gT = din("norm_gT", [NL, 128, 16])
    b_adaT = din("b_adaT", [NL, 128, 48])
    w_ada = din("w_ada", [NL, DM, 3 * DM])
    w_in = din("w_in", [NL, DM, NIN])
    w_br = din("w_br", [NL, DM, DM])
    w_out = din("w_out", [NL, DM, DM])
    qg = din("qg", [NL, 128, 1])
    kg = din("kg", [NL, 128, 1])
    kg_rep = din("kg_rep", [NL, 128, 128])
    rpbT2 = din("rpbT2", [NL, 8, 128, 14 * 64])
    conv_wT = din("conv_wT", [NL, 128, 12, 3])
    conv_bT = din("conv_bT", [NL, 128, 12])
    hy_biasT = din("hy_biasT", [NL, 128, 2, 4])
    f_w1 = din("f_w1", [NL, 33, 64])
    f_b1T = din("f_b1T", [NL, 64, 1])
    f_freqT = din("f_freqT", [NL, 64, 1])
    f_w2 = din("f_w2", [NL, 64, 64])
    f_b2T = din("f_b2T", [NL, 64, 1])
    f_w3 = din("f_w3", [NL, 64, 2048])
    cd = {}
    for k, v in C.items():
        cd[k] = din("c_" + k, v.shape, BF16 if v.dtype == ml_dtypes.bfloat16 else F32)

    y_out = nc.dram_tensor("y_out", [DM, NT], F32, kind="ExternalOutput").ap()
    newk = nc.dram_tensor("newk", [NL, 1024, 1024], F32, kind="ExternalOutput").ap()
    newv = nc.dram_tensor("newv", [NL, 1024, 1024], F32, kind="ExternalOutput").ap()

    xT = dscr("s_xT", [DM, NT])
    qT = dscr("s_qT", [1024, NT], BF16)
    kT = dscr("s_kT", [1024, NT], BF16)
    vtok = dscr("s_vtok", [NT + 128, 1024], BF16)
    gT = dscr("s_gT", [2048, NT], BF16)
    ubT = dscr("s_ubT", [512, NT], BF16)
    hcT = dscr("s_hcT", [1536, NT])
    hccT = dscr("s_hccT", [1536, NT])
    z1T = dscr("s_z1T", [512, NT])
    mgT = dscr("s_mgT", [6144, NT], BF16)
    yT = dscr("s_yT", [2048, NT], BF16)
    ebs = dscr("s_eb", [8, 128, 14 * 64])
    w16 = dscr("s_w16", [8, 128, 16 * 512], BF16)
    hd_s = {L: dscr("s_hd%d" % L, [L + 128, 2048]) for L in (256, 2048)}
    kf_s = {L: dscr("s_kf%d" % L, [2, 2, L, 512]) for L in (256, 2048)}

    def sbp(name, shape, dt=F32):
        return nc.alloc_sbuf_tensor(name, list(shape), dt)

    ident = sbp("ident", [128, 128]); b_ident = Buf()
    identb = sbp("identb", [128, 128], BF16)
    onesb = sbp("onesb", [128, 128], BF16)
    onesf = sbp("onesf", [128, 128])
    epsc = sbp("epsc", [128, 1])
    zeroc = sbp("zeroc", [128, 1])
    b_const = Buf()
    adaT = [sbp("adaT%d" % l, [128, 48, 2]) for l in range(NL)]
    Gm = [sbp("Gm%d" % l, [128, 16, 2]) for l in range(NL)]
    b_ada = Buf()
    small = {}
    for l in range(NL):
        small[l] = dict(
            ng=sbp("ng%d" % l, [128, 16]), ba=sbp("ba%d" % l, [128, 48]),
            qg=sbp("qg%d" % l, [128, 1]), kg=sbp("kg%d" % l, [128, 1]),
            kgr=sbp("kgr%d" % l, [128, 128]),
            cw=sbp("cw%d" % l, [128, 12, 3]), cb=sbp("cb%d" % l, [128, 12]),
            hb=sbp("hb%d" % l, [128, 2, 4]),
        )
    b_small = Buf()

    pst = [nc.alloc_psum_tensor("ps%d" % i, [128, 512], F32) for i in range(8)]
    psb = [Buf() for _ in range(8)]
    ps_avail = list(range(8))
    psi = [0]

    def PS():
        psi[0] = (psi[0] + 1) % len(ps_avail)
        i = ps_avail[psi[0]]
        return pst[i], psb[i]

    def PS_hold(n):
        held = [ps_avail.pop() for _ in range(n)]
        psi[0] = 0
        return [(pst[i], psb[i]) for i in held], held

    def PS_release(held):
        ps_avail.extend(held)

    uid = [0]

    def uname(name):
        uid[0] += 1
        return "%s_%d" % (name, uid[0])

    class Ring:
        def __init__(self, st, name, shape, dt, n):
            self.t = [st.enter_context(nc.sbuf_tensor(uname(name), list(shape), dt)) for i in range(n)]
            self.b = [Buf() for _ in range(n)]
            self.i = 0

        def next(self):
            i = self.i
            self.i = (i + 1) % len(self.t)
            return self.t[i], self.b[i]

    class WStream:
        def __init__(self, ring, srcs, pre=False, q="pool"):
            self.ring = ring; self.srcs = srcs; self.i = 0; self.pre = pre; self.q = q
            self.cur = self._issue(0)

        def _issue(self, i):
            if i >= len(self.srcs):
                return None
            wt, wb = self.ring.next()
            if self.pre:
                LD(wt[:].rearrange("p k n -> p (k n)"), self.srcs[i], [wb], q=self.q)
            else:
                LD(wt[:], self.srcs[i].rearrange("(k p) n -> p k n", p=128), [wb], q="pool")
            return wt, wb

        def get(self):
            c = self.cur
            self.i += 1
            self.cur = self._issue(self.i)
            return c

    def sb(st, name, shape, dt=F32):
        return st.enter_context(nc.sbuf_tensor(uname(name), list(shape), dt))

    def MM(ps, lhsT, rhs, start, stop, reads, pb, skip=False):
        S.op("pe", lambda e: e.matmul(ps, lhsT=lhsT, rhs=rhs, start=start, stop=stop,
                                      skip_group_check=skip), reads=reads, writes=[pb])

    def TR(ps, in_, idt, reads, pb):
        S.op("pe", lambda e: e.transpose(out=ps, in_=in_, identity=idt), reads=reads + [b_ident], writes=[pb])

    def ACT(out, in_, func, reads, writes, scale=None, bias=None):
        kw = {}
        if scale is not None:
            kw["scale"] = scale
        if bias is not None:
            kw["bias"] = bias
        S.op("act", lambda e: e.activation(out=out, in_=in_, func=func, **kw), reads=reads, writes=writes)

    def TT(out, in0, in1, op, reads, writes, eng="dve"):
        S.op(eng, lambda e: e.tensor_tensor(out=out, in0=in0, in1=in1, op=op), reads=reads, writes=writes)

    def TS(out, in0, s1, s2, op0, op1, reads, writes, eng="dve"):
        if op1 is None:
            S.op(eng, lambda e: e.tensor_scalar(out=out, in0=in0, scalar1=s1, scalar2=None, op0=op0), reads=reads, writes=writes)
        else:
            S.op(eng, lambda e: e.tensor_scalar(out=out, in0=in0, scalar1=s1, scalar2=s2, op0=op0, op1=op1), reads=reads, writes=writes)

    def STT(out, in0, scalar, in1, op0, op1, reads, writes):
        S.op("dve", lambda e: e.scalar_tensor_tensor(out=out, in0=in0, scalar=scalar, in1=in1, op0=op0, op1=op1),
             reads=reads, writes=writes)

    def RED(out, in_, reads, writes):
        S.op("dve", lambda e: e.tensor_reduce(out=out, in_=in_, axis=AX.X, op=ALU.add), reads=reads, writes=writes)

    def CP(out, in_, reads, writes, eng="dve"):
        S.op(eng, lambda e: e.tensor_copy(out=out, in_=in_), reads=reads, writes=writes)

    def LD(out, in_, writes, reads=(), q="sp"):
        S.dma(q, lambda e: e.dma_start(out=out, in_=in_), reads=list(reads), writes=list(writes))

    STQ = ["sp"]

    def STo(out, in_, reads, writes=(), is_output=False):
        S.dma(STQ[0], lambda e: e.dma_start(out=out, in_=in_), reads=list(reads), writes=list(writes), is_output=is_output)

    def rstd_from(ps_sum, out, scale, reads, writes):
        ACT(out, ps_sum, AF.Ln, reads + [b_const], writes, scale=scale, bias=epsc[:out.shape[0], 0:1])
        ACT(out, out, AF.Exp, writes, writes, scale=-0.5)

    LD(ident[:], cd["ident"][:, :], [b_ident])
    S.op("dve", lambda e: e.tensor_copy(out=identb[:], in_=ident[:]), reads=[b_ident], writes=[b_ident])
    S.op("dve", lambda e: e.memset(onesb[:], 1.0), writes=[b_const])
    S.op("dve", lambda e: e.memset(onesf[:], 1.0), writes=[b_const])
    S.op("dve", lambda e: e.memset(epsc[:], EPS), writes=[b_const])
    S.op("dve", lambda e: e.memset(zeroc[:], 0.0), writes=[b_const])
    for l in range(NL):
        sm = small[l]
        LD(sm["ng"][:], norm_gT[l], [b_small]); LD(sm["ba"][:], b_adaT[l], [b_small])
        LD(sm["qg"][:], qg[l], [b_small]); LD(sm["kg"][:], kg[l], [b_small])
        LD(sm["kgr"][:], kg_rep[l], [b_small])
        LD(sm["cw"][:], conv_wT[l], [b_small]); LD(sm["cb"][:], conv_bT[l], [b_small])
        LD(sm["hb"][:], hy_biasT[l], [b_small])
        TS(sm["qg"][:], sm["qg"][:], float(128 ** -0.5), None, ALU.mult, None, [b_small], [b_small])

    with ExitStack() as st:
        war = Ring(st, "wada", [128, 16, 512], BF16, 3)
        sil0 = sb(st, "sil0", [128, 16, 2]); b_sil = Buf()
        sil = sb(st, "sil", [128, 16, 2], BF16)
        LD(sil0[:], cvecT[:, :, :], [b_sil])
        ACT(sil[:], sil0[:], AF.Silu, [b_sil], [b_sil])
        for l in range(NL):
            ps, pb = PS()
            for sbk in range(12):
                wt, wb = war.next()
                LD(wt[:], w_ada[l, :, sbk * 512:(sbk + 1) * 512].rearrange("(k p) n -> p k n", p=128), [wb], q="pool")
                for j4 in range(4):
                    j = sbk * 4 + j4
                    for kc in range(16):
                        MM(ps[:, 2 * j:2 * j + 2], wt[:, kc, j4 * 128:(j4 + 1) * 128], sil[:, kc, :],
                           kc == 0, kc == 15, [wb, b_sil], pb)
            for c in range(2):
                TT(adaT[l][:, :, c], ps[:, c:96:2], small[l]["ba"][:], ALU.add, [pb, b_small], [b_ada])
                STT(Gm[l][:, :, c], adaT[l][:, 16:32, c], 1.0, small[l]["ng"][:], ALU.add, ALU.mult,
                    [b_ada, b_small], [b_ada])
        S.fence()

    with ExitStack() as st:
        zpad = sb(st, "zpad", [128, 1024], BF16); bzp = Buf()
        S.op("dve", lambda e: e.memset(zpad[:], 0.0), writes=[bzp])
        STo(vtok[NT:NT + 128, :], zpad[:], [bzp])
        S.fence()

    def x_src(l):
        return xin if l == 0 else xT

    def x_dst(l):
        return y_out if l == NL_RUN - 1 else xT

    marks = [('start', dict(S.ninstr))]
    build_program.marks = marks

    def hyena_filter_full(l, L):
        STQ[0] = "sp"
        nch = L // 128
        kf = kf_s[L]
        with ExitStack() as st:
            hb = sb(st, "fhb", [128, nch, 2048], BF16); bhbs = [Buf() for _ in range(nch)]
            recs = sb(st, "frecs", [128, 2048]); brec = Buf()
            with ExitStack() as st1:
                w1 = sb(st1, "fw1", [33, 64]); w2 = sb(st1, "fw2", [64, 64]); w3 = sb(st1, "fw3", [64, 2048])
                zf = sb(st1, "fzf", [33, L])
                b1 = sb(st1, "fb1", [64, 1]); b2 = sb(st1, "fb2", [64, 1]); fq = sb(st1, "ffq", [64, 1])
                h1 = sb(st1, "fh1", [64, L]); h2 = sb(st1, "fh2", [64, L + 1])
                tmp = Ring(st1, "ftmp", [64, 512], F32, 2)
                tmp2 = Ring(st1, "ftmp2", [64, 512], F32, 2)
                bw = Buf(); bh1 = Buf(); bh2 = Buf()
                LD(w1[:], f_w1[l], [bw]); LD(w2[:], f_w2[l], [bw]); LD(w3[:], f_w3[l], [bw])
                LD(zf[:], cd["zfT%d" % L][:, :], [bw])
                LD(b1[:], f_b1T[l], [bw]); LD(b2[:], f_b2T[l], [bw]); LD(fq[:], f_freqT[l], [bw])

                def sin_layer(w, bcol, src, bsrc, K, dst, bdst):
                    for c0 in range(0, L, 512):
                        n = min(512, L - c0)
                        ps, pb = PS()
                        MM(ps[:64, :n], w[:K, :], src[:K, c0:c0 + n], True, True, [bw, bsrc], pb)
                        a, ab = tmp.next()
                        TS(a[:, :n], ps[:64, :n], bcol[:, 0:1], fq[:, 0:1], ALU.add, ALU.mult, [pb, bw], [ab])
                        m, mb = tmp2.next()
                        TS(m[:, :n], a[:, :n], float(np.pi), float(-2 * np.pi), ALU.is_gt, ALU.mult, [ab], [mb])
                        TT(a[:, :n], a[:, :n], m[:, :n], ALU.add, [ab, mb], [ab])
                        TS(m[:, :n], a[:, :n], float(-np.pi), float(2 * np.pi), ALU.is_lt, ALU.mult, [ab], [mb])
                        TT(a[:, :n], a[:, :n], m[:, :n], ALU.add, [ab, mb], [ab])
                        TS(a[:, :n], a[:, :n], float(np.pi), float(-np.pi), ALU.min, ALU.max, [ab], [ab])
                        ACT(dst[:, c0:c0 + n], a[:, :n], AF.Sin, [ab], [bdst])

                S.op("dve", lambda e: e.memset(h2[:, L:L + 1], 0.0), writes=[bh2])
                sin_layer(w1, b1, zf, bw, 33, h1, bh1)
                sin_layer(w2, b2, h1, bh1, 64, h2, bh2)
                dsr = Ring(st1, "fdsh", [128, 512], F32, 2)
                dec = Ring(st1, "fdec", [128, 512], F32, 2)
                hdr = Ring(st1, "fhd", [128, 2048], F32, 2)
                habs = Ring(st1, "fhabs", [128, 2048], BF16, 3)
                sps, sheld = PS_hold(4)
                fpend = []
                for dc in range(nch):
                    dt_, db_ = dec.next()
                    LD(dt_[:], cd["decay%d" % L][dc * 128:(dc + 1) * 128, :], [db_])
                    ht, hb_ = hdr.next()
                    for cs in range(4):
                        ps, pb = PS()
                        MM(ps[:, :], h2[:, dc * 128:(dc + 1) * 128], w3[:, cs * 512:(cs + 1) * 512], True, True, [bh2, bw], pb)
                        TT(ht[:, cs * 512:(cs + 1) * 512], ps[:, :], dt_[:], ALU.mult, [pb, db_], [hb_])
                    for o in range(2):
                        CP(hb[:, dc, o * 1024:o * 1024 + 512], ht[:, o * 1024:o * 1024 + 512], [hb_], [bhbs[dc]], eng="pool")
                    ds_, dsb = dsr.next()
                    LD(ds_[:], cd["decay%d" % L][dc * 128 + 1:(dc + 1) * 128 + 1, :], [dsb])
                    for o in range(2):
                        ps, pb = PS()
                        MM(ps[:, :], h2[:, dc * 128 + 1:(dc + 1) * 128 + 1], w3[:, o * 1024 + 512:o * 1024 + 1024], True, True, [bh2, bw], pb)
                        TT(hb[:, dc, o * 1024 + 512:o * 1024 + 1024], ps[:, :], ds_[:], ALU.mult, [pb, dsb], [bhbs[dc]])
                    at, ab_ = habs.next()
                    ACT(at[:], ht[:], AF.Abs, [hb_], [ab_])
                    if fpend:
                        fpend.pop(0)()

                    def ones_mm(at=at, ab_=ab_, dc=dc):
                        for cs in range(4):
                            MM(sps[cs][0][:, :], onesb[:], at[:, cs * 512:(cs + 1) * 512], dc == 0, dc == nch - 1,
                               [b_const, ab_], sps[cs][1])
                    fpend.append(ones_mm)
                while fpend:
                    fpend.pop(0)()
                for cs in range(4):
                    TS(recs[:, cs * 512:(cs + 1) * 512], sps[cs][0][:, :], EPS, None, ALU.add, None, [sps[cs][1]], [brec])
                S.op("dve", lambda e: e.reciprocal(out=recs[:], in_=recs[:]), reads=[brec], writes=[brec])
                nt = Ring(st1, "fnt", [128, 512], F32, 4)
                for dc in range(nch):
                    for o in range(2):
                        cF = slice(o * 1024, o * 1024 + 512); cB = slice(o * 1024 + 512, o * 1024 + 1024)
                        t1, t1b = nt.next(); t2, t2b = nt.next()
                        TT(t1[:], hb[:, dc, cF], recs[:, cF], ALU.mult, [bhbs[dc], brec], [t1b])
                        TT(t2[:], hb[:, dc, cB], recs[:, cB], ALU.mult, [bhbs[dc], brec], [t2b])
                        TT(hb[:, dc, cF], t1[:], t2[:], ALU.add, [t1b, t2b], [bhbs[dc]])
                        TT(hb[:, dc, cB], t1[:], t2[:], ALU.subtract, [t1b, t2b], [bhbs[dc]], eng="pool")
                PS_release(sheld)
                S.fence()
            with ExitStack() as st2:
                phi = sb(st2, "fphi", [128, nch, 2]); bphi = Buf()
                LD(phi[:], cd["phi%d" % L][:, :, :], [bphi])
                nsl = min(512, L)
                cr = Ring(st2, "fC", [128, nch, nsl], BF16, 2)
                sr = Ring(st2, "fS", [128, nch, nsl], BF16, 2)
                ko = Ring(st2, "fko", [128, 2, 1024], F32, 2)
                tr_ = Ring(st2, "ft", [128, 512], F32, 8)
                for f0 in range(0, L, nsl):
                    ct, cb_ = cr.next(); st_, sb__ = sr.next()
                    LD(ct[:], cd["hyC%d" % L][:, f0:f0 + nsl].rearrange("(k p) f -> p k f", p=128), [cb_])
                    LD(st_[:], cd["hyS%d" % L][:, f0:f0 + nsl].rearrange("(k p) f -> p k f", p=128), [sb__])
                    for fs in range(nsl // 128):
                        fc = f0 // 128 + fs
                        fsl = slice(fs * 128, (fs + 1) * 128)
                        cph = phi[:, fc, 0:1]; sph = phi[:, fc, 1:2]
                        kt, kb_ = ko.next()
                        for o in range(2):
                            cF = slice(o * 1024, o * 1024 + 512); cB = slice(o * 1024 + 512, o * 1024 + 1024)
                            psA, pbA = PS(); psB, pbB = PS()
                            for dc in range(nch):
                                MM(psA[:, :], ct[:, dc, fsl], hb[:, dc, cF], dc == 0, dc == nch - 1, [cb_, bhbs[dc]], pbA)
                            for dc in range(nch):
                                MM(psB[:, :], st_[:, dc, fsl], hb[:, dc, cB], dc == 0, dc == nch - 1, [sb__, bhbs[dc]], pbB)
                            t1, t1b = tr_.next(); t2, t2b = tr_.next()
                            TS(t1[:], psB[:, :], sph, None, ALU.mult, None, [pbB, bphi], [t1b])
                            STT(kt[:, 0, o * 512:(o + 1) * 512], psA[:, :], cph, t1[:], ALU.mult, ALU.add, [pbA, bphi, t1b], [kb_])
                            TS(t2[:], psB[:, :], cph, None, ALU.mult, None, [pbB, bphi], [t2b])
                            STT(kt[:, 1, o * 512:(o + 1) * 512], psA[:, :], sph, t2[:], ALU.mult, ALU.subtract, [pbA, bphi, t2b], [kb_])
                        for ri in range(2):
                            STo(kf[ri, :, fc * 128:(fc + 1) * 128, :].rearrange("o f c -> f o c"),
                                kt[:, ri, :].rearrange("p (o c) -> p o c", o=2), [kb_])
                S.fence()

    def pass1(l):
        STQ[0] = "sp"
        sm = small[l]
        with ExitStack() as st:
            hTs = [sb(st, "hT", [128, 16, 1024], BF16) for _ in range(2)]
            b_hTs = [Buf(), Buf()]
            wr = Ring(st, "wsb", [128, 16, 512], BF16, 2)
            xc = Ring(st, "xc", [128, 16, 512], F32, 2)
            sqr = Ring(st, "sq", [128, 512], BF16, 3)
            sqf = Ring(st, "sqf", [128, 512], F32, 2)
            accr = Ring(st, "acc", [128, 512], F32, 2)
            rsr = Ring(st, "rs", [128, 512], F32, 3)
            tmr = Ring(st, "tm", [128, 512], F32, 2)
            ef = Ring(st, "ef", [128, 512], F32, 4)
            eb = Ring(st, "eb", [128, 512], BF16, 4)
            sm4 = Ring(st, "sm4", [128, 8], F32, 2)
            loaded = {}

            def prep_load(blk, tc):
                xt_, xb_ = xc.next()
                t0 = blk * 1024
                LD(xt_[:], x_src(l)[:, t0 + tc * 512:t0 + (tc + 1) * 512].rearrange("(k p) t -> p k t", p=128), [xb_])
                loaded[(blk, tc)] = (xt_, xb_)

            def prep_compute(blk, tc):
                cond = 0 if blk == 0 else 1
                hT = hTs[blk % 2]; b_hT = b_hTs[blk % 2]
                xt_, xb_ = loaded.pop((blk, tc))
                psr, pbr = PS()
                if blk == 0:
                    for kc in range(16):
                        sq, sqb = sqr.next()
                        ACT(sq[:], xt_[:, kc, :], AF.Square, [xb_], [sqb])
                        MM(psr[:, :], onesb[:], sq[:], kc == 0, kc == 15, [b_const, sqb], pbr)
                else:
                    acc, accb = accr.next()
                    for kc in range(16):
                        if kc == 0:
                            TT(acc[:], xt_[:, 0, :], xt_[:, 0, :], ALU.mult, [xb_], [accb], eng="pool")
                        else:
                            sq, sqb = sqf.next()
                            TT(sq[:], xt_[:, kc, :], xt_[:, kc, :], ALU.mult, [xb_], [sqb], eng="pool")
                            TT(acc[:], acc[:], sq[:], ALU.add, [accb, sqb], [accb], eng="pool")
                    MM(psr[:, :], onesf[:], acc[:], True, True, [b_const, accb], pbr)
                rs, rsb = rsr.next()
                rstd_from(psr[:, :], rs[:], 1.0 / DM, [pbr], [rsb])
                for kc in range(16):
                    tm, tmb = tmr.next()
                    TT(tm[:], xt_[:, kc, :], rs[:], ALU.mult, [xb_, rsb], [tmb])
                    ACT(hT[:, kc, tc * 512:(tc + 1) * 512], tm[:], AF.Identity, [tmb, b_ada], [b_hT],
                        scale=Gm[l][:, kc, cond:cond + 1], bias=adaT[l][:, kc, cond:cond + 1])

            pending = []

            def flush():
                while pending:
                    pending.pop(0)()

            wstream = WStream(wr, [w_in[l, :, sbk * 512:(sbk + 1) * 512] for blk in range(3) for sbk in range(26)])
            prep_load(0, 0); prep_load(0, 1); prep_compute(0, 0); prep_compute(0, 1)
            for blk in range(3):
                t0 = blk * 1024
                hT = hTs[blk % 2]; b_hT = b_hTs[blk % 2]
                for sbk in range(26):
                    if blk + 1 < 3:
                        if sbk == 1:
                            prep_load(blk + 1, 0)
                        if sbk == 6:
                            prep_compute(blk + 1, 0)
                        if sbk == 8:
                            prep_load(blk + 1, 1)
                        if sbk == 14:
                            prep_compute(blk + 1, 1)
                    wt, wb = wstream.get()
                    fm = sbk not in (4, 5)
                    if fm:
                        for cb4 in range(4):
                            for tc in range(2):
                                ps, pb = PS()
                                for kc in range(16):
                                    MM(ps[:, :], wt[:, kc, cb4 * 128:(cb4 + 1) * 128], hT[:, kc, tc * 512:(tc + 1) * 512],
                                       kc == 0, kc == 15, [wb, b_hT], pb)
                                tsl = slice(t0 + tc * 512, t0 + (tc + 1) * 512)
                                if sbk < 4:
                                    raw, rb = ef.next()
                                    ACT(raw[:], ps[:, :], AF.Identity, [pb], [rb])
                                    sq, sqb = sqr.next()
                                    TT(sq[:], raw[:], raw[:], ALU.mult, [rb], [sqb])
                                    flush()

                                    def partB(raw=raw, rb=rb, sq=sq, sqb=sqb, sbk=sbk, cb4=cb4, tsl=tsl):
                                        ps2, pb2 = PS()
                                        MM(ps2[:, :], onesb[:], sq[:], True, True, [b_const, sqb], pb2)
                                        rs, rsb = rsr.next()
                                        rstd_from(ps2[:, :], rs[:], 1.0 / 128, [pb2], [rsb])
                                        o, ob = eb.next()
                                        gcol = sm["qg"] if sbk < 2 else sm["kg"]
                                        STT(o[:], raw[:], gcol[:, 0:1], rs[:], ALU.mult, ALU.mult, [rb, rsb, b_small], [ob])
                                        dst = qT if sbk < 2 else kT
                                        r0 = (sbk % 2) * 512 + cb4 * 128
                                        STo(dst[r0:r0 + 128, tsl], o[:], [ob])
                                    pending.append(partB)
                                    continue
                                flush()
                                if sbk in (6, 7, 9, 13):
                                    o, ob = eb.next()
                                    ACT(o[:], ps[:, :], AF.Silu, [pb], [ob])
                                    r0 = {6: 0, 7: 512, 9: 1024, 13: 1536}[sbk] + cb4 * 128
                                    STo(gT[r0:r0 + 128, tsl], o[:], [ob])
                                elif sbk == 8:
                                    o, ob = eb.next()
                                    CP(o[:], ps[:, :], [pb], [ob])
                                    STo(ubT[cb4 * 128:(cb4 + 1) * 128, tsl], o[:], [ob])
                                elif sbk in (10, 11, 12):
                                    o, ob = ef.next()
                                    CP(o[:], ps[:, :], [pb], [ob])
                                    r0 = (sbk - 10) * 512 + cb4 * 128
                                    STo(hcT[r0:r0 + 128, tsl], o[:], [ob])
                                else:
                                    o, ob = eb.next()
                                    ACT(o[:], ps[:, :], AF.Sigmoid, [pb], [ob])
                                    r0 = (sbk - 14) * 512 + cb4 * 128
                                    STo(mgT[r0:r0 + 128, tsl], o[:], [ob])
                    if sbk in (4, 5) or (sbk in (2, 3) and blk == 0):
                        for tt in range(8):
                            ps, pb = PS()
                            for kc in range(16):
                                MM(ps[:, :], hT[:, kc, tt * 128:(tt + 1) * 128], wt[:, kc, :], kc == 0, kc == 15, [wb, b_hT], pb)
                            flush()
                            raw, rb = ef.next()
                            ACT(raw[:], ps[:, :], AF.Identity, [pb], [rb])
                            tok0 = t0 + tt * 128
                            if sbk in (4, 5):
                                c0 = (sbk - 4) * 512
                                if blk == 0:
                                    STo(newv[l, tok0:tok0 + 128, c0:c0 + 512], raw[:], [rb], is_output=True)
                                o, ob = eb.next()
                                CP(o[:], raw[:], [rb], [ob])
                                STo(vtok[tok0:tok0 + 128, c0:c0 + 512], o[:], [ob])
                            else:
                                c0 = (sbk - 2) * 512
                                sq, sqb = ef.next()
                                TT(sq[:], raw[:], raw[:], ALU.mult, [rb], [sqb])
                                s4, s4b = sm4.next()
                                RED(s4[:, 0:4], sq[:].rearrange("p (h d) -> p h d", h=4), [sqb], [s4b])
                                rstd_from(s4[:, 0:4], s4[:, 4:8], 1.0 / 128, [s4b], [s4b])
                                for h in range(4):
                                    STT(sq[:, h * 128:(h + 1) * 128], raw[:, h * 128:(h + 1) * 128], s4[:, 4 + h:5 + h],
                                        sm["kgr"][:], ALU.mult, ALU.mult, [rb, s4b, b_small, sqb], [sqb])
                                STo(newk[l, tok0:tok0 + 128, c0:c0 + 512], sq[:], [sqb], is_output=True)
                flush()
            S.fence()

    def attention(l):
        sm = small[l]
        STQ[0] = "pool"
        with ExitStack() as st:
            tr_ = Ring(st, "abt", [128, 14 * 64], F32, 2)
            mk = sb(st, "amask", [128, 14 * 64]); bmk = Buf()
            LD(mk[:], cd["nmask"].rearrange("p a c -> p (a c)"), [bmk])
            for h in range(8):
                t, tb = tr_.next()
                LD(t[:], rpbT2[l, h], [tb])
                ACT(t[:], t[:], AF.Exp, [tb], [tb])
                TT(t[:], t[:], mk[:], ALU.mult, [tb, bmk], [tb])
                STo(ebs[h], t[:], [tb])
            S.fence()
        with ExitStack() as st:
            qr = Ring(st, "aq", [128, 2048], BF16, 3)
            kr = Ring(st, "ak", [128, 2048], BF16, 3)
            v0r = Ring(st, "av0", [128, 16, 128], BF16, 3)
            v1r = Ring(st, "av1", [128, 16, 128], BF16, 2)
            ckr = Ring(st, "ack", [128, 4, 128], F32, 2)
            cktr = Ring(st, "ackT", [128, 512], BF16, 2)
            cvr = Ring(st, "acv", [128, 4, 128], BF16, 2)
            ebr = Ring(st, "aeb", [128, 14, 64], F32, 2)
            pcr = Ring(st, "apc", [128, 4, 512], BF16, 2)
            pwr = Ring(st, "apw", [128, 8, 64], BF16, 6)
            gar = Ring(st, "aga", [128, 512], BF16, 3)
            rcr = Ring(st, "arc", [128, 512], F32, 2)
            t2r = Ring(st, "at2", [128, 512], F32, 2)
            yor = Ring(st, "ayo", [128, 512], BF16, 2)
            chr_ = Ring(st, "yh", [128, NT], F32, 2)
            cor_ = Ring(st, "yo", [128, NT], F32, 2)
            segs = [(s_ * 256, 256) for s_ in range(4)] + [(1024, 2048)]
            conv_todo = list(range(12))
            wcr = Ring(st, "awc", [128, 16, 512], BF16, 2)
            wc_todo = [(w, sbk) for w in (w_br, w_out) for sbk in range(4)]
            wc_pend = []

            def wcast_unit():
                while wc_pend:
                    wc_pend.pop(0)()
                if not wc_todo:
                    return
                w, sbk = wc_todo.pop(0)
                idx = 7 - len(wc_todo)
                wt, wb = wcr.next()
                LD(wt[:], w[l, :, sbk * 512:(sbk + 1) * 512].rearrange("(k p) n -> p k n", p=128), [wb], q="pool")
                wc_pend.append(lambda: S.dma("pool", lambda e: e.dma_start(out=w16[idx], in_=wt[:].rearrange("p k n -> p (k n)")),
                                             reads=[wb], writes=[]))

            def conv_unit():
                if not conv_todo:
                    return
                cb = conv_todo.pop(0)
                ht, hb_ = chr_.next()
                LD(ht[:], hcT[cb * 128:(cb + 1) * 128, :], [hb_])
                ot, ob = cor_.next()
                ACT(ot[:], ht[:], AF.Identity, [hb_, b_small], [ob], scale=sm["cw"][:, cb, 1:2], bias=sm["cb"][:, cb:cb + 1])
                for (t0, L) in segs:
                    STT(ot[:, t0 + 1:t0 + L], ht[:, t0:t0 + L - 1], sm["cw"][:, cb, 0:1], ot[:, t0 + 1:t0 + L], ALU.mult, ALU.add, [hb_, b_small, ob], [ob])
                    STT(ot[:, t0:t0 + L - 1], ht[:, t0 + 1:t0 + L], sm["cw"][:, cb, 2:3], ot[:, t0:t0 + L - 1], ALU.mult, ALU.add, [hb_, b_small, ob], [ob])
                STo(hccT[cb * 128:(cb + 1) * 128, :], ot[:], [ob])

            def epilogue(psO, pbO, psD, pbD, n, grow, tsl):
                rc, rcb = rcr.next()
                S.op("dve", lambda e: e.reciprocal(out=rc[:, :n], in_=psD[:, :n]), reads=[pbD], writes=[rcb])
                ga, gab = gar.next()
                LD(ga[:, :n], gT[grow:grow + 128, tsl], [gab])
                t2, t2b = t2r.next()
                TT(t2[:, :n], psO[:, :n], rc[:, :n], ALU.mult, [pbO, rcb], [t2b])
                yo, yob = yor.next()
                TT(yo[:, :n], t2[:, :n], ga[:, :n], ALU.mult, [t2b, gab], [yob])
                STo(yT[grow:grow + 128, tsl], yo[:, :n], [yob])

            pend = []
            for s in range(4):
                for h in range(8):
                    t0 = s * 256
                    qt, qb = qr.next(); kt, kb = kr.next(); vt, vb = v0r.next()
                    LD(qt[:, 0:256], qT[h * 128:(h + 1) * 128, t0:t0 + 256], [qb])
                    LD(kt[:, 0:256], kT[h * 128:(h + 1) * 128, t0:t0 + 256], [kb])
                    LD(vt[:, 0:2, :], vtok[t0:t0 + 256, h * 128:(h + 1) * 128].rearrange("(k p) d -> p k d", p=128), [vb])
                    pc, pcb = pcr.next()
                    ps, pb = PS()
                    for kc in range(2):
                        MM(ps[:, kc * 256:(kc + 1) * 256], kt[:, kc * 128:(kc + 1) * 128], qt[:, 0:256], True, True, [kb, qb], pb)
                    ACT(pc[:, 0, :], ps[:, :], AF.Exp, [pb], [pcb])
                    while pend:
                        pend.pop(0)()

                    def ph2(pc=pc, pcb=pcb, vt=vt, vb=vb, h=h, t0=t0):
                        psD, pbD = PS(); psO, pbO = PS()
                        for kc in range(2):
                            MM(psD[:, 0:256], onesb[:], pc[:, 0, kc * 256:(kc + 1) * 256], kc == 0, kc == 1, [b_const, pcb], pbD)
                        for kc in range(2):
                            MM(psO[:, 0:256], vt[:, kc, :], pc[:, 0, kc * 256:(kc + 1) * 256], kc == 0, kc == 1, [vb, pcb], pbO)
                        epilogue(psO, pbO, psD, pbD, 256, h * 128, slice(t0, t0 + 256))
                    pend.append(ph2)
            while pend:
                pend.pop(0)()
            for h in range(8):
                qt, qb = qr.next(); kt, kb = kr.next(); v0, v0b = v0r.next(); v1, v1b = v1r.next()
                LD(qt[:], qT[h * 128:(h + 1) * 128, 1024:3072], [qb])
                LD(kt[:], kT[h * 128:(h + 1) * 128, 1024:3072], [kb])
                LD(v0[:], vtok[1024:3072, h * 128:(h + 1) * 128].rearrange("(k p) d -> p k d", p=128), [v0b])
                LD(v1[:], vtok[1088:3136, h * 128:(h + 1) * 128].rearrange("(k p) d -> p k d", p=128), [v1b])
                ckt, ckb = ckr.next()
                LD(ckt[:], ck[l, :, h * 128:(h + 1) * 128].rearrange("(k p) d -> p k d", p=128), [ckb])
                cvt, cvb = cvr.next()
                LD(cvt[:], cv[l, :, h * 128:(h + 1) * 128].rearrange("(k p) d -> p k d", p=128), [cvb], q="pool")
                ebt, ebb = ebr.next()
                LD(ebt[:], ebs[h].rearrange("p (a c) -> p a c", a=14), [ebb])
                ps, pb = PS()
                for kc in range(4):
                    TR(ps[:, kc * 128:(kc + 1) * 128], ckt[:, kc, :], ident[:], [ckb], pb)
                cT, cTb = cktr.next()
                CP(cT[:], ps[:, :], [pb], [cTb])
                for g in range(4):
                    qs = slice(g * 512, (g + 1) * 512)
                    pc, pcb = pcr.next()
                    for kc in range(4):
                        ps, pb = PS()
                        MM(ps[:, :], cT[:, kc * 128:(kc + 1) * 128], qt[:, qs], True, True, [cTb, qb], pb)
                        ACT(pc[:, kc, :], ps[:, :], AF.Exp, [pb], [pcb])
                    (hp, held) = PS_hold(2)
                    (psD, pbD), (psO, pbO) = hp
                    for kc in range(4):
                        MM(psD[:, :], onesb[:], pc[:, kc, :], kc == 0, False, [b_const, pcb], pbD, skip=True)
                    for kc in range(4):
                        MM(psO[:, :], cvt[:, kc, :], pc[:, kc, :], kc == 0, False, [cvb, pcb], pbO, skip=True)
                    rowinfo = []
                    for pr in range(4):
                        psS, pbS = PS()
                        pw, pwb = pwr.next()
                        for half in range(2):
                            r = g * 8 + pr * 2 + half
                            rs_ = min(max(r - 4, 0), 24)
                            q64 = slice(r * 64, (r + 1) * 64)
                            for j in range(4):
                                k0 = (rs_ + 2 * j) * 64
                                c0 = half * 256 + j * 64
                                MM(psS[:, c0:c0 + 64], kt[:, k0:k0 + 128], qt[:, q64], True, True, [kb, qb], pbS)
                        ACT(pw[:].rearrange("p a c -> p (a c)"), psS[:, :], AF.Exp, [pbS], [pwb])
                        for half in range(2):
                            r = g * 8 + pr * 2 + half
                            rs_ = min(max(r - 4, 0), 24)
                            dr0 = rs_ - r + 7
                            TT(pw[:, half * 4:(half + 1) * 4, :], pw[:, half * 4:(half + 1) * 4, :], ebt[:, dr0:dr0 + 7:2, :],
                               ALU.mult, [pwb, ebb], [pwb])
                            rowinfo.append((pr * 2 + half, rs_, pw, pwb, half))
                    for (rr, rs_, pw, pwb, half) in rowinfo:
                        o64 = slice(rr * 64, (rr + 1) * 64)
                        last = rr == 7
                        for j in range(4):
                            MM(psD[:, o64], onesb[:], pw[:, half * 4 + j, :], False, last and j == 3, [b_const, pwb], pbD, skip=True)
                        for j in range(4):
                            row0 = rs_ + 2 * j
                            if row0 % 2 == 0:
                                vap = v0[:, row0 // 2, :]; vbb = v0b
                            else:
                                vap = v1[:, (row0 - 1) // 2, :]; vbb = v1b
                            MM(psO[:, o64], vap, pw[:, half * 4 + j, :], False, last and j == 3, [vbb, pwb], pbO, skip=True)
                    epilogue(psO, pbO, psD, pbD, 512, h * 128, slice(1024 + g * 512, 1024 + (g + 1) * 512))
                    PS_release(held)
                    if g % 2 == 1:
                        conv_unit()
                    else:
                        wcast_unit()
            while conv_todo:
                conv_unit()
            while wc_todo or wc_pend:
                wcast_unit()
            S.fence()
        STQ[0] = "sp"

    def fnet(l):
        STQ[0] = "pool"
        with ExitStack() as st:
            cs_ = sb(st, "ncs", [128, 256], BF16); bcs = Buf()
            LD(cs_[:], cd["fnCS"][:, :], [bcs])
            for (L, seqs) in ((256, [(s * 256) for s in range(4)]), (2048, [1024])):
                nch = L // 128
                with ExitStack() as st1:
                    ur = Ring(st1, "nu", [128, L], BF16, 2)
                    P12 = sb(st1, "nP", [128, 4, nch, 256], BF16); bP = Buf()
                    nsl = min(512, L)
                    clr = Ring(st1, "ncl", [128, nch, nsl], BF16, 2)
                    slr = Ring(st1, "nsl", [128, nch, nsl], BF16, 2)
                    gbr = Ring(st1, "ngb", [128, 512], BF16, 2)
                    yor = Ring(st1, "nyo", [128, 512], BF16, 2)
                    for t0 in seqs:
                        for g in range(4):
                            ut, ub_ = ur.next()
                            LD(ut[:], ubT[g * 128:(g + 1) * 128, t0:t0 + L], [ub_])
                            for lc in range(nch):
                                ps, pb = PS()
                                MM(ps[:, 0:256], ut[:, lc * 128:(lc + 1) * 128], cs_[:], True, True, [ub_, bcs], pb)
                                if lc % 2 == 0:
                                    CP(P12[:, g, lc, :], ps[:, 0:256], [pb], [bP])
                                else:
                                    ACT(P12[:, g, lc, :], ps[:, 0:256], AF.Identity, [pb], [bP])
                        for c0 in range(0, L, nsl):
                            ct, cb_ = clr.next(); st_, sb__ = slr.next()
                            LD(ct[:], cd["fnC%d" % L][:, c0:c0 + nsl].rearrange("(k p) n -> p k n", p=128), [cb_])
                            LD(st_[:], cd["fnSn%d" % L][:, c0:c0 + nsl].rearrange("(k p) n -> p k n", p=128), [sb__])
                            for g in range(4):
                                ps, pb = PS()
                                for lc in range(nch):
                                    MM(ps[:, :nsl], P12[:, g, lc, 0:128], ct[:, lc, :], lc == 0, False, [bP, cb_], pb)
                                for lc in range(nch):
                                    MM(ps[:, :nsl], P12[:, g, lc, 128:256], st_[:, lc, :], False, lc == nch - 1, [bP, sb__], pb)
                                gb, gbb = gbr.next()
                                tsl = slice(t0 + c0, t0 + c0 + nsl)
                                LD(gb[:, :nsl], gT[1024 + g * 128:1024 + (g + 1) * 128, tsl], [gbb])
                                yo, yob = yor.next()
                                TT(yo[:, :nsl], ps[:, :nsl], gb[:, :nsl], ALU.mult, [pb, gbb], [yob])
                                STo(yT[1024 + g * 128:1024 + (g + 1) * 128, tsl], yo[:, :nsl], [yob])
                    S.fence()
            S.fence()

    def hyena(l):
        sm = small[l]
        STQ[0] = "pool"
        for (L, seqs) in ((256, [s * 256 for s in range(4)]), (2048, [1024])):
            nch = L // 128
            nsl = min(512, L)
            kf = kf_s[L]
            with ExitStack() as st:
                nbuf = 4 if L == 256 else 1
                ztr = Ring(st, "yz", [128, nch, 512], BF16, nbuf)
                yfr = Ring(st, "yY", [128, nch, 2, 512], BF16, nbuf)
                zin = Ring(st, "yzin", [128, L], F32, nbuf)
                zbf = Ring(st, "yzbf", [128, L], BF16, nbuf)
                cr = Ring(st, "yC", [128, nch, nsl], BF16, 2)
                sr = Ring(st, "yS", [128, nch, nsl], BF16, 2)
                kr_ = Ring(st, "yk", [128, 2, 512], F32, 2)
                t1 = Ring(st, "yt1", [128, 512], F32, 8)
                xr_ = Ring(st, "yx", [128, 512], F32, 6)
                o32 = Ring(st, "yo32", [128, 512], F32, 2)
                o16 = Ring(st, "yo16", [128, 512], BF16, 4)

                cs_cache = {}
                kf_cache = {}
                kfr_small = Ring(st, "ykc", [128, 2, 512], F32, 4) if L == 256 else None

                def get_cs(f0):
                    if L == 256 and f0 in cs_cache:
                        return cs_cache[f0]
                    ct, cb_ = cr.next(); st_, sb__ = sr.next()
                    LD(ct[:], cd["hyC%d" % L][:, f0:f0 + nsl].rearrange("(k p) n -> p k n", p=128), [cb_])
                    LD(st_[:], cd["hyS%d" % L][:, f0:f0 + nsl].rearrange("(k p) n -> p k n", p=128), [sb__])
                    cs_cache[f0] = (ct, cb_, st_, sb__)
                    return cs_cache[f0]

                def get_kf(o, fc):
                    if L == 256 and (o, fc) in kf_cache:
                        return kf_cache[(o, fc)]
                    kt, kb_ = (kfr_small if L == 256 else kr_).next()
                    LD(kt[:, 0, :], kf[0, o, fc * 128:(fc + 1) * 128, :], [kb_])
                    LD(kt[:, 1, :], kf[1, o, fc * 128:(fc + 1) * 128, :], [kb_])
                    kf_cache[(o, fc)] = (kt, kb_)
                    return kf_cache[(o, fc)]

                def load_ztok(src, t0, ztok, bz):
                    for cb in range(4):
                        zt, zb_ = zin.next()
                        LD(zt[:], src[cb * 128:(cb + 1) * 128, t0:t0 + L], [zb_])
                        zh, zhb = zbf.next()
                        CP(zh[:], zt[:], [zb_], [zhb])
                        for tc0 in range(0, nch, 4):
                            nb = min(4, nch - tc0)
                            ps, pb = PS()
                            psv = ps[:, :].bitcast(BF16)
                            for j in range(nb):
                                TR(psv[:, j * 128:(j + 1) * 128], zh[:, (tc0 + j) * 128:(tc0 + j + 1) * 128], identb[:], [zhb], pb)
                            CP(ztok[:, tc0:tc0 + nb, cb * 128:(cb + 1) * 128],
                               psv[:, 0:nb * 128].rearrange("p (j c) -> p j c", j=nb), [pb], [bz])

                for o in range(2):
                    units = []
                    for t0 in seqs:
                        ztok, bz = ztr.next()
                        Yf, bY = yfr.next()
                        load_ztok(hccT if o == 0 else z1T, t0, ztok, bz)
                        units.append((t0, ztok, bz, Yf, bY))
                    for (t0, ztok, bz, Yf, bY) in units:
                        for f0 in range(0, L, nsl):
                            ct, cb_, st_, sb__ = get_cs(f0)
                            for fs in range(nsl // 128):
                                fc = f0 // 128 + fs
                                kt, kb_ = get_kf(o, fc)
                                psA, pbA = PS(); psB, pbB = PS()
                                for tc in range(nch):
                                    MM(psA[:, :], ct[:, tc, fs * 128:(fs + 1) * 128], ztok[:, tc, :], tc == 0, tc == nch - 1, [cb_, bz], pbA)
                                for tc in range(nch):
                                    MM(psB[:, :], st_[:, tc, fs * 128:(fs + 1) * 128], ztok[:, tc, :], tc == 0, tc == nch - 1, [sb__, bz], pbB)
                                a, ab = t1.next(); b, bb = t1.next()
                                TT(a[:], psA[:, :], kt[:, 0, :], ALU.mult, [pbA, kb_], [ab])
                                TT(b[:], psB[:, :], kt[:, 1, :], ALU.mult, [pbB, kb_], [bb])
                                TT(Yf[:, fc, 0, :], a[:], b[:], ALU.add, [ab, bb], [bY], eng="pool")
                                a2, ab2 = t1.next(); b2, bb2 = t1.next()
                                TT(a2[:], psB[:, :], kt[:, 0, :], ALU.mult, [pbB, kb_], [ab2])
                                TT(b2[:], psA[:, :], kt[:, 1, :], ALU.mult, [pbA, kb_], [bb2])
                                TT(Yf[:, fc, 1, :], a2[:], b2[:], ALU.subtract, [ab2, bb2], [bY], eng="pool")
                    for (t0, ztok, bz, Yf, bY) in units:
                        for c0 in range(0, L, nsl):
                            ct, cb_, st_, sb__ = get_cs(c0)
                            tsl = slice(t0 + c0, t0 + c0 + nsl)
                            for cb in range(4):
                                ps, pb = PS()
                                for fc in range(nch):
                                    MM(ps[:, :nsl], Yf[:, fc, 0, cb * 128:(cb + 1) * 128], ct[:, fc, :], fc == 0, False, [bY, cb_], pb)
                                for fc in range(nch):
                                    MM(ps[:, :nsl], Yf[:, fc, 1, cb * 128:(cb + 1) * 128], st_[:, fc, :], False, fc == nch - 1, [bY, sb__], pb)
                                zi, zib = xr_.next(); xg, xgb = xr_.next()
                                if o == 0:
                                    LD(zi[:, :nsl], hccT[cb * 128:(cb + 1) * 128, tsl], [zib])
                                    LD(xg[:, :nsl], hccT[512 + cb * 128:512 + (cb + 1) * 128, tsl], [xgb])
                                else:
                                    LD(zi[:, :nsl], z1T[cb * 128:(cb + 1) * 128, tsl], [zib])
                                    LD(xg[:, :nsl], hccT[1024 + cb * 128:1024 + (cb + 1) * 128, tsl], [xgb])
                                r, rb = o32.next()
                                STT(r[:, :nsl], zi[:, :nsl], sm["hb"][:, o, cb:cb + 1], ps[:, :nsl], ALU.mult, ALU.add, [zib, b_small, pb], [rb])
                                TT(r[:, :nsl], r[:, :nsl], xg[:, :nsl], ALU.mult, [rb, xgb], [rb])
                                if o == 0:
                                    STo(z1T[cb * 128:(cb + 1) * 128, tsl], r[:, :nsl], [rb])
                                else:
                                    gc, gcb = o16.next()
                                    LD(gc[:, :nsl], gT[1536 + cb * 128:1536 + (cb + 1) * 128, tsl], [gcb])
                                    yo, yob = o16.next()
                                    TT(yo[:, :nsl], r[:, :nsl], gc[:, :nsl], ALU.mult, [rb, gcb], [yob])
                                    STo(yT[1536 + cb * 128:1536 + (cb + 1) * 128, tsl], yo[:, :nsl], [yob])
                    S.fence()
                S.fence()

    def pass2(l):
        STQ[0] = "sp"
        with ExitStack() as st:
            ySr = Ring(st, "pY", [128, 16, 1024], BF16, 2)
            mS = sb(st, "pM", [128, 16, 1024], BF16); bMs = Buf()
            ynext = ySr.next()
            LD(ynext[0][:], yT[:, 0:1024].rearrange("(k p) t -> p k t", p=128), [ynext[1]])
            wr = Ring(st, "pw", [128, 16, 512], BF16, 2)
            wstream = WStream(wr, [w16[i] for blk in range(3) for i in range(8)], pre=True, q="act")
            gr = Ring(st, "pg", [128, 3, 512], BF16, 3)
            t1 = Ring(st, "pt", [128, 512], F32, 9)
            xr_ = Ring(st, "px", [128, 512], F32, 2)
            xo = Ring(st, "pxo", [128, 512], F32, 2)
            p2pend = []
            for blk in range(3):
                cond = 0 if blk == 0 else 1
                t0 = blk * 1024
                yS, bYs = ynext
                if blk < 2:
                    ynext = ySr.next()
                    LD(ynext[0][:], yT[:, t0 + 1024:t0 + 2048].rearrange("(k p) t -> p k t", p=128), [ynext[1]])
                for sbk in range(4):
                    wt, wb = wstream.get()
                    for cb4 in range(4):
                        cb = sbk * 4 + cb4
                        for tc in range(2):
                            tsl = slice(t0 + tc * 512, t0 + (tc + 1) * 512)
                            gt, gb_ = gr.next()
                            for i in range(3):
                                LD(gt[:, i, :], mgT[i * 2048 + cb * 128:i * 2048 + (cb + 1) * 128, tsl], [gb_])
                            ts_ = []
                            for i, (k0, k1) in enumerate(((0, 8), (8, 12), (12, 16))):
                                ps, pb = PS()
                                for kc in range(k0, k1):
                                    MM(ps[:, :], wt[:, kc, cb4 * 128:(cb4 + 1) * 128], yS[:, kc, tc * 512:(tc + 1) * 512],
                                       kc == k0, kc == k1 - 1, [wb, bYs], pb)
                                t, tb = t1.next()
                                TT(t[:], ps[:, :], gt[:, i, :], ALU.mult, [pb, gb_], [tb])
                                ts_.append((t, tb))
                            while p2pend:
                                p2pend.pop(0)()

                            def adds(ts_=ts_, cb=cb, tc=tc):
                                (ta, tab), (tb_, tbb), (tc_, tcb) = ts_
                                TT(ta[:], ta[:], tb_[:], ALU.add, [tab, tbb], [tab], eng="pool")
                                TT(mS[:, cb, tc * 512:(tc + 1) * 512], ta[:], tc_[:], ALU.add, [tab, tcb], [bMs], eng="pool")
                            p2pend.append(adds)
                while p2pend:
                    p2pend.pop(0)()
                for sbk in range(4):
                    wt, wb = wstream.get()
                    for cb4 in range(4):
                        cb = sbk * 4 + cb4
                        for tc in range(2):
                            tsl = slice(t0 + tc * 512, t0 + (tc + 1) * 512)
                            ps, pb = PS()
                            for kc in range(16):
                                MM(ps[:, :], wt[:, kc, cb4 * 128:(cb4 + 1) * 128], mS[:, kc, tc * 512:(tc + 1) * 512],
                                   kc == 0, kc == 15, [wb, bMs], pb)
                            xt_, xb_ = xr_.next()
                            LD(xt_[:], x_src(l)[cb * 128:(cb + 1) * 128, tsl], [xb_])
                            o, ob = xo.next()
                            STT(o[:], ps[:, :], adaT[l][:, 32 + cb, cond:cond + 1], xt_[:], ALU.mult, ALU.add, [pb, b_ada, xb_], [ob])
                            STo(x_dst(l)[cb * 128:(cb + 1) * 128, tsl], o[:], [ob], is_output=(l == NL_RUN - 1))
            S.fence()

    def on(name):
        marks.append((name, dict(S.ninstr)))
        return STAGES is None or name in STAGES

    for l in range(NL_RUN):
        if on("filt256"):
            hyena_filter_full(l, 256)
        if on("filt2048"):
            hyena_filter_full(l, 2048)
        if on("pass1"):
            pass1(l)
        if on("attn"):
            attention(l)
        if on("fnet"):
            fnet(l)
        if on("hyena"):
            hyena(l)
        if on("pass2"):
            pass2(l)

    S.emit()
    build_program.stats = {e: len(v) for e, v in S.prog.items()}
    return nc


_NC = None


def _host_inputs(inp, core):
    C = make_consts()
    b = core // 4
    xp = inp["x_prompt"][4 * core:4 * core + 4].reshape(1024, DM)
    xs = inp["x_sample"][b]
    m = {}
    m["xin"] = _f32(np.concatenate([xp, xs], 0).T)
    m["ck"] = _f32(inp["cache_k"][b].reshape(NL, 512, 1024))
    m["cv"] = _f32(inp["cache_v"][b].reshape(NL, 512, 1024))
    cvec = np.stack([inp["c_ctx"], inp["c"][b]], -1)
    m["cvecT"] = _f32(cvec.reshape(16, 128, 2).transpose(1, 0, 2))
    return m


def _shared_inputs(inp):
    C = make_consts()
    m = {}
    m["norm_gT"] = _f32(inp["norm_g"].reshape(NL, 16, 128).transpose(0, 2, 1))
    m["b_adaT"] = _f32(inp["b_ada"].reshape(NL, 48, 128).transpose(0, 2, 1))
    m["w_ada"] = _f32(inp["w_ada"]); m["w_in"] = _f32(inp["w_in"])
    m["w_br"] = _f32(inp["w_br"]); m["w_out"] = _f32(inp["w_out"])
    m["qg"] = _f32(inp["q_norm_g"].reshape(NL, 128, 1))
    m["kg"] = _f32(inp["k_norm_g"].reshape(NL, 128, 1))
    m["kg_rep"] = _f32(np.broadcast_to(inp["k_norm_g"][:, None, :], (NL, 128, 128)))
    m["rpbT2"] = rpb_gather(np.asarray(inp["rpb"]))
    m["conv_wT"] = _f32(inp["conv_w"].reshape(NL, 3, 12, 128).transpose(0, 3, 2, 1))
    m["conv_bT"] = _f32(inp["conv_b"].reshape(NL, 12, 128).transpose(0, 2, 1))
    m["hy_biasT"] = _f32(inp["hy_bias"].reshape(NL, 2, 4, 128).transpose(0, 3, 1, 2))
    m["f_w1"] = _f32(inp["f_w1"]); m["f_w2"] = _f32(inp["f_w2"]); m["f_w3"] = _f32(inp["f_w3"])
    m["f_b1T"] = _f32(inp["f_b1"].reshape(NL, 64, 1))
    m["f_b2T"] = _f32(inp["f_b2"].reshape(NL, 64, 1))
    m["f_freqT"] = _f32(inp["f_freq"].reshape(NL, 64, 1))
    for k, v in C.items():
        m["c_" + k] = v
    return m


def kernel(**inputs):
    global _NC
    inp = {k: np.asarray(v) for k, v in inputs.items()}
    if _NC is None:
        _NC = build_program()
    nc = _NC
    shared = _shared_inputs(inp)
    in_maps = []
    for core in range(8):
        m = dict(shared)
        m.update(_host_inputs(inp, core))
        in_maps.append(m)
    res = run_bass_kernel_spmd(nc, in_maps, core_ids=list(range(8)))
    R = res.results
    y_prompt = np.concatenate([np.ascontiguousarray(R[c]["y_out"][:, :1024].T).reshape(4, 256, DM) for c in range(8)], 0)
    y_sample = np.stack([np.ascontiguousarray(R[0]["y_out"][:, 1024:].T), np.ascontiguousarray(R[4]["y_out"][:, 1024:].T)], 0)
    nk = np.concatenate([R[c]["newk"].reshape(NL, 4, 256, 8, 128).transpose(1, 0, 2, 3, 4) for c in range(8)], 0)
    nv = np.concatenate([R[c]["newv"].reshape(NL, 4, 256, 8, 128).transpose(1, 0, 2, 3, 4) for c in range(8)], 0)
    if DEBUG:
        kernel.debug = R
    return (y_prompt.astype(np.float32), y_sample.astype(np.float32), nk.astype(np.float32), nv.astype(np.float32))
```

```python
import numpy as np
import concourse.bass as bass
import concourse.mybir as mybir

F32 = mybir.dt.float32
BF16 = mybir.dt.bfloat16
AF = mybir.ActivationFunctionType
ALU = mybir.AluOpType

COMPUTE = ("pe", "act", "dve", "pool")
NDMA_SEMS = 26
NSW = 6


class Buf:
    __slots__ = ("name", "w", "r")

    def __init__(self, name=""):
        self.name = name
        self.w = None
        self.r = []


class Sched:
    def __init__(self, nc):
        self.nc = nc
        self.prog = {e: [] for e in ("pe", "act", "dve", "pool", "sp")}
        self.flag = {e: set() for e in COMPUTE}
        self.ninstr = {e: 0 for e in COMPUTE}
        self.dma_n = [0] * NDMA_SEMS
        self.dma_rr = 0
        self.sw_rr = 0
        self.all_out_tokens = []
        self.pending_fence = {}

    def _collect(self, reads, writes):
        deps = []
        for b in reads:
            if b.w is not None:
                deps.append(b.w)
        for b in writes:
            if b.w is not None:
                deps.append(b.w)
            deps.extend(b.r)
        return deps

    def _commit(self, tok, reads, writes):
        for b in reads:
            b.r.append(tok)
        for b in writes:
            b.w = tok
            b.r = []

    def fence(self):
        toks = []
        for e in COMPUTE:
            if self.ninstr[e] > 0:
                toks.append(("c", e, self.ninstr[e] - 1))
        for k in range(NDMA_SEMS):
            if self.dma_n[k] > 0:
                toks.append(("d", k, self.dma_n[k]))
        self.pending_fence = {q: list(toks) for q in self.prog}

    def op(self, eng, fn, reads=(), writes=()):
        deps = self._collect(reads, writes)
        deps += self.pending_fence.pop(eng, [])
        idx = self.ninstr[eng]
        self.ninstr[eng] += 1
        tok = ("c", eng, idx)
        self.prog[eng].append([deps, fn, tok])
        self._commit(tok, reads, writes)
        return tok

    def dma(self, q, fn, reads=(), writes=(), is_output=False):
        deps = self._collect(reads, writes)
        deps += self.pending_fence.pop(q, [])
        if q == "pool":
            k = self.sw_rr
            self.sw_rr = (self.sw_rr + 1) % NSW
        else:
            k = NSW + self.dma_rr
            self.dma_rr = (self.dma_rr + 1) % (NDMA_SEMS - NSW)
        if self.dma_n[k] > 0:
            deps.append(("d", k, self.dma_n[k]))
        self.dma_n[k] += 1
        tok = ("d", k, self.dma_n[k])
        self.prog[q].append([deps, fn, tok])
        self._commit(tok, reads, writes)
        if is_output:
            self.all_out_tokens.append(tok)
        return tok

    def emit(self):
        nc = self.nc
        for e, items in self.prog.items():
            for deps, fn, tok in items:
                for d in deps:
                    if d[0] == "c":
                        if d[1] == "pe" and e == "pe":
                            continue
                        self.flag[d[1]].add(d[2])
        final_deps = list(self.all_out_tokens)
        rank = {}
        for e in COMPUTE:
            r = 0
            m = {}
            fl = self.flag[e]
            for i in range(self.ninstr[e]):
                if i in fl:
                    r += 1
                    m[i] = r
            rank[e] = m
        self.rank = rank

        def tokval(d):
            if d[0] == "c":
                return ("c", d[1]), rank[d[1]][d[2]]
            return ("d", d[1]), 16 * d[2]

        import contextlib
        with contextlib.ExitStack() as st:
            sems = {}
            for e in COMPUTE:
                sems[("c", e)] = st.enter_context(nc.semaphore("sem_" + e))
            for k in range(NDMA_SEMS):
                sems[("d", k)] = st.enter_context(nc.semaphore("sem_dma%d" % k))
            block = st.enter_context(nc.Block())

            def run(ekey, engine_obj, extra_final=None):
                seen = {}
                for deps, fn, tok in self.prog[ekey]:
                    need = {}
                    for d in deps:
                        if d[0] == "c" and d[1] == "pe" and ekey == "pe":
                            continue
                        key, val = tokval(d)
                        if seen.get(key, 0) < val:
                            if need.get(key, 0) < val:
                                need[key] = val
                    for key, val in need.items():
                        engine_obj.wait_ge(sems[key], val)
                        seen[key] = val
                    ins = fn(engine_obj)
                    if tok[0] == "c":
                        if tok[2] in self.flag[tok[1]]:
                            ins.then_inc(sems[("c", tok[1])], 1)
                    else:
                        ins.then_inc(sems[("d", tok[1])], 16)
                if extra_final:
                    need = {}
                    for d in extra_final:
                        key, val = tokval(d)
                        if need.get(key, 0) < val:
                            need[key] = val
                    for key, val in need.items():
                        engine_obj.wait_ge(sems[key], val)

            @block.sync
            def _(e):
                run("sp", e, final_deps)

            @block.tensor
            def _(e):
                run("pe", e)

            @block.scalar
            def _(e):
                run("act", e)

            @block.vector
            def _(e):
                run("dve", e)

            @block.gpsimd
            def _(e):
                run("pool", e, final_deps)

import math
from contextlib import ExitStack
import ml_dtypes
from concourse.bass_utils import run_bass_kernel_spmd

AX = mybir.AxisListType
NT = 3072
DM = 2048
NIN = 13312
NL = 2
EPS = 1e-6
MIN_DECAY = math.log(1e-2) / 1.5
MAX_DECAY = math.log(1e-2) / 0.3
DEBUG = False
NL_RUN = 2
STAGES = None

_bf = lambda a: np.ascontiguousarray(a.astype(ml_dtypes.bfloat16))
_f32 = lambda a: np.ascontiguousarray(a, dtype=np.float32)

_CONST = None


def make_consts():
    global _CONST
    if _CONST is not None:
        return _CONST
    c = {}
    c["ident"] = np.eye(128, dtype=np.float32)
    n = np.arange(128)
    ang = 2 * np.pi * (np.outer(n, n) % 128) / 128
    c["fnCS"] = _bf(np.concatenate([np.cos(ang), np.sin(ang)], 1) / np.sqrt(128))
    for L in (256, 2048):
        l = np.arange(L)
        ang = 2 * np.pi * (np.outer(l, l) % L) / L
        c["fnC%d" % L] = _bf(np.cos(ang) / np.sqrt(L))
        c["fnSn%d" % L] = _bf(-np.sin(ang) / np.sqrt(L))
        N = 2 * L
        m = np.outer(2 * l + 1, 2 * l + 1) % (4 * N)
        ang = 2 * np.pi * m / (4 * N)
        c["hyC%d" % L] = _bf(np.cos(ang))
        c["hyS%d" % L] = _bf(np.sin(ang))
        ph = np.pi * (l + 0.5) / N
        tab = np.stack([np.cos(ph) * 2 / N, np.sin(ph) * 2 / N], -1)
        c["phi%d" % L] = _f32(tab.reshape(L // 128, 128, 2).transpose(1, 0, 2))
        t = np.linspace(0.0, 1.0, L, dtype=np.float32)[:, None]
        w = (np.float32(2.0 * math.pi / L) * np.arange(L, dtype=np.float32))[:, None]
        fr = np.linspace(1e-4, 15, 16, dtype=np.float32)[None, :]
        z = np.concatenate([t, np.cos(w * fr), -np.sin(w * fr)], -1).astype(np.float32)
        c["zfT%d" % L] = _f32(z.T)
        deltas = np.abs(np.linspace(MIN_DECAY, MAX_DECAY, 512, dtype=np.float32))
        c["decay%d" % L] = _f32(np.concatenate([np.exp(-t * deltas[None, :]), np.zeros((1, 512), np.float32)], 0))
    p = np.arange(128)
    cp = p % 64
    cq = np.arange(64)
    cs = np.clip(cq - 8, 0, 48)
    valid = (cp[:, None] >= cs[None, :]) & (cp[:, None] < cs[None, :] + 16)
    c["nmask"] = _f32(np.broadcast_to(valid[:, None, :], (128, 14, 64)))
    _CONST = c
    return c


def rpb_gather(rpb):
    p = np.arange(128)
    cp = p % 64
    half = p // 64
    cq = np.arange(64)
    dc = cp[:, None] - cq[None, :] + 15
    ok = (dc >= 0) & (dc <= 30)
    dcc = np.clip(dc, 0, 30)
    dr = np.arange(14)
    drr = dr[None, :] + half[:, None]
    out = rpb[:, :, drr[:, :, None], dcc[:, None, :]]
    out = np.where(ok[None, None, :, None, :], out, 0.0)
    return _f32(out.reshape(NL, 8, 128, 14 * 64))


def build_program():
    nc = bass.Bass("TRN2", target_bir_lowering=False)
    S = Sched(nc)
    C = make_consts()

    def din(name, shape, dt=F32):
        return nc.dram_tensor(name, list(shape), dt, kind="ExternalInput").ap()

    def dscr(name, shape, dt=F32):
        kind = "ExternalOutput" if DEBUG else "Internal"
        return nc.dram_tensor(name, list(shape), dt, kind=kind).ap()

    xin = din("xin", [DM, NT])
    ck = din("ck", [NL, 512, 1024])
    cv = din("cv", [NL, 512, 1024])
    cvecT = din("cvecT", [128, 16, 2])
    norm_gT = din("norm_gT", [NL, 128, 16])
    b_adaT = din("b_adaT", [NL, 128, 48])
    w_ada = din("w_ada", [NL, DM, 3 * DM])
    w_in = din("w_in", [NL, DM, NIN])
    w_br = din("w_br", [NL, DM, DM])
    w_out = din("w_out", [NL, DM, DM])
    qg = din("qg", [NL, 128, 1])
    kg = din("kg", [NL, 128, 1])
    kg_rep = din("kg_rep", [NL, 128, 128])
    rpbT2 = din("rpbT2", [NL, 8, 128, 14 * 64])
    conv_wT = din("conv_wT", [NL, 128, 12, 3])
    conv_bT = din("conv_bT", [NL, 128, 12])
    hy_biasT = din("hy_biasT", [NL, 128, 2, 4])
    f_w1 = din("f_w1", [NL, 33, 64])
    f_b1T = din("f_b1T", [NL, 64, 1])
    f_freqT = din("f_freqT", [NL, 64, 1])
    f_w2 = din("f_w2", [NL, 64, 64])
    f_b2T = din("f_b2T", [NL, 64, 1])
    f_w3 = din("f_w3", [NL, 64, 2048])
    cd = {}
    for k, v in C.items():
        cd[k] = din("c_" + k, v.shape, BF16 if v.dtype == ml_dtypes.bfloat16 else F32)

    y_out = nc.dram_tensor("y_out", [DM, NT], F32, kind="ExternalOutput").ap()
    newk = nc.dram_tensor("newk", [NL, 1024, 1024], F32, kind="ExternalOutput").ap()
    newv = nc.dram_tensor("newv", [NL, 1024, 1024], F32, kind="ExternalOutput").ap()

    xT = dscr("s_xT", [DM, NT])
    qT = dscr("s_qT", [1024, NT], BF16)
    kT = dscr("s_kT", [1024, NT], BF16)
    vtok = dscr("s_vtok", [NT + 128, 1024], BF16)
    gT = dscr("s_gT", [2048, NT], BF16)
    ubT = dscr("s_ubT", [512, NT], BF16)
    hcT = dscr("s_hcT", [1536, NT])
    hccT = dscr("s_hccT", [1536, NT])
    z1T = dscr("s_z1T", [512, NT])
    mgT = dscr("s_mgT", [6144, NT], BF16)
    yT = dscr("s_yT", [2048, NT], BF16)
    ebs = dscr("s_eb", [8, 128, 14 * 64])
    w16 = dscr("s_w16", [8, 128, 16 * 512], BF16)
    hd_s = {L: dscr("s_hd%d" % L, [L + 128, 2048]) for L in (256, 2048)}
    kf_s = {L: dscr("s_kf%d" % L, [2, 2, L, 512]) for L in (256, 2048)}

    def sbp(name, shape, dt=F32):
        return nc.alloc_sbuf_tensor(name, list(shape), dt)

    ident = sbp("ident", [128, 128]); b_ident = Buf()
    identb = sbp("identb", [128, 128], BF16)
    onesb = sbp("onesb", [128, 128], BF16)
    onesf = sbp("onesf", [128, 128])
    epsc = sbp("epsc", [128, 1])
    zeroc = sbp("zeroc", [128, 1])
    b_const = Buf()
    adaT = [sbp("adaT%d" % l, [128, 48, 2]) for l in range(NL)]
    Gm = [sbp("Gm%d" % l, [128, 16, 2]) for l in range(NL)]
    b_ada = Buf()
    small = {}
    for l in range(NL):
        small[l] = dict(
            ng=sbp("ng%d" % l, [128, 16]), ba=sbp("ba%d" % l, [128, 48]),
            qg=sbp("qg%d" % l, [128, 1]), kg=sbp("kg%d" % l, [128, 1]),
            kgr=sbp("kgr%d" % l, [128, 128]),
            cw=sbp("cw%d" % l, [128, 12, 3]), cb=sbp("cb%d" % l, [128, 12]),
            hb=sbp("hb%d" % l, [128, 2, 4]),
        )
    b_small = Buf()

    pst = [nc.alloc_psum_tensor("ps%d" % i, [128, 512], F32) for i in range(8)]
    psb = [Buf() for _ in range(8)]
    ps_avail = list(range(8))
    psi = [0]

    def PS():
        psi[0] = (psi[0] + 1) % len(ps_avail)
        i = ps_avail[psi[0]]
        return pst[i], psb[i]

    def PS_hold(n):
        held = [ps_avail.pop() for _ in range(n)]
        psi[0] = 0
        return [(pst[i], psb[i]) for i in held], held

    def PS_release(held):
        ps_avail.extend(held)

    uid = [0]

    def uname(name):
        uid[0] += 1
        return "%s_%d" % (name, uid[0])

    class Ring:
        def __init__(self, st, name, shape, dt, n):
            self.t = [st.enter_context(nc.sbuf_tensor(uname(name), list(shape), dt)) for i in range(n)]
            self.b = [Buf() for _ in range(n)]
            self.i = 0

        def next(self):
            i = self.i
            self.i = (i + 1) % len(self.t)
            return self.t[i], self.b[i]

    class WStream:
        def __init__(self, ring, srcs, pre=False, q="pool"):
            self.ring = ring; self.srcs = srcs; self.i = 0; self.pre = pre; self.q = q
            self.cur = self._issue(0)

        def _issue(self, i):
            if i >= len(self.srcs):
                return None
            wt, wb = self.ring.next()
            if self.pre:
                LD(wt[:].rearrange("p k n -> p (k n)"), self.srcs[i], [wb], q=self.q)
            else:
                LD(wt[:], self.srcs[i].rearrange("(k p) n -> p k n", p=128), [wb], q="pool")
            return wt, wb

        def get(self):
            c = self.cur
            self.i += 1
            self.cur = self._issue(self.i)
            return c

    def sb(st, name, shape, dt=F32):
        return st.enter_context(nc.sbuf_tensor(uname(name), list(shape), dt))

    def MM(ps, lhsT, rhs, start, stop, reads, pb, skip=False):
        S.op("pe", lambda e: e.matmul(ps, lhsT=lhsT, rhs=rhs, start=start, stop=stop,
                                      skip_group_check=skip), reads=reads, writes=[pb])

    def TR(ps, in_, idt, reads, pb):
        S.op("pe", lambda e: e.transpose(out=ps, in_=in_, identity=idt), reads=reads + [b_ident], writes=[pb])

    def ACT(out, in_, func, reads, writes, scale=None, bias=None):
        kw = {}
        if scale is not None:
            kw["scale"] = scale
        if bias is not None:
            kw["bias"] = bias
        S.op("act", lambda e: e.activation(out=out, in_=in_, func=func, **kw), reads=reads, writes=writes)

    def TT(out, in0, in1, op, reads, writes, eng="dve"):
        S.op(eng, lambda e: e.tensor_tensor(out=out, in0=in0, in1=in1, op=op), reads=reads, writes=writes)

    def TS(out, in0, s1, s2, op0, op1, reads, writes, eng="dve"):
        if op1 is None:
            S.op(eng, lambda e: e.tensor_scalar(out=out, in0=in0, scalar1=s1, scalar2=None, op0=op0), reads=reads, writes=writes)
        else:
            S.op(eng, lambda e: e.tensor_scalar(out=out, in0=in0, scalar1=s1, scalar2=s2, op0=op0, op1=op1), reads=reads, writes=writes)

    def STT(out, in0, scalar, in1, op0, op1, reads, writes):
        S.op("dve", lambda e: e.scalar_tensor_tensor(out=out, in0=in0, scalar=scalar, in1=in1, op0=op0, op1=op1),
             reads=reads, writes=writes)

    def RED(out, in_, reads, writes):
        S.op("dve", lambda e: e.tensor_reduce(out=out, in_=in_, axis=AX.X, op=ALU.add), reads=reads, writes=writes)

    def CP(out, in_, reads, writes, eng="dve"):
        S.op(eng, lambda e: e.tensor_copy(out=out, in_=in_), reads=reads, writes=writes)

    def LD(out, in_, writes, reads=(), q="sp"):
        S.dma(q, lambda e: e.dma_start(out=out, in_=in_), reads=list(reads), writes=list(writes))

    STQ = ["sp"]

    def STo(out, in_, reads, writes=(), is_output=False):
        S.dma(STQ[0], lambda e: e.dma_start(out=out, in_=in_), reads=list(reads), writes=list(writes), is_output=is_output)

    def rstd_from(ps_sum, out, scale, reads, writes):
        ACT(out, ps_sum, AF.Ln, reads + [b_const], writes, scale=scale, bias=epsc[:out.shape[0], 0:1])
        ACT(out, out, AF.Exp, writes, writes, scale=-0.5)

    LD(ident[:], cd["ident"][:, :], [b_ident])
    S.op("dve", lambda e: e.tensor_copy(out=identb[:], in_=ident[:]), reads=[b_ident], writes=[b_ident])
    S.op("dve", lambda e: e.memset(onesb[:], 1.0), writes=[b_const])
    S.op("dve", lambda e: e.memset(onesf[:], 1.0), writes=[b_const])
    S.op("dve", lambda e: e.memset(epsc[:], EPS), writes=[b_const])
    S.op("dve", lambda e: e.memset(zeroc[:], 0.0), writes=[b_const])
    for l in range(NL):
        sm = small[l]
        LD(sm["ng"][:], norm_gT[l], [b_small]); LD(sm["ba"][:], b_adaT[l], [b_small])
        LD(sm["qg"][:], qg[l], [b_small]); LD(sm["kg"][:], kg[l], [b_small])
        LD(sm["kgr"][:], kg_rep[l], [b_small])
        LD(sm["cw"][:], conv_wT[l], [b_small]); LD(sm["cb"][:], conv_bT[l], [b_small])
        LD(sm["hb"][:], hy_biasT[l], [b_small])
        TS(sm["qg"][:], sm["qg"][:], float(128 ** -0.5), None, ALU.mult, None, [b_small], [b_small])

    with ExitStack() as st:
        war = Ring(st, "wada", [128, 16, 512], BF16, 3)
        sil0 = sb(st, "sil0", [128, 16, 2]); b_sil = Buf()
        sil = sb(st, "sil", [128, 16, 2], BF16)
        LD(sil0[:], cvecT[:, :, :], [b_sil])
        ACT(sil[:], sil0[:], AF.Silu, [b_sil], [b_sil])
        for l in range(NL):
            ps, pb = PS()
            for sbk in range(12):
                wt, wb = war.next()
                LD(wt[:], w_ada[l, :, sbk * 512:(sbk + 1) * 512].rearrange("(k p) n -> p k n", p=128), [wb], q="pool")
                for j4 in range(4):
                    j = sbk * 4 + j4
                    for kc in range(16):
                        MM(ps[:, 2 * j:2 * j + 2], wt[:, kc, j4 * 128:(j4 + 1) * 128], sil[:, kc, :],
                           kc == 0, kc == 15, [wb, b_sil], pb)
            for c in range(2):
                TT(adaT[l][:, :, c], ps[:, c:96:2], small[l]["ba"][:], ALU.add, [pb, b_small], [b_ada])
                STT(Gm[l][:, :, c], adaT[l][:, 16:32, c], 1.0, small[l]["ng"][:], ALU.add, ALU.mult,
                    [b_ada, b_small], [b_ada])
        S.fence()

    with ExitStack() as st:
        zpad = sb(st, "zpad", [128, 1024], BF16); bzp = Buf()
        S.op("dve", lambda e: e.memset(zpad[:], 0.0), writes=[bzp])
        STo(vtok[NT:NT + 128, :], zpad[:], [bzp])
        S.fence()

    def x_src(l):
        return xin if l == 0 else xT

    def x_dst(l):
        return y_out if l == NL_RUN - 1 else xT

    marks = [('start', dict(S.ninstr))]
    build_program.marks = marks

    def hyena_filter_full(l, L):
        STQ[0] = "sp"
        nch = L // 128
        kf = kf_s[L]
        with ExitStack() as st:
            hb = sb(st, "fhb", [128, nch, 2048], BF16); bhbs = [Buf() for _ in range(nch)]
            recs = sb(st, "frecs", [128, 2048]); brec = Buf()
            with ExitStack() as st1:
                w1 = sb(st1, "fw1", [33, 64]); w2 = sb(st1, "fw2", [64, 64]); w3 = sb(st1, "fw3", [64, 2048])
                zf = sb(st1, "fzf", [33, L])
                b1 = sb(st1, "fb1", [64, 1]); b2 = sb(st1, "fb2", [64, 1]); fq = sb(st1, "ffq", [64, 1])
                h1 = sb(st1, "fh1", [64, L]); h2 = sb(st1, "fh2", [64, L + 1])
                tmp = Ring(st1, "ftmp", [64, 512], F32, 2)
                tmp2 = Ring(st1, "ftmp2", [64, 512], F32, 2)
                bw = Buf(); bh1 = Buf(); bh2 = Buf()
                LD(w1[:], f_w1[l], [bw]); LD(w2[:], f_w2[l], [bw]); LD(w3[:], f_w3[l], [bw])
                LD(zf[:], cd["zfT%d" % L][:, :], [bw])
                LD(b1[:], f_b1T[l], [bw]); LD(b2[:], f_b2T[l], [bw]); LD(fq[:], f_freqT[l], [bw])

                def sin_layer(w, bcol, src, bsrc, K, dst, bdst):
                    for c0 in range(0, L, 512):
                        n = min(512, L - c0)
                        ps, pb = PS()
                        MM(ps[:64, :n], w[:K, :], src[:K, c0:c0 + n], True, True, [bw, bsrc], pb)
                        a, ab = tmp.next()
                        TS(a[:, :n], ps[:64, :n], bcol[:, 0:1], fq[:, 0:1], ALU.add, ALU.mult, [pb, bw], [ab])
                        m, mb = tmp2.next()
                        TS(m[:, :n], a[:, :n], float(np.pi), float(-2 * np.pi), ALU.is_gt, ALU.mult, [ab], [mb])
                        TT(a[:, :n], a[:, :n], m[:, :n], ALU.add, [ab, mb], [ab])
                        TS(m[:, :n], a[:, :n], float(-np.pi), float(2 * np.pi), ALU.is_lt, ALU.mult, [ab], [mb])
                        TT(a[:, :n], a[:, :n], m[:, :n], ALU.add, [ab, mb], [ab])
                        TS(a[:, :n], a[:, :n], float(np.pi), float(-np.pi), ALU.min, ALU.max, [ab], [ab])
                        ACT(dst[:, c0:c0 + n], a[:, :n], AF.Sin, [ab], [bdst])

                S.op("dve", lambda e: e.memset(h2[:, L:L + 1], 0.0), writes=[bh2])
                sin_layer(w1, b1, zf, bw, 33, h1, bh1)
                sin_layer(w2, b2, h1, bh1, 64, h2, bh2)
                dsr = Ring(st1, "fdsh", [128, 512], F32, 2)
                dec = Ring(st1, "fdec", [128, 512], F32, 2)
                hdr = Ring(st1, "fhd", [128, 2048], F32, 2)
                habs = Ring(st1, "fhabs", [128, 2048], BF16, 3)
                sps, sheld = PS_hold(4)
                fpend = []
                for dc in range(nch):
                    dt_, db_ = dec.next()
                    LD(dt_[:], cd["decay%d" % L][dc * 128:(dc + 1) * 128, :], [db_])
                    ht, hb_ = hdr.next()
                    for cs in range(4):
                        ps, pb = PS()
                        MM(ps[:, :], h2[:, dc * 128:(dc + 1) * 128], w3[:, cs * 512:(cs + 1) * 512], True, True, [bh2, bw], pb)
                        TT(ht[:, cs * 512:(cs + 1) * 512], ps[:, :], dt_[:], ALU.mult, [pb, db_], [hb_])
                    for o in range(2):
                        CP(hb[:, dc, o * 1024:o * 1024 + 512], ht[:, o * 1024:o * 1024 + 512], [hb_], [bhbs[dc]], eng="pool")
                    ds_, dsb = dsr.next()
                    LD(ds_[:], cd["decay%d" % L][dc * 128 + 1:(dc + 1) * 128 + 1, :], [dsb])
                    for o in range(2):
                        ps, pb = PS()
                        MM(ps[:, :], h2[:, dc * 128 + 1:(dc + 1) * 128 + 1], w3[:, o * 1024 + 512:o * 1024 + 1024], True, True, [bh2, bw], pb)
                        TT(hb[:, dc, o * 1024 + 512:o * 1024 + 1024], ps[:, :], ds_[:], ALU.mult, [pb, dsb], [bhbs[dc]])
                    at, ab_ = habs.next()
                    ACT(at[:], ht[:], AF.Abs, [hb_], [ab_])
                    if fpend:
                        fpend.pop(0)()

                    def ones_mm(at=at, ab_=ab_, dc=dc):
                        for cs in range(4):
                            MM(sps[cs][0][:, :], onesb[:], at[:, cs * 512:(cs + 1) * 512], dc == 0, dc == nch - 1,
                               [b_const, ab_], sps[cs][1])
                    fpend.append(ones_mm)
                while fpend:
                    fpend.pop(0)()
                for cs in range(4):
                    TS(recs[:, cs * 512:(cs + 1) * 512], sps[cs][0][:, :], EPS, None, ALU.add, None, [sps[cs][1]], [brec])
                S.op("dve", lambda e: e.reciprocal(out=recs[:], in_=recs[:]), reads=[brec], writes=[brec])
                nt = Ring(st1, "fnt", [128, 512], F32, 4)
                for dc in range(nch):
                    for o in range(2):
                        cF = slice(o * 1024, o * 1024 + 512); cB = slice(o * 1024 + 512, o * 1024 + 1024)
                        t1, t1b = nt.next(); t2, t2b = nt.next()
                        TT(t1[:], hb[:, dc, cF], recs[:, cF], ALU.mult, [bhbs[dc], brec], [t1b])
                        TT(t2[:], hb[:, dc, cB], recs[:, cB], ALU.mult, [bhbs[dc], brec], [t2b])
                        TT(hb[:, dc, cF], t1[:], t2[:], ALU.add, [t1b, t2b], [bhbs[dc]])
                        TT(hb[:, dc, cB], t1[:], t2[:], ALU.subtract, [t1b, t2b], [bhbs[dc]], eng="pool")
                PS_release(sheld)
                S.fence()
            with ExitStack() as st2:
                phi = sb(st2, "fphi", [128, nch, 2]); bphi = Buf()
                LD(phi[:], cd["phi%d" % L][:, :, :], [bphi])
                nsl = min(512, L)
                cr = Ring(st2, "fC", [128, nch, nsl], BF16, 2)
                sr = Ring(st2, "fS", [128, nch, nsl], BF16, 2)
                ko = Ring(st2, "fko", [128, 2, 1024], F32, 2)
                tr_ = Ring(st2, "ft", [128, 512], F32, 8)
                for f0 in range(0, L, nsl):
                    ct, cb_ = cr.next(); st_, sb__ = sr.next()
                    LD(ct[:], cd["hyC%d" % L][:, f0:f0 + nsl].rearrange("(k p) f -> p k f", p=128), [cb_])
                    LD(st_[:], cd["hyS%d" % L][:, f0:f0 + nsl].rearrange("(k p) f -> p k f", p=128), [sb__])
                    for fs in range(nsl // 128):
                        fc = f0 // 128 + fs
                        fsl = slice(fs * 128, (fs + 1) * 128)
                        cph = phi[:, fc, 0:1]; sph = phi[:, fc, 1:2]
                        kt, kb_ = ko.next()
                        for o in range(2):
                            cF = slice(o * 1024, o * 1024 + 512); cB = slice(o * 1024 + 512, o * 1024 + 1024)
                            psA, pbA = PS(); psB, pbB = PS()
                            for dc in range(nch):
                                MM(psA[:, :], ct[:, dc, fsl], hb[:, dc, cF], dc == 0, dc == nch - 1, [cb_, bhbs[dc]], pbA)
                            for dc in range(nch):
                                MM(psB[:, :], st_[:, dc, fsl], hb[:, dc, cB], dc == 0, dc == nch - 1, [sb__, bhbs[dc]], pbB)
                            t1, t1b = tr_.next(); t2, t2b = tr_.next()
                            TS(t1[:], psB[:, :], sph, None, ALU.mult, None, [pbB, bphi], [t1b])
                            STT(kt[:, 0, o * 512:(o + 1) * 512], psA[:, :], cph, t1[:], ALU.mult, ALU.add, [pbA, bphi, t1b], [kb_])
                            TS(t2[:], psB[:, :], cph, None, ALU.mult, None, [pbB, bphi], [t2b])
                            STT(kt[:, 1, o * 512:(o + 1) * 512], psA[:, :], sph, t2[:], ALU.mult, ALU.subtract, [pbA, bphi, t2b], [kb_])
                        for ri in range(2):
                            STo(kf[ri, :, fc * 128:(fc + 1) * 128, :].rearrange("o f c -> f o c"),
                                kt[:, ri, :].rearrange("p (o c) -> p o c", o=2), [kb_])
                S.fence()

    def pass1(l):
        STQ[0] = "sp"
        sm = small[l]
        with ExitStack() as st:
            hTs = [sb(st, "hT", [128, 16, 1024], BF16) for _ in range(2)]
            b_hTs = [Buf(), Buf()]
            wr = Ring(st, "wsb", [128, 16, 512], BF16, 2)
            xc = Ring(st, "xc", [128, 16, 512], F32, 2)
            sqr = Ring(st, "sq", [128, 512], BF16, 3)
            sqf = Ring(st, "sqf", [128, 512], F32, 2)
            accr = Ring(st, "acc", [128, 512], F32, 2)
            rsr = Ring(st, "rs", [128, 512], F32, 3)
            tmr = Ring(st, "tm", [128, 512], F32, 2)
            ef = Ring(st, "ef", [128, 512], F32, 4)
            eb = Ring(st, "eb", [128, 512], BF16, 4)
            sm4 = Ring(st, "sm4", [128, 8], F32, 2)
            loaded = {}

            def prep_load(blk, tc):
                xt_, xb_ = xc.next()
                t0 = blk * 1024
                LD(xt_[:], x_src(l)[:, t0 + tc * 512:t0 + (tc + 1) * 512].rearrange("(k p) t -> p k t", p=128), [xb_])
                loaded[(blk, tc)] = (xt_, xb_)

            def prep_compute(blk, tc):
                cond = 0 if blk == 0 else 1
                hT = hTs[blk % 2]; b_hT = b_hTs[blk % 2]
                xt_, xb_ = loaded.pop((blk, tc))
                psr, pbr = PS()
                if blk == 0:
                    for kc in range(16):
                        sq, sqb = sqr.next()
                        ACT(sq[:], xt_[:, kc, :], AF.Square, [xb_], [sqb])
                        MM(psr[:, :], onesb[:], sq[:], kc == 0, kc == 15, [b_const, sqb], pbr)
                else:
                    acc, accb = accr.next()
                    for kc in range(16):
                        if kc == 0:
                            TT(acc[:], xt_[:, 0, :], xt_[:, 0, :], ALU.mult, [xb_], [accb], eng="pool")
                        else:
                            sq, sqb = sqf.next()
                            TT(sq[:], xt_[:, kc, :], xt_[:, kc, :], ALU.mult, [xb_], [sqb], eng="pool")
                            TT(acc[:], acc[:], sq[:], ALU.add, [accb, sqb], [accb], eng="pool")
                    MM(psr[:, :], onesf[:], acc[:], True, True, [b_const, accb], pbr)
                rs, rsb = rsr.next()
                rstd_from(psr[:, :], rs[:], 1.0 / DM, [pbr], [rsb])
                for kc in range(16):
                    tm, tmb = tmr.next()
                    TT(tm[:], xt_[:, kc, :], rs[:], ALU.mult, [xb_, rsb], [tmb])
                    ACT(hT[:, kc, tc * 512:(tc + 1) * 512], tm[:], AF.Identity, [tmb, b_ada], [b_hT],
                        scale=Gm[l][:, kc, cond:cond + 1], bias=adaT[l][:, kc, cond:cond + 1])

            pending = []

            def flush():
                while pending:
                    pending.pop(0)()

            wstream = WStream(wr, [w_in[l, :, sbk * 512:(sbk + 1) * 512] for blk in range(3) for sbk in range(26)])
            prep_load(0, 0); prep_load(0, 1); prep_compute(0, 0); prep_compute(0, 1)
            for blk in range(3):
                t0 = blk * 1024
                hT = hTs[blk % 2]; b_hT = b_hTs[blk % 2]
                for sbk in range(26):
                    if blk + 1 < 3:
                        if sbk == 1:
                            prep_load(blk + 1, 0)
                        if sbk == 6:
                            prep_compute(blk + 1, 0)
                        if sbk == 8:
                            prep_load(blk + 1, 1)
                        if sbk == 14:
                            prep_compute(blk + 1, 1)
                    wt, wb = wstream.get()
                    fm = sbk not in (4, 5)
                    if fm:
                        for cb4 in range(4):
                            for tc in range(2):
                                ps, pb = PS()
                                for kc in range(16):
                                    MM(ps[:, :], wt[:, kc, cb4 * 128:(cb4 + 1) * 128], hT[:, kc, tc * 512:(tc + 1) * 512],
                                       kc == 0, kc == 15, [wb, b_hT], pb)
                                tsl = slice(t0 + tc * 512, t0 + (tc + 1) * 512)
                                if sbk < 4:
                                    raw, rb = ef.next()
                                    ACT(raw[:], ps[:, :], AF.Identity, [pb], [rb])
                                    sq, sqb = sqr.next()
                                    TT(sq[:], raw[:], raw[:], ALU.mult, [rb], [sqb])
                                    flush()

                                    def partB(raw=raw, rb=rb, sq=sq, sqb=sqb, sbk=sbk, cb4=cb4, tsl=tsl):
                                        ps2, pb2 = PS()
                                        MM(ps2[:, :], onesb[:], sq[:], True, True, [b_const, sqb], pb2)
                                        rs, rsb = rsr.next()
                                        rstd_from(ps2[:, :], rs[:], 1.0 / 128, [pb2], [rsb])
                                        o, ob = eb.next()
                                        gcol = sm["qg"] if sbk < 2 else sm["kg"]
                                        STT(o[:], raw[:], gcol[:, 0:1], rs[:], ALU.mult, ALU.mult, [rb, rsb, b_small], [ob])
                                        dst = qT if sbk < 2 else kT
                                        r0 = (sbk % 2) * 512 + cb4 * 128
                                        STo(dst[r0:r0 + 128, tsl], o[:], [ob])
                                    pending.append(partB)
                                    continue
                                flush()
                                if sbk in (6, 7, 9, 13):
                                    o, ob = eb.next()
                                    ACT(o[:], ps[:, :], AF.Silu, [pb], [ob])
                                    r0 = {6: 0, 7: 512, 9: 1024, 13: 1536}[sbk] + cb4 * 128
                                    STo(gT[r0:r0 + 128, tsl], o[:], [ob])
                                elif sbk == 8:
                                    o, ob = eb.next()
                                    CP(o[:], ps[:, :], [pb], [ob])
                                    STo(ubT[cb4 * 128:(cb4 + 1) * 128, tsl], o[:], [ob])
                                elif sbk in (10, 11, 12):
                                    o, ob = ef.next()
                                    CP(o[:], ps[:, :], [pb], [ob])
                                    r0 = (sbk - 10) * 512 + cb4 * 128
                                    STo(hcT[r0:r0 + 128, tsl], o[:], [ob])
                                else:
                                    o, ob = eb.next()
                                    ACT(o[:], ps[:, :], AF.Sigmoid, [pb], [ob])
                                    r0 = (sbk - 14) * 512 + cb4 * 128
                                    STo(mgT[r0:r0 + 128, tsl], o[:], [ob])
                    if sbk in (4, 5) or (sbk in (2, 3) and blk == 0):
                        for tt in range(8):
                            ps, pb = PS()
                            for kc in range(16):
                                MM(ps[:, :], hT[:, kc, tt * 128:(tt + 1) * 128], wt[:, kc, :], kc == 0, kc == 15, [wb, b_hT], pb)
                            flush()
                            raw, rb = ef.next()
                            ACT(raw[:], ps[:, :], AF.Identity, [pb], [rb])
                            tok0 = t0 + tt * 128
                            if sbk in (4, 5):
                                c0 = (sbk - 4) * 512
                                if blk == 0:
                                    STo(newv[l, tok0:tok0 + 128, c0:c0 + 512], raw[:], [rb], is_output=True)
                                o, ob = eb.next()
                                CP(o[:], raw[:], [rb], [ob])
                                STo(vtok[tok0:tok0 + 128, c0:c0 + 512], o[:], [ob])
                            else:
                                c0 = (sbk - 2) * 512
                                sq, sqb = ef.next()
                                TT(sq[:], raw[:], raw[:], ALU.mult, [rb], [sqb])
                                s4, s4b = sm4.next()
                                RED(s4[:, 0:4], sq[:].rearrange("p (h d) -> p h d", h=4), [sqb], [s4b])
                                rstd_from(s4[:, 0:4], s4[:, 4:8], 1.0 / 128, [s4b], [s4b])
                                for h in range(4):
                                    STT(sq[:, h * 128:(h + 1) * 128], raw[:, h * 128:(h + 1) * 128], s4[:, 4 + h:5 + h],
                                        sm["kgr"][:], ALU.mult, ALU.mult, [rb, s4b, b_small, sqb], [sqb])
                                STo(newk[l, tok0:tok0 + 128, c0:c0 + 512], sq[:], [sqb], is_output=True)
                flush()
            S.fence()

    def attention(l):
        sm = small[l]
        STQ[0] = "pool"
        with ExitStack() as st:
            tr_ = Ring(st, "abt", [128, 14 * 64], F32, 2)
            mk = sb(st, "amask", [128, 14 * 64]); bmk = Buf()
            LD(mk[:], cd["nmask"].rearrange("p a c -> p (a c)"), [bmk])
            for h in range(8):
                t, tb = tr_.next()
                LD(t[:], rpbT2[l, h], [tb])
                ACT(t[:], t[:], AF.Exp, [tb], [tb])
                TT(t[:], t[:], mk[:], ALU.mult, [tb, bmk], [tb])
                STo(ebs[h], t[:], [tb])
            S.fence()
        with ExitStack() as st:
            qr = Ring(st, "aq", [128, 2048], BF16, 3)
            kr = Ring(st, "ak", [128, 2048], BF16, 3)
            v0r = Ring(st, "av0", [128, 16, 128], BF16, 3)
            v1r = Ring(st, "av1", [128, 16, 128], BF16, 2)
            ckr = Ring(st, "ack", [128, 4, 128], F32, 2)
            cktr = Ring(st, "ackT", [128, 512], BF16, 2)
            cvr = Ring(st, "acv", [128, 4, 128], BF16, 2)
            ebr = Ring(st, "aeb", [128, 14, 64], F32, 2)
            pcr = Ring(st, "apc", [128, 4, 512], BF16, 2)
            pwr = Ring(st, "apw", [128, 8, 64], BF16, 6)
            gar = Ring(st, "aga", [128, 512], BF16, 3)
            rcr = Ring(st, "arc", [128, 512], F32, 2)
            t2r = Ring(st, "at2", [128, 512], F32, 2)
            yor = Ring(st, "ayo", [128, 512], BF16, 2)
            chr_ = Ring(st, "yh", [128, NT], F32, 2)
            cor_ = Ring(st, "yo", [128, NT], F32, 2)
            segs = [(s_ * 256, 256) for s_ in range(4)] + [(1024, 2048)]
            conv_todo = list(range(12))
            wcr = Ring(st, "awc", [128, 16, 512], BF16, 2)
            wc_todo = [(w, sbk) for w in (w_br, w_out) for sbk in range(4)]
            wc_pend = []

            def wcast_unit():
                while wc_pend:
                    wc_pend.pop(0)()
                if not wc_todo:
                    return
                w, sbk = wc_todo.pop(0)
                idx = 7 - len(wc_todo)
                wt, wb = wcr.next()
                LD(wt[:], w[l, :, sbk * 512:(sbk + 1) * 512].rearrange("(k p) n -> p k n", p=128), [wb], q="pool")
                wc_pend.append(lambda: S.dma("pool", lambda e: e.dma_start(out=w16[idx], in_=wt[:].rearrange("p k n -> p (k n)")),
                                             reads=[wb], writes=[]))

            def conv_unit():
                if not conv_todo:
                    return
                cb = conv_todo.pop(0)
                ht, hb_ = chr_.next()
                LD(ht[:], hcT[cb * 128:(cb + 1) * 128, :], [hb_])
                ot, ob = cor_.next()
                ACT(ot[:], ht[:], AF.Identity, [hb_, b_small], [ob], scale=sm["cw"][:, cb, 1:2], bias=sm["cb"][:, cb:cb + 1])
                for (t0, L) in segs:
                    STT(ot[:, t0 + 1:t0 + L], ht[:, t0:t0 + L - 1], sm["cw"][:, cb, 0:1], ot[:, t0 + 1:t0 + L], ALU.mult, ALU.add, [hb_, b_small, ob], [ob])
                    STT(ot[:, t0:t0 + L - 1], ht[:, t0 + 1:t0 + L], sm["cw"][:, cb, 2:3], ot[:, t0:t0 + L - 1], ALU.mult, ALU.add, [hb_, b_small, ob], [ob])
                STo(hccT[cb * 128:(cb + 1) * 128, :], ot[:], [ob])

            def epilogue(psO, pbO, psD, pbD, n, grow, tsl):
                rc, rcb = rcr.next()
                S.op("dve", lambda e: e.reciprocal(out=rc[:, :n], in_=psD[:, :n]), reads=[pbD], writes=[rcb])
                ga, gab = gar.next()
                LD(ga[:, :n], gT[grow:grow + 128, tsl], [gab])
                t2, t2b = t2r.next()
                TT(t2[:, :n], psO[:, :n], rc[:, :n], ALU.mult, [pbO, rcb], [t2b])
                yo, yob = yor.next()
                TT(yo[:, :n], t2[:, :n], ga[:, :n], ALU.mult, [t2b, gab], [yob])
                STo(yT[grow:grow + 128, tsl], yo[:, :n], [yob])

            pend = []
            for s in range(4):
                for h in range(8):
                    t0 = s * 256
                    qt, qb = qr.next(); kt, kb = kr.next(); vt, vb = v0r.next()
                    LD(qt[:, 0:256], qT[h * 128:(h + 1) * 128, t0:t0 + 256], [qb])
                    LD(kt[:, 0:256], kT[h * 128:(h + 1) * 128, t0:t0 + 256], [kb])
                    LD(vt[:, 0:2, :], vtok[t0:t0 + 256, h * 128:(h + 1) * 128].rearrange("(k p) d -> p k d", p=128), [vb])
                    pc, pcb = pcr.next()
                    ps, pb = PS()
                    for kc in range(2):
                        MM(ps[:, kc * 256:(kc + 1) * 256], kt[:, kc * 128:(kc + 1) * 128], qt[:, 0:256], True, True, [kb, qb], pb)
                    ACT(pc[:, 0, :], ps[:, :], AF.Exp, [pb], [pcb])
                    while pend:
                        pend.pop(0)()

                    def ph2(pc=pc, pcb=pcb, vt=vt, vb=vb, h=h, t0=t0):
                        psD, pbD = PS(); psO, pbO = PS()
                        for kc in range(2):
                            MM(psD[:, 0:256], onesb[:], pc[:, 0, kc * 256:(kc + 1) * 256], kc == 0, kc == 1, [b_const, pcb], pbD)
                        for kc in range(2):
                            MM(psO[:, 0:256], vt[:, kc, :], pc[:, 0, kc * 256:(kc + 1) * 256], kc == 0, kc == 1, [vb, pcb], pbO)
                        epilogue(psO, pbO, psD, pbD, 256, h * 128, slice(t0, t0 + 256))
                    pend.append(ph2)
            while pend:
                pend.pop(0)()
            for h in range(8):
                qt, qb = qr.next(); kt, kb = kr.next(); v0, v0b = v0r.next(); v1, v1b = v1r.next()
                LD(qt[:], qT[h * 128:(h + 1) * 128, 1024:3072], [qb])
                LD(kt[:], kT[h * 128:(h + 1) * 128, 1024:3072], [kb])
                LD(v0[:], vtok[1024:3072, h * 128:(h + 1) * 128].rearrange("(k p) d -> p k d", p=128), [v0b])
                LD(v1[:], vtok[1088:3136, h * 128:(h + 1) * 128].rearrange("(k p) d -> p k d", p=128), [v1b])
                ckt, ckb = ckr.next()
                LD(ckt[:], ck[l, :, h * 128:(h + 1) * 128].rearrange("(k p) d -> p k d", p=128), [ckb])
                cvt, cvb = cvr.next()
                LD(cvt[:], cv[l, :, h * 128:(h + 1) * 128].rearrange("(k p) d -> p k d", p=128), [cvb], q="pool")
                ebt, ebb = ebr.next()
                LD(ebt[:], ebs[h].rearrange("p (a c) -> p a c", a=14), [ebb])
                ps, pb = PS()
                for kc in range(4):
                    TR(ps[:, kc * 128:(kc + 1) * 128], ckt[:, kc, :], ident[:], [ckb], pb)
                cT, cTb = cktr.next()
                CP(cT[:], ps[:, :], [pb], [cTb])
                for g in range(4):
                    qs = slice(g * 512, (g + 1) * 512)
                    pc, pcb = pcr.next()
                    for kc in range(4):
                        ps, pb = PS()
                        MM(ps[:, :], cT[:, kc * 128:(kc + 1) * 128], qt[:, qs], True, True, [cTb, qb], pb)
                        ACT(pc[:, kc, :], ps[:, :], AF.Exp, [pb], [pcb])
                    (hp, held) = PS_hold(2)
                    (psD, pbD), (psO, pbO) = hp
                    for kc in range(4):
                        MM(psD[:, :], onesb[:], pc[:, kc, :], kc == 0, False, [b_const, pcb], pbD, skip=True)
                    for kc in range(4):
                        MM(psO[:, :], cvt[:, kc, :], pc[:, kc, :], kc == 0, False, [cvb, pcb], pbO, skip=True)
                    rowinfo = []
                    for pr in range(4):
                        psS, pbS = PS()
                        pw, pwb = pwr.next()
                        for half in range(2):
                            r = g * 8 + pr * 2 + half
                            rs_ = min(max(r - 4, 0), 24)
                            q64 = slice(r * 64, (r + 1) * 64)
                            for j in range(4):
                                k0 = (rs_ + 2 * j) * 64
                                c0 = half * 256 + j * 64
                                MM(psS[:, c0:c0 + 64], kt[:, k0:k0 + 128], qt[:, q64], True, True, [kb, qb], pbS)
                        ACT(pw[:].rearrange("p a c -> p (a c)"), psS[:, :], AF.Exp, [pbS], [pwb])
                        for half in range(2):
                            r = g * 8 + pr * 2 + half
                            rs_ = min(max(r - 4, 0), 24)
                            dr0 = rs_ - r + 7
                            TT(pw[:, half * 4:(half + 1) * 4, :], pw[:, half * 4:(half + 1) * 4, :], ebt[:, dr0:dr0 + 7:2, :],
                               ALU.mult, [pwb, ebb], [pwb])
                            rowinfo.append((pr * 2 + half, rs_, pw, pwb, half))
                    for (rr, rs_, pw, pwb, half) in rowinfo:
                        o64 = slice(rr * 64, (rr + 1) * 64)
                        last = rr == 7
                        for j in range(4):
                            MM(psD[:, o64], onesb[:], pw[:, half * 4 + j, :], False, last and j == 3, [b_const, pwb], pbD, skip=True)
                        for j in range(4):
                            row0 = rs_ + 2 * j
                            if row0 % 2 == 0:
                                vap = v0[:, row0 // 2, :]; vbb = v0b
                            else:
                                vap = v1[:, (row0 - 1) // 2, :]; vbb = v1b
                            MM(psO[:, o64], vap, pw[:, half * 4 + j, :], False, last and j == 3, [vbb, pwb], pbO, skip=True)
                    epilogue(psO, pbO, psD, pbD, 512, h * 128, slice(1024 + g * 512, 1024 + (g + 1) * 512))
                    PS_release(held)
                    if g % 2 == 1:
                        conv_unit()
                    else:
                        wcast_unit()
            while conv_todo:
                conv_unit()
            while wc_todo or wc_pend:
                wcast_unit()
            S.fence()
        STQ[0] = "sp"

    def fnet(l):
        STQ[0] = "pool"
        with ExitStack() as st:
            cs_ = sb(st, "ncs", [128, 256], BF16); bcs = Buf()
            LD(cs_[:], cd["fnCS"][:, :], [bcs])
            for (L, seqs) in ((256, [(s * 256) for s in range(4)]), (2048, [1024])):
                nch = L // 128
                with ExitStack() as st1:
                    ur = Ring(st1, "nu", [128, L], BF16, 2)
                    P12 = sb(st1, "nP", [128, 4, nch, 256], BF16); bP = Buf()
                    nsl = min(512, L)
                    clr = Ring(st1, "ncl", [128, nch, nsl], BF16, 2)
                    slr = Ring(st1, "nsl", [128, nch, nsl], BF16, 2)
                    gbr = Ring(st1, "ngb", [128, 512], BF16, 2)
                    yor = Ring(st1, "nyo", [128, 512], BF16, 2)
                    for t0 in seqs:
                        for g in range(4):
                            ut, ub_ = ur.next()
                            LD(ut[:], ubT[g * 128:(g + 1) * 128, t0:t0 + L], [ub_])
                            for lc in range(nch):
                                ps, pb = PS()
                                MM(ps[:, 0:256], ut[:, lc * 128:(lc + 1) * 128], cs_[:], True, True, [ub_, bcs], pb)
                                if lc % 2 == 0:
                                    CP(P12[:, g, lc, :], ps[:, 0:256], [pb], [bP])
                                else:
                                    ACT(P12[:, g, lc, :], ps[:, 0:256], AF.Identity, [pb], [bP])
                        for c0 in range(0, L, nsl):
                            ct, cb_ = clr.next(); st_, sb__ = slr.next()
                            LD(ct[:], cd["fnC%d" % L][:, c0:c0 + nsl].rearrange("(k p) n -> p k n", p=128), [cb_])
                            LD(st_[:], cd["fnSn%d" % L][:, c0:c0 + nsl].rearrange("(k p) n -> p k n", p=128), [sb__])
                            for g in range(4):
                                ps, pb = PS()
                                for lc in range(nch):
                                    MM(ps[:, :nsl], P12[:, g, lc, 0:128], ct[:, lc, :], lc == 0, False, [bP, cb_], pb)
                                for lc in range(nch):
                                    MM(ps[:, :nsl], P12[:, g, lc, 128:256], st_[:, lc, :], False, lc == nch - 1, [bP, sb__], pb)
                                gb, gbb = gbr.next()
                                tsl = slice(t0 + c0, t0 + c0 + nsl)
                                LD(gb[:, :nsl], gT[1024 + g * 128:1024 + (g + 1) * 128, tsl], [gbb])
                                yo, yob = yor.next()
                                TT(yo[:, :nsl], ps[:, :nsl], gb[:, :nsl], ALU.mult, [pb, gbb], [yob])
                                STo(yT[1024 + g * 128:1024 + (g + 1) * 128, tsl], yo[:, :nsl], [yob])
                    S.fence()
            S.fence()

    def hyena(l):
        sm = small[l]
        STQ[0] = "pool"
        for (L, seqs) in ((256, [s * 256 for s in range(4)]), (2048, [1024])):
            nch = L // 128
            nsl = min(512, L)
            kf = kf_s[L]
            with ExitStack() as st:
                nbuf = 4 if L == 256 else 1
                ztr = Ring(st, "yz", [128, nch, 512], BF16, nbuf)
                yfr = Ring(st, "yY", [128, nch, 2, 512], BF16, nbuf)
                zin = Ring(st, "yzin", [128, L], F32, nbuf)
                zbf = Ring(st, "yzbf", [128, L], BF16, nbuf)
                cr = Ring(st, "yC", [128, nch, nsl], BF16, 2)
                sr = Ring(st, "yS", [128, nch, nsl], BF16, 2)
                kr_ = Ring(st, "yk", [128, 2, 512], F32, 2)
                t1 = Ring(st, "yt1", [128, 512], F32, 8)
                xr_ = Ring(st, "yx", [128, 512], F32, 6)
                o32 = Ring(st, "yo32", [128, 512], F32, 2)
                o16 = Ring(st, "yo16", [128, 512], BF16, 4)

                cs_cache = {}
                kf_cache = {}
                kfr_small = Ring(st, "ykc", [128, 2, 512], F32, 4) if L == 256 else None

                def get_cs(f0):
                    if L == 256 and f0 in cs_cache:
                        return cs_cache[f0]
                    ct, cb_ = cr.next(); st_, sb__ = sr.next()
                    LD(ct[:], cd["hyC%d" % L][:, f0:f0 + nsl].rearrange("(k p) n -> p k n", p=128), [cb_])
                    LD(st_[:], cd["hyS%d" % L][:, f0:f0 + nsl].rearrange("(k p) n -> p k n", p=128), [sb__])
                    cs_cache[f0] = (ct, cb_, st_, sb__)
                    return cs_cache[f0]

                def get_kf(o, fc):
                    if L == 256 and (o, fc) in kf_cache:
                        return kf_cache[(o, fc)]
                    kt, kb_ = (kfr_small if L == 256 else kr_).next()
                    LD(kt[:], kf[:, o, fc * 128:(fc + 1) * 128, :].rearrange("r f c -> f r c"), [kb_])
                    kf_cache[(o, fc)] = (kt, kb_)
                    return kf_cache[(o, fc)]

                def load_ztok(src, t0, ztok, bz):
                    for cb in range(4):
                        zt, zb_ = zin.next()
                        LD(zt[:], src[cb * 128:(cb + 1) * 128, t0:t0 + L], [zb_])
                        zh, zhb = zbf.next()
                        CP(zh[:], zt[:], [zb_], [zhb])
                        for tc0 in range(0, nch, 4):
                            nb = min(4, nch - tc0)
                            ps, pb = PS()
                            psv = ps[:, :].bitcast(BF16)
                            for j in range(nb):
                                TR(psv[:, j * 128:(j + 1) * 128], zh[:, (tc0 + j) * 128:(tc0 + j + 1) * 128], identb[:], [zhb], pb)
                            CP(ztok[:, tc0:tc0 + nb, cb * 128:(cb + 1) * 128],
                               psv[:, 0:nb * 128].rearrange("p (j c) -> p j c", j=nb), [pb], [bz])

                for o in range(2):
                    units = []
                    for t0 in seqs:
                        ztok, bz = ztr.next()
                        Yf, bY = yfr.next()
                        load_ztok(hccT if o == 0 else z1T, t0, ztok, bz)
                        units.append((t0, ztok, bz, Yf, bY))
                    for (t0, ztok, bz, Yf, bY) in units:
                        for f0 in range(0, L, nsl):
                            ct, cb_, st_, sb__ = get_cs(f0)
                            for fs in range(nsl // 128):
                                fc = f0 // 128 + fs
                                kt, kb_ = get_kf(o, fc)
                                psA, pbA = PS(); psB, pbB = PS()
                                for tc in range(nch):
                                    MM(psA[:, :], ct[:, tc, fs * 128:(fs + 1) * 128], ztok[:, tc, :], tc == 0, tc == nch - 1, [cb_, bz], pbA)
                                for tc in range(nch):
                                    MM(psB[:, :], st_[:, tc, fs * 128:(fs + 1) * 128], ztok[:, tc, :], tc == 0, tc == nch - 1, [sb__, bz], pbB)
                                a, ab = t1.next(); b, bb = t1.next()
                                TT(a[:], psA[:, :], kt[:, 0, :], ALU.mult, [pbA, kb_], [ab])
                                TT(b[:], psB[:, :], kt[:, 1, :], ALU.mult, [pbB, kb_], [bb])
                                TT(Yf[:, fc, 0, :], a[:], b[:], ALU.add, [ab, bb], [bY], eng="pool")
                                a2, ab2 = t1.next(); b2, bb2 = t1.next()
                                TT(a2[:], psB[:, :], kt[:, 0, :], ALU.mult, [pbB, kb_], [ab2])
                                TT(b2[:], psA[:, :], kt[:, 1, :], ALU.mult, [pbA, kb_], [bb2])
                                TT(Yf[:, fc, 1, :], a2[:], b2[:], ALU.subtract, [ab2, bb2], [bY], eng="pool")
                    for (t0, ztok, bz, Yf, bY) in units:
                        for c0 in range(0, L, nsl):
                            ct, cb_, st_, sb__ = get_cs(c0)
                            tsl = slice(t0 + c0, t0 + c0 + nsl)
                            for cb in range(4):
                                ps, pb = PS()
                                for fc in range(nch):
                                    MM(ps[:, :nsl], Yf[:, fc, 0, cb * 128:(cb + 1) * 128], ct[:, fc, :], fc == 0, False, [bY, cb_], pb)
                                for fc in range(nch):
                                    MM(ps[:, :nsl], Yf[:, fc, 1, cb * 128:(cb + 1) * 128], st_[:, fc, :], False, fc == nch - 1, [bY, sb__], pb)
                                zi, zib = xr_.next(); xg, xgb = xr_.next()
                                if o == 0:
                                    LD(zi[:, :nsl], hccT[cb * 128:(cb + 1) * 128, tsl], [zib])
                                    LD(xg[:, :nsl], hccT[512 + cb * 128:512 + (cb + 1) * 128, tsl], [xgb])
                                else:
                                    LD(zi[:, :nsl], z1T[cb * 128:(cb + 1) * 128, tsl], [zib])
                                    LD(xg[:, :nsl], hccT[1024 + cb * 128:1024 + (cb + 1) * 128, tsl], [xgb])
                                r, rb = o32.next()
                                STT(r[:, :nsl], zi[:, :nsl], sm["hb"][:, o, cb:cb + 1], ps[:, :nsl], ALU.mult, ALU.add, [zib, b_small, pb], [rb])
                                TT(r[:, :nsl], r[:, :nsl], xg[:, :nsl], ALU.mult, [rb, xgb], [rb])
                                if o == 0:
                                    STo(z1T[cb * 128:(cb + 1) * 128, tsl], r[:, :nsl], [rb])
                                else:
                                    gc, gcb = o16.next()
                                    LD(gc[:, :nsl], gT[1536 + cb * 128:1536 + (cb + 1) * 128, tsl], [gcb])
                                    yo, yob = o16.next()
                                    TT(yo[:, :nsl], r[:, :nsl], gc[:, :nsl], ALU.mult, [rb, gcb], [yob])
                                    STo(yT[1536 + cb * 128:1536 + (cb + 1) * 128, tsl], yo[:, :nsl], [yob])
                    S.fence()
                S.fence()

    def pass2(l):
        STQ[0] = "sp"
        with ExitStack() as st:
            ySr = Ring(st, "pY", [128, 16, 1024], BF16, 2)
            mS = sb(st, "pM", [128, 16, 1024], BF16); bMs = Buf()
            ynext = ySr.next()
            LD(ynext[0][:], yT[:, 0:1024].rearrange("(k p) t -> p k t", p=128), [ynext[1]])
            wr = Ring(st, "pw", [128, 16, 512], BF16, 2)
            wstream = WStream(wr, [w16[i] for blk in range(3) for i in range(8)], pre=True, q="act")
            gr = Ring(st, "pg", [128, 3, 512], BF16, 3)
            t1 = Ring(st, "pt", [128, 512], F32, 9)
            xr_ = Ring(st, "px", [128, 512], F32, 2)
            xo = Ring(st, "pxo", [128, 512], F32, 2)
            p2pend = []
            for blk in range(3):
                cond = 0 if blk == 0 else 1
                t0 = blk * 1024
                yS, bYs = ynext
                if blk < 2:
                    ynext = ySr.next()
                    LD(ynext[0][:], yT[:, t0 + 1024:t0 + 2048].rearrange("(k p) t -> p k t", p=128), [ynext[1]])
                for sbk in range(4):
                    wt, wb = wstream.get()
                    for cb4 in range(4):
                        cb = sbk * 4 + cb4
                        for tc in range(2):
                            tsl = slice(t0 + tc * 512, t0 + (tc + 1) * 512)
                            gt, gb_ = gr.next()
                            LD(gt[:], mgT.rearrange("(i r) t -> r i t", i=3)[cb * 128:(cb + 1) * 128, :, tsl], [gb_])
                            ts_ = []
                            for i, (k0, k1) in enumerate(((0, 8), (8, 12), (12, 16))):
                                ps, pb = PS()
                                for kc in range(k0, k1):
                                    MM(ps[:, :], wt[:, kc, cb4 * 128:(cb4 + 1) * 128], yS[:, kc, tc * 512:(tc + 1) * 512],
                                       kc == k0, kc == k1 - 1, [wb, bYs], pb)
                                t, tb = t1.next()
                                TT(t[:], ps[:, :], gt[:, i, :], ALU.mult, [pb, gb_], [tb])
                                ts_.append((t, tb))
                            while p2pend:
                                p2pend.pop(0)()

                            def adds(ts_=ts_, cb=cb, tc=tc):
                                (ta, tab), (tb_, tbb), (tc_, tcb) = ts_
                                TT(ta[:], ta[:], tb_[:], ALU.add, [tab, tbb], [tab], eng="pool")
                                TT(mS[:, cb, tc * 512:(tc + 1) * 512], ta[:], tc_[:], ALU.add, [tab, tcb], [bMs], eng="pool")
                            p2pend.append(adds)
                while p2pend:
                    p2pend.pop(0)()
                for sbk in range(4):
                    wt, wb = wstream.get()
                    for cb4 in range(4):
                        cb = sbk * 4 + cb4
                        for tc in range(2):
                            tsl = slice(t0 + tc * 512, t0 + (tc + 1) * 512)
                            ps, pb = PS()
                            for kc in range(16):
                                MM(ps[:, :], wt[:, kc, cb4 * 128:(cb4 + 1) * 128], mS[:, kc, tc * 512:(tc + 1) * 512],
                                   kc == 0, kc == 15, [wb, bMs], pb)
                            xt_, xb_ = xr_.next()
                            LD(xt_[:], x_src(l)[cb * 128:(cb + 1) * 128, tsl], [xb_])
                            o, ob = xo.next()
                            STT(o[:], ps[:, :], adaT[l][:, 32 + cb, cond:cond + 1], xt_[:], ALU.mult, ALU.add, [pb, b_ada, xb_], [ob])
                            STo(x_dst(l)[cb * 128:(cb + 1) * 128, tsl], o[:], [ob], is_output=(l == NL_RUN - 1))
            S.fence()

    def on(name):
        marks.append((name, dict(S.ninstr)))
        return STAGES is None or name in STAGES

    for l in range(NL_RUN):
        if on("filt256"):
            hyena_filter_full(l, 256)
        if on("filt2048"):
            hyena_filter_full(l, 2048)
        if on("pass1"):
            pass1(l)
        if on("attn"):
            attention(l)
        if on("fnet"):
            fnet(l)
        if on("hyena"):
            hyena(l)
        if on("pass2"):
            pass2(l)

    S.emit()
    build_program.stats = {e: len(v) for e, v in S.prog.items()}
    return nc


_NC = None


def _host_inputs(inp, core):
    C = make_consts()
    b = core // 4
    xp = inp["x_prompt"][4 * core:4 * core + 4].reshape(1024, DM)
    xs = inp["x_sample"][b]
    m = {}
    m["xin"] = _f32(np.concatenate([xp, xs], 0).T)
    m["ck"] = _f32(inp["cache_k"][b].reshape(NL, 512, 1024))
    m["cv"] = _f32(inp["cache_v"][b].reshape(NL, 512, 1024))
    cvec = np.stack([inp["c_ctx"], inp["c"][b]], -1)
    m["cvecT"] = _f32(cvec.reshape(16, 128, 2).transpose(1, 0, 2))
    return m


def _shared_inputs(inp):
    C = make_consts()
    m = {}
    m["norm_gT"] = _f32(inp["norm_g"].reshape(NL, 16, 128).transpose(0, 2, 1))
    m["b_adaT"] = _f32(inp["b_ada"].reshape(NL, 48, 128).transpose(0, 2, 1))
    m["w_ada"] = _f32(inp["w_ada"]); m["w_in"] = _f32(inp["w_in"])
    m["w_br"] = _f32(inp["w_br"]); m["w_out"] = _f32(inp["w_out"])
    m["qg"] = _f32(inp["q_norm_g"].reshape(NL, 128, 1))
    m["kg"] = _f32(inp["k_norm_g"].reshape(NL, 128, 1))
    m["kg_rep"] = _f32(np.broadcast_to(inp["k_norm_g"][:, None, :], (NL, 128, 128)))
    m["rpbT2"] = rpb_gather(np.asarray(inp["rpb"]))
    m["conv_wT"] = _f32(inp["conv_w"].reshape(NL, 3, 12, 128).transpose(0, 3, 2, 1))
    m["conv_bT"] = _f32(inp["conv_b"].reshape(NL, 12, 128).transpose(0, 2, 1))
    m["hy_biasT"] = _f32(inp["hy_bias"].reshape(NL, 2, 4, 128).transpose(0, 3, 1, 2))
    m["f_w1"] = _f32(inp["f_w1"]); m["f_w2"] = _f32(inp["f_w2"]); m["f_w3"] = _f32(inp["f_w3"])
    m["f_b1T"] = _f32(inp["f_b1"].reshape(NL, 64, 1))
    m["f_b2T"] = _f32(inp["f_b2"].reshape(NL, 64, 1))
    m["f_freqT"] = _f32(inp["f_freq"].reshape(NL, 64, 1))
    for k, v in C.items():
        m["c_" + k] = v
    return m


def kernel(**inputs):
    global _NC
    inp = {k: np.asarray(v) for k, v in inputs.items()}
    if _NC is None:
        _NC = build_program()
    nc = _NC
    shared = _shared_inputs(inp)
    in_maps = []
    for core in range(8):
        m = dict(shared)
        m.update(_host_inputs(inp, core))
        in_maps.append(m)
    res = run_bass_kernel_spmd(nc, in_maps, core_ids=list(range(8)))
    R = res.results
    y_prompt = np.concatenate([np.ascontiguousarray(R[c]["y_out"][:, :1024].T).reshape(4, 256, DM) for c in range(8)], 0)
    y_sample = np.stack([np.ascontiguousarray(R[0]["y_out"][:, 1024:].T), np.ascontiguousarray(R[4]["y_out"][:, 1024:].T)], 0)
    nk = np.concatenate([R[c]["newk"].reshape(NL, 4, 256, 8, 128).transpose(1, 0, 2, 3, 4) for c in range(8)], 0)
    nv = np.concatenate([R[c]["newv"].reshape(NL, 4, 256, 8, 128).transpose(1, 0, 2, 3, 4) for c in range(8)], 0)
    if DEBUG:
        kernel.debug = R
    return (y_prompt.astype(np.float32), y_sample.astype(np.float32), nk.astype(np.float32), nv.astype(np.float32))
```
